# Optimizing a Trainium2 kernel written in Bass

```python
import math
import jax, jax.numpy as jnp
from jax import lax
import numpy as np

D_MODEL = 1024
BATCH = 4
SEQ = 8192
DEPTH = 1

HEAD_DIM = 64
N_HEADS_A = 8
N_KV_A = 2
N_HEADS_B = 8
WIDTH_A = N_HEADS_A * HEAD_DIM
WIDTH_B = N_HEADS_B * HEAD_DIM
MIX_WIDTH = WIDTH_A + WIDTH_B
KV_WIDTH_A = N_KV_A * HEAD_DIM
IN_WIDTH = WIDTH_A + 2 * KV_WIDTH_A + 3 * WIDTH_B
GRID_W = 64
WIN_R_MAX = 8
WIN_C = 16
Q_BLOCK = 128
ROPE_THETA = 10000.0
D_FF = -(-8 * D_MODEL // (3 * 256)) * 256
EPS = 1e-6

kernel_name = "hymba_gqa_axialrope_natten2d_swiglu"


def rmsnorm(x, g):
    xf = x.astype(jnp.float32)
    y = xf * lax.rsqrt(jnp.mean(xf * xf, axis=-1, keepdims=True) + EPS)
    return (y * g.astype(jnp.float32)).astype(x.dtype)


def axial_rope_tables(seq):
    t = jnp.arange(seq)
    row = (t // GRID_W).astype(jnp.float32)
    col = (t % GRID_W).astype(jnp.float32)
    half = HEAD_DIM // 2
    inv_freq = ROPE_THETA ** (-jnp.arange(0, half, 2, dtype=jnp.float32) / half)
    ang_r = row[:, None] * inv_freq[None, :]
    ang_c = col[:, None] * inv_freq[None, :]
    return jnp.cos(ang_r), jnp.sin(ang_r), jnp.cos(ang_c), jnp.sin(ang_c)


def rope_half(x, cos, sin):
    x1, x2 = jnp.split(x, 2, axis=-1)
    c = cos[None, :, None, :]
    s = sin[None, :, None, :]
    return jnp.concatenate([x1 * c - x2 * s, x2 * c + x1 * s], axis=-1)


def apply_axial_rope(x, tabs):
    cr, sr, cc, sc = tabs
    xf = x.astype(jnp.float32)
    half = HEAD_DIM // 2
    out = jnp.concatenate([rope_half(xf[..., :half], cr, sr), rope_half(xf[..., half:], cc, sc)], axis=-1)
    return out.astype(x.dtype)


def gqa_axial_attention(q, k, v, q_gain, k_gain):
    b, s, _ = q.shape
    g = N_HEADS_A // N_KV_A
    q = rmsnorm(q.reshape(b, s, N_HEADS_A, HEAD_DIM), q_gain)
    k = rmsnorm(k.reshape(b, s, N_KV_A, HEAD_DIM), k_gain)
    v = v.reshape(b, s, N_KV_A, HEAD_DIM)
    tabs = axial_rope_tables(s)
    q = apply_axial_rope(q, tabs) * (1.0 / math.sqrt(HEAD_DIM))
    k = apply_axial_rope(k, tabs)
    nb = s // Q_BLOCK
    qb = q.reshape(b, nb, Q_BLOCK, N_KV_A, g, HEAD_DIM).transpose(1, 0, 2, 3, 4, 5)

    def block(qblk):
        sc = jnp.einsum('bqkgd,bskd->bkgqs', qblk, k)
        p = jax.nn.softmax(sc.astype(jnp.float32), axis=-1).astype(v.dtype)
        return jnp.einsum('bkgqs,bskd->bqkgd', p, v)

    o = lax.map(block, qb)
    return o.transpose(1, 0, 2, 3, 4, 5).reshape(b, s, WIDTH_A)


def neighbourhood_attention_2d(q, k, v, rpb):
    b, s, _ = q.shape
    rows = s // GRID_W
    wr = min(WIN_R_MAX, rows)
    nk = wr * WIN_C
    q = q.reshape(b, s, N_HEADS_B, HEAD_DIM) * (1.0 / math.sqrt(HEAD_DIM))
    k = k.reshape(b, s, N_HEADS_B, HEAD_DIM)
    v = v.reshape(b, s, N_HEADS_B, HEAD_DIM)
    t = jnp.arange(s)
    r = t // GRID_W
    c = t % GRID_W
    rs = jnp.clip(r - wr // 2, 0, rows - wr)
    cs = jnp.clip(c - WIN_C // 2, 0, GRID_W - WIN_C)
    kr = rs[:, None, None] + jnp.arange(wr)[None, :, None]
    kc = cs[:, None, None] + jnp.arange(WIN_C)[None, None, :]
    idx = (kr * GRID_W + kc).reshape(s, nk)
    dr = kr - r[:, None, None] + (WIN_R_MAX - 1)
    dc = kc - c[:, None, None] + (WIN_C - 1)
    bidx = (dr * (2 * WIN_C - 1) + dc).reshape(s, nk)
    rpb_flat = rpb.reshape(N_HEADS_B, -1)
    nb = s // Q_BLOCK
    qb = q.reshape(b, nb, Q_BLOCK, N_HEADS_B, HEAD_DIM).transpose(1, 0, 2, 3, 4)
    idx_b = idx.reshape(nb, Q_BLOCK, nk)
    bidx_b = bidx.reshape(nb, Q_BLOCK, nk)

    def block(args):
        qblk, ib, bb = args
        kg = jnp.take(k, ib, axis=1)
        vg = jnp.take(v, ib, axis=1)
        sc = jnp.einsum('bqhd,bqnhd->bhqn', qblk, kg).astype(jnp.float32)
        sc = sc + rpb_flat[:, bb].astype(jnp.float32)[None]
        p = jax.nn.softmax(sc, axis=-1).astype(v.dtype)
        return jnp.einsum('bhqn,bqnhd->bqhd', p, vg)

    o = lax.map(block, (qb, idx_b, bidx_b))
    return o.transpose(1, 0, 2, 3, 4).reshape(b, s, WIDTH_B)


def setup_inputs(seed: int = 0) -> dict:
    key = jax.random.key(seed)
    ks = jax.random.split(key, 16)
    f32 = jnp.float32

    def w(k, shape, fan_in):
        return jax.random.normal(k, shape, f32) * fan_in ** -0.5

    def gain(k, shape):
        return 1.0 + 0.01 * jax.random.normal(k, shape, f32)

    return {
        "x": jax.random.normal(ks[0], (BATCH, SEQ, D_MODEL), f32),
        "norm_mix": gain(ks[1], (DEPTH, D_MODEL)),
        "w_in": w(ks[2], (DEPTH, D_MODEL, IN_WIDTH), D_MODEL),
        "q_norm_a": gain(ks[3], (DEPTH, HEAD_DIM)),
        "k_norm_a": gain(ks[4], (DEPTH, HEAD_DIM)),
        "rpb_b": 0.1 * jax.random.normal(ks[5], (DEPTH, N_HEADS_B, 2 * WIN_R_MAX - 1, 2 * WIN_C - 1), f32),
        "out_norm_a": gain(ks[6], (DEPTH, WIDTH_A)),
        "out_norm_b": gain(ks[7], (DEPTH, WIDTH_B)),
        "w_out": w(ks[8], (DEPTH, MIX_WIDTH, D_MODEL), MIX_WIDTH),
        "norm_ffn": gain(ks[9], (DEPTH, D_MODEL)),
        "w_gate": w(ks[10], (DEPTH, D_MODEL, D_FF), D_MODEL),
        "w_up": w(ks[11], (DEPTH, D_MODEL, D_FF), D_MODEL),
        "w_down": w(ks[12], (DEPTH, D_FF, D_MODEL), D_FF),
        "norm_final": gain(ks[13], (D_MODEL,)),
    }


def reference(x, norm_mix, w_in, q_norm_a, k_norm_a, rpb_b, out_norm_a, out_norm_b,
              w_out, norm_ffn, w_gate, w_up, w_down, norm_final):
    splits = [WIDTH_A, WIDTH_A + KV_WIDTH_A, WIDTH_A + 2 * KV_WIDTH_A,
              WIDTH_A + 2 * KV_WIDTH_A + WIDTH_B, WIDTH_A + 2 * KV_WIDTH_A + 2 * WIDTH_B]
    for l in range(DEPTH):
        h = rmsnorm(x, norm_mix[l])
        proj = jnp.einsum('bsd,de->bse', h, w_in[l])
        q_a, k_a, v_a, q_b, k_b, v_b = jnp.split(proj, splits, axis=-1)
        o_a = gqa_axial_attention(q_a, k_a, v_a, q_norm_a[l], k_norm_a[l])
        o_b = neighbourhood_attention_2d(q_b, k_b, v_b, rpb_b[l])
        mixed = jnp.concatenate([rmsnorm(o_a, out_norm_a[l]), rmsnorm(o_b, out_norm_b[l])], axis=-1)
        x = x + jnp.einsum('bse,ed->bsd', mixed, w_out[l])
        h = rmsnorm(x, norm_ffn[l])
        gate = jnp.einsum('bsd,df->bsf', h, w_gate[l])
        up = jnp.einsum('bsd,df->bsf', h, w_up[l])
        x = x + jnp.einsum('bsf,fd->bsd', jax.nn.silu(gate) * up, w_down[l])
    return rmsnorm(x, norm_final)
```

```python
from contextlib import ExitStack

import numpy as np
import ml_dtypes

import concourse.bass as bass
import concourse.mybir as mybir
from concourse.bass_utils import run_bass_kernel_spmd

F32 = mybir.dt.float32
BF16 = mybir.dt.bfloat16
ALU = mybir.AluOpType
AF = mybir.ActivationFunctionType
AX = mybir.AxisListType

EPS = 1e-6
NEG = -30000.0
D_FF = 2816
NFC = 22
ENGS = ("pe", "act", "dve", "pool", "sp")


class Prog:
    def __init__(self, nc, stack):
        self.nc = nc
        self.stack = stack
        self.sems = {}
        self.cnt = {}
        self.seen = {e: {} for e in ENGS}
        self._reset()

    def _reset(self):
        self.insts = []
        self.lw = {}
        self.rd = {}

    def sem(self, key):
        if key not in self.sems:
            self.sems[key] = self.stack.enter_context(self.nc.semaphore("s%d" % len(self.sems)))
            self.cnt[key] = 0
        return self.sems[key]

    def add(self, eng, fn, r=(), w=(), dma=None):
        idx = len(self.insts)
        deps = set()
        for k in r:
            if k in self.lw:
                deps.add(self.lw[k])
        for k in w:
            if k in self.lw:
                deps.add(self.lw[k])
            for d in self.rd.get(k, {}).values():
                deps.add(d)
        deps.discard(idx)
        self.insts.append(dict(eng=eng, fn=fn, deps=deps, dma=dma))
        stream = ("dma", dma) if dma is not None else eng
        for k in r:
            self.rd.setdefault(k, {})[stream] = idx
        for k in w:
            self.lw[k] = idx
            self.rd[k] = {}
        return idx

    def barrier(self):
        last = {}
        for i, ins in enumerate(self.insts):
            st = ("dma", ins["dma"]) if ins["dma"] is not None else ins["eng"]
            if ins["fn"] is not None:
                last[st] = i
        deps = set(last.values())
        for e in ENGS:
            self.insts.append(dict(eng=e, fn=None, deps=set(deps), dma=None))

    def flush(self):
        insts = self.insts
        signaled = set()
        for i, ins in enumerate(insts):
            best = {}
            for d in ins["deps"]:
                p = insts[d]
                if p["dma"] is not None:
                    st = ("dma", p["dma"])
                else:
                    st = p["eng"]
                    if st == ins["eng"] and ins["dma"] is None and st == "pe":
                        continue
                if st not in best or best[st] < d:
                    best[st] = d
            ins["rdeps"] = sorted(best.values())
            for d in ins["rdeps"]:
                signaled.add(d)
        tok = {}
        for i, ins in enumerate(insts):
            if ins["fn"] is None:
                continue
            if ins["dma"] is not None:
                key = ("dma", ins["dma"])
                self.sem(key)
                self.cnt[key] += 16
                tok[i] = (key, self.cnt[key])
            elif i in signaled:
                key = ins["eng"]
                self.sem(key)
                self.cnt[key] += 1
                tok[i] = (key, self.cnt[key])
        per = {e: [] for e in ENGS}
        for i, ins in enumerate(insts):
            per[ins["eng"]].append(i)

        def mk(en):
            def body(e):
                seen = self.seen[en]
                for i in per[en]:
                    ins = insts[i]
                    for d in ins["rdeps"]:
                        key, val = tok[d]
                        if seen.get(key, 0) < val:
                            e.wait_ge(self.sems[key], val)
                            seen[key] = val
                    if ins["fn"] is not None:
                        bi = ins["fn"](e)
                        if i in tok:
                            bi.then_inc(self.sems[tok[i][0]], 16 if ins["dma"] is not None else 1)
            return body

        with self.nc.Block() as block:
            block.tensor(mk("pe"))
            block.scalar(mk("act"))
            block.vector(mk("dve"))
            block.gpsimd(mk("pool"))
            block.sync(mk("sp"))
        self._reset()


def build_program(stage="full"):
    nc = bass.Bass("TRN2", target_bir_lowering=False)

    def din(name, shape, dt=F32):
        return nc.dram_tensor(name, list(shape), dt, kind="ExternalInput").ap()

    x_own = din("x_own", [4096, 1024])
    x_oth = din("x_oth", [4096, 1024])
    x_ext = din("x_ext", [4608, 1024])
    w_a = din("w_a", [1024, 768])
    w_b = din("w_b", [1024, 1536])
    w_out = din("w_out", [1024, 1024])
    w_g = din("w_g", [1024, D_FF])
    w_u = din("w_u", [1024, D_FF])
    w_d = din("w_d", [D_FF, 1024])
    g_mix = din("g_mix", [128, 8])
    g_ffn = din("g_ffn", [128, 8])
    g_qk = din("g_qk", [1, 640])
    g_oa = din("g_oa", [1, 512])
    g_ob = din("g_ob", [1, 512])
    g_fin = din("g_fin", [1, 1024])
    rope = din("rope", [64, 128, 128])
    bias_i = din("bias_i", [128, 8 * 640])
    bias_bd = din("bias_bd", [4, 8, 128, 768])
    ident = din("ident", [128, 128])
    out = nc.dram_tensor("out", [4096, 1024], F32, kind="ExternalOutput").ap()
    wgu_s = nc.dram_tensor("wgu_s", [NFC, 128, 2, 8, 128], BF16, kind="Internal").ap()
    wd_s = nc.dram_tensor("wd_s", [NFC, 128, 1024], BF16, kind="Internal").ap()
    dbg = None
    if stage != "full":
        dbg = nc.dram_tensor("dbg", [128, 8, 4096], BF16, kind="ExternalOutput").ap()

    with ExitStack() as top:
        P = Prog(nc, top)
        E = top.enter_context

        def sb(name, shape, dt=F32, st=None):
            return (st or top).enter_context(nc.sbuf_tensor(name, list(shape), dt))

        ps = E(nc.psum_tensor("ps", [128, 4096], F32))

        def bank(i, n=1):
            return ps[:, i * 512:(i + n) * 512]

        def bankb(i):
            return ps[:, i * 512:(i + 1) * 512].bitcast(BF16)

        identf = sb("identf", [128, 128])
        identb = sb("identb", [128, 128], BF16)
        gmix = sb("gmix", [128, 8])
        gffn = sb("gffn", [128, 8])
        epsA = sb("epsA", [128, 1])
        epsB = sb("epsB", [128, 1])
        mixbT = sb("mixbT", [128, 4, 4096], BF16)

        P.add("sp", lambda e: e.dma_start(out=identf[:], in_=ident), w=["identf"], dma="c0")
        P.add("sp", lambda e: e.dma_start(out=gmix[:], in_=g_mix), w=["gmix"], dma="c1")
        P.add("sp", lambda e: e.dma_start(out=gffn[:], in_=g_ffn), w=["gffn"], dma="c2")
        P.add("dve", lambda e: e.tensor_copy(out=identb[:], in_=identf[:]), r=["identf"], w=["identb"])
        P.add("dve", lambda e: e.memset(epsA[:], EPS), w=["epsA"])
        P.add("dve", lambda e: e.memset(epsB[:], 64.0 * EPS), w=["epsB"])

        def rms_tile(src_ap, xt, xs, ss, slot, pbank, hT_dst, hT_key, gtile, gkey, cnt):
            P.add("sp", lambda e: e.dma_start(out=xt[:, slot, :], in_=src_ap),
                  w=[("xt", slot)], dma=("xt", slot))
            P.add("act", lambda e: e.activation(out=xs[:, slot, :], in_=xt[:, slot, :], func=AF.Square,
                                               accum_out=ss[:, slot, 0:1]),
                  r=[("xt", slot)], w=[("xs", slot), ("ss", slot)])
            P.add("act", lambda e: e.activation(out=ss[:, slot, 1:2], in_=ss[:, slot, 0:1], func=AF.Sqrt,
                                               scale=1.0 / 1024, bias=epsA[:]),
                  r=[("ss", slot), "epsA"], w=[("ss", slot)])
            P.add("dve", lambda e: e.reciprocal(out=ss[:, slot, 2:3], in_=ss[:, slot, 1:2]),
                  r=[("ss", slot)], w=[("ss", slot)])
            P.add("dve", lambda e: e.tensor_scalar(out=xs[:, slot, :], in0=xt[:, slot, :],
                                                  scalar1=ss[:, slot, 2:3], scalar2=None, op0=ALU.mult),
                  r=[("xt", slot), ("ss", slot)], w=[("xs", slot)])
            pb = bankb(pbank)
            for dc in range(8):
                P.add("pe", lambda e, dc=dc: e.transpose(pb[:, dc * 128:(dc + 1) * 128],
                                                         xs[:, slot, dc * 128:(dc + 1) * 128], identb[:]),
                      r=[("xs", slot), "identb"], w=[("ps", pbank)])
            P.add("dve", lambda e: e.tensor_tensor(
                out=hT_dst, in0=pb[:, 0:1024].rearrange("p (c t) -> p c t", t=128),
                in1=gtile[:, :].unsqueeze(2).to_broadcast([128, 8, 128]), op=ALU.mult),
                r=[("ps", pbank), gkey], w=[hT_key])

        def load_weight_bf16(dst, src, ncols, wstg, keyname, gname):
            for dc in range(8):
                s = dc % 2
                P.add("sp", lambda e, dc=dc, s=s: e.dma_start(out=wstg[:, s, 0:ncols],
                                                             in_=src[dc * 128:(dc + 1) * 128, :]),
                      w=[("wstg", s)], dma=("wstg", s))
                if dc % 2 == 0:
                    P.add("dve", lambda e, dc=dc, s=s: e.tensor_copy(out=dst[:, dc, :], in_=wstg[:, s, 0:ncols]),
                          r=[("wstg", s)], w=[(keyname, dc)])
                else:
                    P.add("act", lambda e, dc=dc, s=s: e.activation(out=dst[:, dc, :], in_=wstg[:, s, 0:ncols],
                                                                   func=AF.Copy),
                          r=[("wstg", s)], w=[(keyname, dc)])

        with ExitStack() as sB:
            WB = sb("WB", [128, 8, 1536], BF16, sB)
            wstg = sb("wstg", [128, 2, 1536], F32, sB)
            cstg = sb("cstg", [128, 2, 1408], F32, sB)
            cbf = sb("cbf", [128, 2, 1408], BF16, sB)
            xt = sb("xt", [128, 2, 1024], F32, sB)
            xs = sb("xs", [128, 4, 1024], BF16, sB)
            ss = sb("ss", [128, 4, 4], F32, sB)
            hT = sb("hT", [128, 2, 8, 512], BF16, sB)
            QbT = sb("QbT", [128, 4, 1024], BF16, sB)
            KbT = sb("KbT", [128, 4, 1536], BF16, sB)
            Vb = sb("Vb", [128, 12, 8, 65], BF16, sB)
            biasI = sb("biasI", [128, 8, 640], F32, sB)
            biasD = sb("biasD", [128, 2, 768], F32, sB)
            tmpS = sb("tmpS", [128, 2, 768], F32, sB)
            PT = sb("PT", [128, 2, 768], BF16, sB)
            OTs = sb("OTs", [65, 2, 512], F32, sB)
            gob = sb("gob", [128, 512], F32, sB)
            ob = sb("ob", [128, 512], F32, sB)
            obn = sb("obn", [128, 512], BF16, sB)
            junk = sb("junkb", [128, 512], BF16, sB)
            st2 = sb("st2", [128, 16], F32, sB)

            if stage in ("full", "B"):
                load_weight_bf16(WB, w_b, 1536, wstg, "WB", None)
                WBk = [("WB", dc) for dc in range(8)]
            if stage in ("full", "F"):
                jobs = []
                for which, src in ((0, w_g), (1, w_u)):
                    for dc in range(8):
                        for hf in range(2):
                            jobs.append(("gu", which, dc, hf, src))
                for fc in range(NFC):
                    jobs.append(("d", fc))
                for n, job in enumerate(jobs):
                    s = n % 2
                    if job[0] == "gu":
                        _, which, dc, hf, src = job
                        P.add("pool", lambda e, s=s, dc=dc, hf=hf, src=src: e.dma_start(
                            out=cstg[:, s, :], in_=src[dc * 128:(dc + 1) * 128, hf * 1408:(hf + 1) * 1408]),
                            r=([("WB", 7)] if n < 2 else []), w=[("cstg", s)], dma=("cstg", s))
                        P.add("pool", lambda e, s=s: e.tensor_copy(out=cbf[:, s, :], in_=cstg[:, s, :]),
                              r=[("cstg", s)], w=[("cbf", s)])
                        P.add("pool", lambda e, s=s, which=which, dc=dc, hf=hf: e.dma_start(
                            out=wgu_s[hf * 11:(hf + 1) * 11, :, which, dc, :].rearrange("f p c -> p f c"),
                            in_=cbf[:, s, :].rearrange("p (f c) -> p f c", c=128)),
                            r=[("cbf", s)], dma=("cbfo", s))
                    else:
                        fc = job[1]
                        P.add("pool", lambda e, s=s, fc=fc: e.dma_start(
                            out=cstg[:, s, 0:1024], in_=w_d[fc * 128:(fc + 1) * 128, :]),
                            w=[("cstg", s)], dma=("cstg", s))
                        P.add("pool", lambda e, s=s: e.tensor_copy(out=cbf[:, s, 0:1024], in_=cstg[:, s, 0:1024]),
                              r=[("cstg", s)], w=[("cbf", s)])
                        P.add("pool", lambda e, s=s, fc=fc: e.dma_start(out=wd_s[fc], in_=cbf[:, s, 0:1024]),
                              r=[("cbf", s)], dma=("cbfo", s))

            if stage in ("full", "B"):
                P.add("act", lambda e: e.dma_start(out=biasI[:].rearrange("p h n -> p (h n)"), in_=bias_i),
                      w=["biasI"], dma="biasI")
                P.add("act", lambda e: e.dma_start(out=gob[:], in_=g_ob.partition_broadcast(128)),
                      w=["gob"], dma="gob")
                P.add("dve", lambda e: e.memset(Vb[:].rearrange("p a h c -> p (a h c)"), 1.0), w=["Vb"])
                state = dict(tcount=0, bd_n=0)

                def emit_Ra(gb):
                    qt, bt = gb // 3, gb % 3
                    for t in range(4):
                        et = 8 * qt + 4 * bt + t
                        tc = state["tcount"]
                        state["tcount"] += 1
                        xsl = tc % 2
                        P.add("sp", lambda e, et=et, xsl=xsl: e.dma_start(out=xt[:, xsl, :],
                                                                         in_=x_ext[et * 128:(et + 1) * 128, :]),
                              w=[("xt", xsl)], dma=("xt", xsl))
                        P.add("act", lambda e, t=t, xsl=xsl: e.activation(out=xs[:, t, :], in_=xt[:, xsl, :], func=AF.Square,
                                                                       accum_out=ss[:, t, 0:1]),
                              r=[("xt", xsl)], w=[("xs", t), ("ss", t)])
                        P.add("act", lambda e, t=t: e.activation(out=ss[:, t, 1:2], in_=ss[:, t, 0:1], func=AF.Sqrt,
                                                                scale=1.0 / 1024, bias=epsA[:]),
                              r=[("ss", t), "epsA"], w=[("ss", t)])
                        P.add("dve", lambda e, t=t: e.reciprocal(out=ss[:, t, 2:3], in_=ss[:, t, 1:2]),
                              r=[("ss", t)], w=[("ss", t)])
                        P.add("dve", lambda e, t=t, xsl=xsl: e.tensor_scalar(
                            out=xs[:, t, :], in0=xt[:, xsl, :], scalar1=ss[:, t, 2:3], scalar2=None, op0=ALU.mult),
                            r=[("xt", xsl), ("ss", t)], w=[("xs", t)])

                def emit_Rb(gb):
                    hb = gb % 2
                    for t in range(4):
                        pbank = t % 2
                        pb = bankb(pbank)
                        for dc in range(8):
                            P.add("pe", lambda e, dc=dc, t=t, pb=pb: e.transpose(
                                pb[:, dc * 128:(dc + 1) * 128], xs[:, t, dc * 128:(dc + 1) * 128], identb[:]),
                                r=[("xs", t), "identb"], w=[("ps", pbank)])
                        P.add("dve", lambda e, t=t, pb=pb, pbank=pbank: e.tensor_tensor(
                            out=hT[:, hb, :, t * 128:(t + 1) * 128],
                            in0=pb[:, 0:1024].rearrange("p (c t) -> p c t", t=128),
                            in1=gmix[:, :].unsqueeze(2).to_broadcast([128, 8, 128]), op=ALU.mult),
                            r=[("ps", pbank), "gmix"], w=[("hT", hb, t)])

                def emit_M(gb):
                    qt, bt = gb // 3, gb % 3
                    hb = gb % 2
                    hTk = [("hT", hb, t) for t in range(4)]
                    for fc in range(4):
                        pbk = 2 + (fc % 2)
                        for dc in range(8):
                            P.add("pe", lambda e, fc=fc, dc=dc, pbk=pbk: e.matmul(
                                bank(pbk), lhsT=WB[:, dc, 512 + fc * 128:512 + (fc + 1) * 128],
                                rhs=hT[:, hb, dc, :], start=(dc == 0), stop=(dc == 7)),
                                r=hTk + WBk, w=[("ps", pbk)])
                        P.add("act", lambda e, fc=fc, pbk=pbk, bt=bt: e.activation(
                            out=KbT[:, fc, bt * 512:(bt + 1) * 512], in_=bank(pbk), func=AF.Copy),
                            r=[("ps", pbk)], w=[("KbT", bt)])
                    lo, hi = {0: (256, 512), 1: (0, 512), 2: (0, 256)}[bt]
                    qoff = {0: 0, 1: 256, 2: 768}[bt]
                    n = hi - lo
                    for fc in range(4):
                        pbk = 4 + (fc % 2)
                        for dc in range(8):
                            P.add("pe", lambda e, fc=fc, dc=dc, pbk=pbk, lo=lo, hi=hi, n=n: e.matmul(
                                bank(pbk)[:, 0:n], lhsT=WB[:, dc, fc * 128:(fc + 1) * 128],
                                rhs=hT[:, hb, dc, lo:hi], start=(dc == 0), stop=(dc == 7)),
                                r=hTk + WBk, w=[("ps", pbk)])
                        P.add("act", lambda e, fc=fc, pbk=pbk, n=n, qoff=qoff: e.activation(
                            out=QbT[:, fc, qoff:qoff + n], in_=bank(pbk)[:, 0:n], func=AF.Copy),
                            r=[("ps", pbk)], w=[("QbT", bt)])
                    for t in range(4):
                        pbk = 6 + (t % 2)
                        ch = 4 * bt + t
                        for dc in range(8):
                            P.add("pe", lambda e, t=t, dc=dc, pbk=pbk: e.matmul(
                                bank(pbk), lhsT=hT[:, hb, dc, t * 128:(t + 1) * 128],
                                rhs=WB[:, dc, 1024:1536], start=(dc == 0), stop=(dc == 7)),
                                r=hTk + WBk, w=[("ps", pbk)])
                        P.add("dve", lambda e, pbk=pbk, ch=ch: e.tensor_copy(
                            out=Vb[:, ch, :, 0:64], in_=bank(pbk).rearrange("p (h d) -> p h d", d=64)),
                            r=[("ps", pbk)], w=["Vb"])

                Kk = [("KbT", b) for b in range(3)]
                Qk = [("QbT", b) for b in range(3)]

                def emit_attention(qt):
                    steps = [(jl, h) for jl in range(8) for h in range(8)]
                    info = {}

                    def blockinfo(jl):
                        j = 8 * qt + jl
                        if j in (0, 1):
                            return j, list(range(0, 6)), True
                        if j in (30, 31):
                            return j, list(range(6, 12)), True
                        return j, list(range(jl, jl + 5)), False

                    def emit_st(n):
                        jl, h = steps[n]
                        j, chunks, border = blockinfo(jl)
                        W = len(chunks) * 128
                        fc = h // 2
                        pb0 = (h % 2) * 64
                        sbk = 2 + 2 * (n % 2)
                        if border:
                            bslot = state["bd_n"] % 2
                            state["bd_n"] += 1
                            bidx = {0: 0, 1: 1, 30: 2, 31: 3}[j]
                            P.add("sp", lambda e, bslot=bslot, bidx=bidx, h=h: e.dma_start(
                                out=biasD[:, bslot, :], in_=bias_bd[bidx, h]),
                                w=[("biasD", bslot)], dma=("biasD", bslot))
                            info[n] = (biasD[:, bslot, 0:W], ("biasD", bslot))
                        else:
                            info[n] = (biasI[:, h, 0:W], "biasI")
                        for ci, cl in enumerate(chunks):
                            P.add("pe", lambda e, ci=ci, cl=cl, fc=fc, pb0=pb0, sbk=sbk, jl=jl: e.matmul(
                                ps[:, sbk * 512 + ci * 128: sbk * 512 + (ci + 1) * 128],
                                lhsT=KbT[pb0:pb0 + 64, fc, cl * 128:(cl + 1) * 128],
                                rhs=QbT[pb0:pb0 + 64, fc, jl * 128:(jl + 1) * 128], start=True, stop=True),
                                r=Kk + Qk, w=[("ps", sbk), ("ps", sbk + 1)])

                    def group_epilogue(grp, h):
                        osl = grp % 2
                        obk = 0
                        for q4 in range(4):
                            P.add("pe", lambda e, q4=q4, obk=obk, osl=osl: e.transpose(
                                ps[:, obk * 512 + q4 * 65: obk * 512 + (q4 + 1) * 65],
                                OTs[0:65, osl, q4 * 128:(q4 + 1) * 128], identf[0:65, 0:65]),
                                r=[("OTs", osl), "identf"], w=[("ps", obk)])
                        g0 = 0 if h == 3 else 4
                        src3 = ps[:, obk * 512: obk * 512 + 260].rearrange("p (h c) -> p h c", c=65)
                        P.add("dve", lambda e, src3=src3, g0=g0: e.reciprocal(
                            out=st2[:, g0:g0 + 4].unsqueeze(2), in_=src3[:, :, 64:65]),
                            r=[("ps", obk)], w=[("st2", g0)])
                        P.add("dve", lambda e, src3=src3, g0=g0: e.tensor_tensor(
                            out=ob[:, g0 * 64:(g0 + 4) * 64].rearrange("p (h d) -> p h d", d=64),
                            in0=src3[:, :, 0:64],
                            in1=st2[:, g0:g0 + 4].unsqueeze(2).to_broadcast([128, 4, 64]), op=ALU.mult),
                            r=[("ps", obk), ("st2", g0)], w=[("ob", g0)])

                    def block_epilogue(j):
                        P.add("act", lambda e: e.activation(out=junk[:, :], in_=ob[:, :], func=AF.Square,
                                                           accum_out=st2[:, 8:9]),
                              r=[("ob", 0), ("ob", 4)], w=["junk", ("st2", 8)])
                        P.add("act", lambda e: e.activation(out=st2[:, 9:10], in_=st2[:, 8:9], func=AF.Sqrt,
                                                           scale=1.0 / 512, bias=epsA[:]),
                              r=[("st2", 8), "epsA"], w=[("st2", 8)])
                        P.add("dve", lambda e: e.reciprocal(out=st2[:, 10:11], in_=st2[:, 9:10]),
                              r=[("st2", 8)], w=[("st2", 8)])
                        P.add("dve", lambda e: e.scalar_tensor_tensor(
                            out=obn[:, :], in0=ob[:, :], scalar=st2[:, 10:11], in1=gob[:, :],
                            op0=ALU.mult, op1=ALU.mult),
                            r=[("ob", 0), ("ob", 4), ("st2", 8), "gob"], w=["obn"])
                        pb = bankb(1)
                        for c in range(4):
                            P.add("pe", lambda e, c=c, pb=pb: e.transpose(
                                pb[:, c * 128:(c + 1) * 128], obn[:, c * 128:(c + 1) * 128], identb[:]),
                                r=["obn", "identb"], w=[("ps", 1)])
                        P.add("dve", lambda e, j=j, pb=pb: e.tensor_copy(
                            out=mixbT[:, :, j * 128:(j + 1) * 128],
                            in_=pb[:, 0:512].rearrange("p (c t) -> p c t", t=128)),
                            r=[("ps", 1)], w=[("mixbT", j)])

                    pending = []
                    NS = len(steps)

                    def sinfo(n):
                        jl, h = steps[n]
                        j, chunks, border = blockinfo(jl)
                        return jl, h, j, chunks, len(chunks) * 128, n % 2, 2 + 2 * (n % 2)

                    def emit_add(n):
                        jl, h, j, chunks, W, sslot, sbk = sinfo(n)
                        bsrc, bkey = info[n]
                        P.add("dve", lambda e: e.scalar_tensor_tensor(
                            out=tmpS[:, sslot, 0:W], in0=ps[:, sbk * 512: sbk * 512 + W], scalar=0.125,
                            in1=bsrc, op0=ALU.mult, op1=ALU.add),
                            r=[("ps", sbk), ("ps", sbk + 1), bkey], w=[("tmpS", sslot)])

                    def emit_exp(n):
                        jl, h, j, chunks, W, sslot, sbk = sinfo(n)
                        P.add("act", lambda e: e.activation(
                            out=PT[:, sslot, 0:W], in_=tmpS[:, sslot, 0:W], func=AF.Exp),
                            r=[("tmpS", sslot)], w=[("PT", sslot)])

                    def emit_pv(n):
                        jl, h, j, chunks, W, sslot, sbk = sinfo(n)
                        nch = len(chunks)
                        grp = n // 4
                        otb = 6 + (grp % 2)
                        hh = h % 4
                        for ci, cl in enumerate(chunks):
                            P.add("pe", lambda e, ci=ci, cl=cl: e.matmul(
                                ps[0:65, otb * 512 + hh * 128: otb * 512 + (hh + 1) * 128],
                                lhsT=Vb[:, cl, h, :], rhs=PT[:, sslot, ci * 128:(ci + 1) * 128],
                                start=(ci == 0), stop=(ci == nch - 1)),
                                r=["Vb", ("PT", sslot)], w=[("ps", otb)])
                        if hh == 3:
                            osl = grp % 2
                            P.add("act", lambda e: e.activation(
                                out=OTs[:, osl, :], in_=ps[0:65, otb * 512:(otb + 1) * 512], func=AF.Copy),
                                r=[("ps", otb)], w=[("OTs", osl)])
                            pending.append((n + 2, lambda: group_epilogue(grp, h)))
                            if h == 7:
                                pending.append((n + 3, lambda: block_epilogue(j)))
                        while pending and pending[0][0] <= n:
                            pending.pop(0)[1]()

                    for tau in range(NS + 3):
                        if 0 <= tau - 3 < NS:
                            emit_pv(tau - 3)
                        if 0 <= tau - 2 < NS:
                            emit_exp(tau - 2)
                        if 0 <= tau - 1 < NS:
                            emit_add(tau - 1)
                        if tau < NS:
                            emit_st(tau)
                    for _, fn in pending:
                        fn()

                NB = 12
                emit_Ra(0)
                emit_Rb(0)
                emit_Ra(1)
                emit_Rb(1)
                for gb in range(NB):
                    if gb + 2 < NB:
                        emit_Ra(gb + 2)
                    emit_M(gb)
                    if gb + 2 < NB:
                        emit_Rb(gb + 2)
                    if gb % 3 == 2:
                        emit_attention(gb // 3)
            P.barrier()
            if stage == "B":
                for c in range(4):
                    P.add("sp", lambda e, c=c: e.dma_start(out=dbg[:, c, :], in_=mixbT[:, c, :]), dma=("dbg", c))
                P.barrier()
            if stage == "A":
                for c in range(4):
                    P.add("sp", lambda e, c=c: e.dma_start(out=dbg[:, 4 + c, :], in_=mixbT[:, c, :]), dma=("dbg", 4 + c))
                P.barrier()
            P.flush()

        if stage == "B":
            return nc

        with ExitStack() as sQ:
            QT = sb("QT", [128, 4, 4096], BF16, sQ)
            with ExitStack() as sA:
                WA = sb("WA", [128, 8, 768], BF16, sA)
                wstg = sb("wstgA", [128, 2, 768], F32, sA)
                xt = sb("xtA", [128, 2, 1024], F32, sA)
                xs = sb("xsA", [128, 2, 1024], BF16, sA)
                ss = sb("ssA", [128, 2, 4], F32, sA)
                hT = sb("hTA", [128, 2, 8, 128], BF16, sA)
                KT = sb("KT", [128, 2, 8192], BF16, sA)
                Va = sb("Va", [128, 64, 2, 65], BF16, sA)
                rp = sb("rp", [128, 5, 128], F32, sA)
                gqk = sb("gqk", [128, 640], F32, sA)
                goa = sb("goa", [128, 512], F32, sA)
                sq = sb("sq", [128, 640], F32, sA)
                yv = sb("yv", [128, 2, 640], F32, sA)
                t1 = sb("t1", [128, 640], F32, sA)
                t2 = sb("t2", [128, 640], F32, sA)
                zf = sb("zf", [128, 2, 640], BF16, sA)
                st = sb("stA", [128, 2, 32], F32, sA)
                PTa = sb("PTa", [128, 4, 1024], BF16, sA)
                OTa = sb("OTa", [65, 2, 512], F32, sA)
                oa = sb("oa", [128, 4, 512], F32, sA)
                oan = sb("oan", [128, 2, 512], BF16, sA)
                junk = sb("junkA", [128, 512], BF16, sA)
                st3 = sb("st3", [128, 16], F32, sA)

                if stage in ("full", "A"):
                    load_weight_bf16(WA, w_a, 768, wstg, "WA", None)
                    WAk = [("WA", dc) for dc in range(8)]
                    P.add("sp", lambda e: e.dma_start(out=gqk[:], in_=g_qk.partition_broadcast(128)),
                          w=["gqk"], dma="gqk")
                    P.add("sp", lambda e: e.dma_start(out=goa[:], in_=g_oa.partition_broadcast(128)),
                          w=["goa"], dma="goa")
                    P.add("dve", lambda e: e.memset(Va[:].rearrange("p a h c -> p (a h c)"), 1.0), w=["Va"])
                    P.add("pool", lambda e: e.memset(KT[:].rearrange("p g k -> p (g k)"), 0.0), w=["KTz"])
                    NRP = 5

                    def a_stage(k, ti):
                        own = ti < 32
                        slot = ti % 2
                        rslot = ti % NRP
                        rb = 2 + 2 * slot
                        hk = [("hTA", slot)]
                        c0 = 0 if own else 512
                        nh = 10 if own else 2
                        Wc = nh * 64
                        reg = ps[:, rb * 512 + c0: rb * 512 + c0 + Wc]
                        pk = [("ps", rb), ("ps", rb + 1)] if own else [("ps", rb + 1)]
                        sl = slice(c0, c0 + Wc)
                        if k == 0:
                            src = x_own[ti * 128:(ti + 1) * 128, :] if own else x_oth[(ti - 32) * 128:(ti - 31) * 128, :]
                            P.add("sp", lambda e: e.dma_start(out=rp[:, rslot, :], in_=rope[ti]),
                                  w=[("rp", rslot)], dma=("rp", rslot))
                            P.add("sp", lambda e: e.dma_start(out=xt[:, slot, :], in_=src),
                                  w=[("xt", slot)], dma=("xt", slot))
                            P.add("act", lambda e: e.activation(out=xs[:, slot, :], in_=xt[:, slot, :], func=AF.Square,
                                                               accum_out=ss[:, slot, 0:1]),
                                  r=[("xt", slot)], w=[("xs", slot), ("ss", slot)])
                            P.add("act", lambda e: e.activation(out=ss[:, slot, 1:2], in_=ss[:, slot, 0:1], func=AF.Sqrt,
                                                               scale=1.0 / 1024, bias=epsA[:]),
                                  r=[("ss", slot), "epsA"], w=[("ss", slot)])
                            P.add("dve", lambda e: e.reciprocal(out=ss[:, slot, 2:3], in_=ss[:, slot, 1:2]),
                                  r=[("ss", slot)], w=[("ss", slot)])
                            P.add("dve", lambda e: e.tensor_scalar(out=xs[:, slot, :], in0=xt[:, slot, :],
                                                                  scalar1=ss[:, slot, 2:3], scalar2=None, op0=ALU.mult),
                                  r=[("xt", slot), ("ss", slot)], w=[("xs", slot)])
                        elif k == 1:
                            pb = bankb(slot)
                            for dc in range(8):
                                P.add("pe", lambda e, dc=dc: e.transpose(pb[:, dc * 128:(dc + 1) * 128],
                                                                         xs[:, slot, dc * 128:(dc + 1) * 128], identb[:]),
                                      r=[("xs", slot), "identb"], w=[("ps", slot)])
                            P.add("dve", lambda e: e.tensor_tensor(
                                out=hT[:, slot, :, :], in0=pb[:, 0:1024].rearrange("p (c t) -> p c t", t=128),
                                in1=gmix[:, :].unsqueeze(2).to_broadcast([128, 8, 128]), op=ALU.mult),
                                r=[("ps", slot), "gmix"], w=[("hTA", slot)])
                        elif k == 2:
                            if own:
                                for dc in range(8):
                                    P.add("pe", lambda e, dc=dc: e.matmul(
                                        bank(rb), lhsT=hT[:, slot, dc, :], rhs=WA[:, dc, 0:512],
                                        start=(dc == 0), stop=(dc == 7)), r=hk + WAk, w=[("ps", rb)])
                            for dc in range(8):
                                P.add("pe", lambda e, dc=dc: e.matmul(
                                    bank(rb + 1)[:, 0:256], lhsT=hT[:, slot, dc, :], rhs=WA[:, dc, 512:768],
                                    start=(dc == 0), stop=(dc == 7)), r=hk + WAk, w=[("ps", rb + 1)])
                        elif k == 3:
                            g_ap = gqk[:, c0:c0 + Wc]
                            P.add("act", lambda e: e.activation(out=sq[:, sl], in_=reg, func=AF.Square),
                                  r=pk, w=["sq"])
                            P.add("dve", lambda e: e.tensor_tensor(
                                out=yv[:, slot, sl], in0=reg, in1=g_ap, op=ALU.mult), r=pk + ["gqk"], w=[("yv", slot)])
                            P.add("dve", lambda e: e.tensor_copy(
                                out=Va[:, ti, :, 0:64],
                                in_=ps[:, (rb + 1) * 512 + 128:(rb + 1) * 512 + 256].rearrange("p (h d) -> p h d", d=64)),
                                r=[("ps", rb + 1)], w=[("Va", ti)])
                            P.add("dve", lambda e: e.tensor_reduce(
                                out=st[:, slot, 0:nh], in_=sq[:, sl].rearrange("p (h d) -> p h d", d=64),
                                axis=AX.X, op=ALU.add), r=["sq"], w=[("stA", slot)])
                            if own:
                                P.add("act", lambda e: e.activation(
                                    out=st[:, slot, 10:18], in_=st[:, slot, 0:8], func=AF.Sqrt, scale=1.0, bias=epsB[:]),
                                    r=[("stA", slot), "epsB"], w=[("stA", slot)])
                                P.add("act", lambda e: e.activation(
                                    out=st[:, slot, 18:20], in_=st[:, slot, 8:10], func=AF.Sqrt, scale=1.0 / 64,
                                    bias=epsA[:]), r=[("stA", slot), "epsA"], w=[("stA", slot)])
                                P.add("dve", lambda e: e.reciprocal(out=st[:, slot, 20:30], in_=st[:, slot, 10:20]),
                                      r=[("stA", slot)], w=[("stA", slot)])
                            else:
                                P.add("act", lambda e: e.activation(
                                    out=st[:, slot, 10:12], in_=st[:, slot, 0:2], func=AF.Sqrt, scale=1.0 / 64,
                                    bias=epsA[:]), r=[("stA", slot), "epsA"], w=[("stA", slot)])
                                P.add("dve", lambda e: e.reciprocal(out=st[:, slot, 20:22], in_=st[:, slot, 10:12]),
                                      r=[("stA", slot)], w=[("stA", slot)])
                        elif k == 4:
                            y5 = yv[:, slot, sl].rearrange("p (h a t s) -> p h a t s", a=2, t=2, s=16)
                            t25 = t2[:, sl].rearrange("p (h a t s) -> p h a t s", a=2, t=2, s=16)
                            C3 = rp[:, rslot, 0:64].unsqueeze(1).to_broadcast([128, nh, 64])
                            S4 = rp[:, rslot, 64:128].rearrange("p (a t s) -> p a t s", a=2, t=2)
                            P.add("dve", lambda e: e.tensor_tensor(
                                out=t1[:, sl].rearrange("p (h d) -> p h d", d=64),
                                in0=yv[:, slot, sl].rearrange("p (h d) -> p h d", d=64), in1=C3, op=ALU.mult),
                                r=[("yv", slot), ("rp", rslot)], w=["t1"])
                            for tt in range(2):
                                P.add("pool", lambda e, tt=tt: e.tensor_tensor(
                                    out=t25[:, :, :, tt, :], in0=y5[:, :, :, 1 - tt, :],
                                    in1=S4[:, :, tt, :].unsqueeze(1).to_broadcast([128, nh, 2, 16]), op=ALU.mult),
                                    r=[("yv", slot), ("rp", rslot)], w=[("t2", tt)])
                            P.add("dve", lambda e: e.tensor_tensor(out=t1[:, sl], in0=t1[:, sl], in1=t2[:, sl],
                                                                  op=ALU.add),
                                  r=["t1", ("t2", 0), ("t2", 1)], w=["t1"])
                            P.add("dve", lambda e: e.tensor_tensor(
                                out=zf[:, slot, sl].rearrange("p (h d) -> p h d", d=64),
                                in0=t1[:, sl].rearrange("p (h d) -> p h d", d=64),
                                in1=st[:, slot, 20:20 + nh].unsqueeze(2).to_broadcast([128, nh, 64]), op=ALU.mult),
                                r=["t1", ("stA", slot)], w=[("zf", slot)])
                        elif k == 5:
                            tb = 6 + slot
                            pb = bankb(tb)
                            cs = range(0, 5) if own else range(4, 5)
                            for c in cs:
                                P.add("pe", lambda e, c=c: e.transpose(
                                    pb[:, c * 128:(c + 1) * 128], zf[:, slot, c * 128:(c + 1) * 128], identb[:]),
                                    r=[("zf", slot), "identb"], w=[("ps", tb)])
                            if own:
                                P.add("act", lambda e: e.activation(
                                    out=QT[:, :, ti * 128:(ti + 1) * 128],
                                    in_=pb[:, 0:512].rearrange("p (c t) -> p c t", t=128), func=AF.Copy),
                                    r=[("ps", tb)], w=[("QT", ti // 4)])
                            for g in range(2):
                                P.add("act", lambda e, g=g: e.activation(
                                    out=KT[g * 64:(g + 1) * 64, g, ti * 128:(ti + 1) * 128],
                                    in_=pb[g * 64:(g + 1) * 64, 512:640], func=AF.Copy),
                                    r=[("ps", tb), "KTz"], w=[("KT", ti, g)])

                    NST = 6
                    for tau in range(64 + NST - 1):
                        for k in reversed(range(NST)):
                            ti = tau - k
                            if 0 <= ti < 64:
                                a_stage(k, ti)

                    KTk = [("KT", ti, g) for ti in range(64) for g in range(2)]
                    Vak = [("Va", ti) for ti in range(64)] + ["Va"]
                    steps = [(qb, i, g, kc2) for qb in range(8) for i in range(4) for g in range(2)
                             for kc2 in range(32)]
                    NS = len(steps)

                    def emit_st(n):
                        qb, i, g, kc2 = steps[n]
                        p0 = g * 64
                        sb2 = 2 * (n % 3)
                        for u in range(2):
                            kc = 2 * kc2 + u
                            P.add("pe", lambda e, u=u, kc=kc, sb2=sb2, g=g, i=i, qb=qb: e.matmul(
                                bank(sb2 + u), lhsT=KT[:, g, kc * 128:(kc + 1) * 128],
                                rhs=QT[:, i, qb * 512:(qb + 1) * 512], start=True, stop=True),
                                r=KTk + [("QT", qb)], w=[("ps", sb2 + u)])

                    def head_epilogue(hidx, qb, i, g):
                        osl = hidx % 2
                        fo = i * 128 + g * 64
                        for hp in range(2):
                            for q2 in range(2):
                                q4 = 2 * hp + q2
                                P.add("pe", lambda e, q4=q4, q2=q2, osl=osl: e.transpose(
                                    ps[:, 7 * 512 + q2 * 65: 7 * 512 + (q2 + 1) * 65],
                                    OTa[0:65, osl, q4 * 128:(q4 + 1) * 128], identf[0:65, 0:65]),
                                    r=[("OTa", osl), "identf"], w=[("ps", 7)])
                            src3 = ps[:, 7 * 512: 7 * 512 + 130].rearrange("p (t c) -> p t c", c=65)
                            P.add("dve", lambda e, src3=src3, hp=hp: e.reciprocal(
                                out=st3[:, 2 * hp:2 * hp + 2].unsqueeze(2), in_=src3[:, :, 64:65]),
                                r=[("ps", 7)], w=[("st3", hp)])
                            P.add("dve", lambda e, src3=src3, fo=fo, hp=hp: e.tensor_tensor(
                                out=oa[:, 2 * hp:2 * hp + 2, fo:fo + 64], in0=src3[:, :, 0:64],
                                in1=st3[:, 2 * hp:2 * hp + 2].unsqueeze(2).to_broadcast([128, 2, 64]), op=ALU.mult),
                                r=[("ps", 7), ("st3", hp)], w=["oa"])

                    def qb_epilogue(qb):
                        for t in range(4):
                            P.add("act", lambda e, t=t: e.activation(out=junk[:, :], in_=oa[:, t, :], func=AF.Square,
                                                                    accum_out=st3[:, 8 + t:9 + t]),
                                  r=["oa"], w=["junkA", ("st3b", t)])
                        for t in range(4):
                            P.add("act", lambda e, t=t: e.activation(out=st3[:, 8 + t:9 + t], in_=st3[:, 8 + t:9 + t],
                                                                    func=AF.Sqrt, scale=1.0 / 512, bias=epsA[:]),
                                  r=[("st3b", t), "epsA"], w=[("st3b", t)])
                        for t in range(4):
                            P.add("dve", lambda e, t=t: e.reciprocal(out=st3[:, 12 + t:13 + t], in_=st3[:, 8 + t:9 + t]),
                                  r=[("st3b", t)], w=[("st3c", t)])
                            P.add("dve", lambda e, t=t: e.scalar_tensor_tensor(
                                out=oan[:, t % 2, :], in0=oa[:, t, :], scalar=st3[:, 12 + t:13 + t], in1=goa[:, :],
                                op0=ALU.mult, op1=ALU.mult), r=["oa", ("st3c", t), "goa"], w=[("oan", t % 2)])
                            pb = bankb(7)
                            for c in range(4):
                                P.add("pe", lambda e, c=c, pb=pb, t=t: e.transpose(
                                    pb[:, c * 128:(c + 1) * 128], oan[:, t % 2, c * 128:(c + 1) * 128], identb[:]),
                                    r=[("oan", t % 2), "identb"], w=[("ps", 7)])
                            tok0 = qb * 512 + t * 128
                            P.add("dve", lambda e, pb=pb, tok0=tok0: e.tensor_copy(
                                out=QT[:, :, tok0:tok0 + 128],
                                in_=pb[:, 0:512].rearrange("p (c t) -> p c t", t=128)),
                                r=[("ps", 7)], w=[("QT", qb)])

                    pending = []
                    emit_st(0)
                    emit_st(1)
                    for n in range(NS):
                        qb, i, g, kc2 = steps[n]
                        hidx = n // 32
                        ob_ = 6
                        osl = hidx % 2
                        pslot = n % 4
                        sb2 = 2 * (n % 3)
                        if n + 2 < NS:
                            emit_st(n + 2)
                        P.add("act", lambda e, sb2=sb2, pslot=pslot: e.activation(
                            out=PTa[:, pslot, :], in_=bank(sb2, 2), func=AF.Exp),
                            r=[("ps", sb2), ("ps", sb2 + 1)], w=[("PTa", pslot)])
                        for u in range(2):
                            kc = 2 * kc2 + u
                            P.add("pe", lambda e, u=u, kc=kc, pslot=pslot, g=g, ob_=ob_: e.matmul(
                                ps[0:65, ob_ * 512:(ob_ + 1) * 512], lhsT=Va[:, kc, g, :],
                                rhs=PTa[:, pslot, u * 512:(u + 1) * 512],
                                start=(kc == 0), stop=(kc == 63)),
                                r=Vak + [("PTa", pslot)], w=[("ps", ob_)])
                        if kc2 == 31:
                            P.add("dve", lambda e, ob_=ob_, osl=osl: e.tensor_copy(
                                out=OTa[:, osl, :], in_=ps[0:65, ob_ * 512:(ob_ + 1) * 512]),
                                r=[("ps", ob_)], w=[("OTa", osl)])
                            pending.append((n + 3, lambda hidx=hidx, qb=qb, i=i, g=g: head_epilogue(hidx, qb, i, g)))
                            if i == 3 and g == 1:
                                pending.append((n + 6, lambda qb=qb: qb_epilogue(qb)))
                        while pending and pending[0][0] <= n:
                            pending.pop(0)[1]()
                    for _, fn in pending:
                        fn()
                P.barrier()
                if stage == "A":
                    for c in range(4):
                        P.add("sp", lambda e, c=c: e.dma_start(out=dbg[:, c, :], in_=QT[:, c, :]), dma=("dbg", c))
                    P.barrier()
                P.flush()

            if stage == "A":
                return nc

            with ExitStack() as sF:
                Wo = sb("Wo", [128, 8, 1024], BF16, sF)
                wstg = sb("wstgF", [128, 2, 1024], F32, sF)
                x1 = sb("x1", [128, 2, 4, 1024], F32, sF)
                xsF = sb("xsF", [128, 2, 1024], BF16, sF)
                ssF = sb("ssF", [128, 2, 4, 8], F32, sF)
                h2T = sb("h2T", [128, 2, 8, 512], BF16, sF)
                actT = sb("actT", [128, 2, 4, 512], BF16, sF)
                wgu = sb("wgu", [128, 6, 2, 8, 128], BF16, sF)
                wdr = sb("wdr", [128, 6, 1024], BF16, sF)
                sg = sb("sg", [128, 2, 512], F32, sF)
                gfin = sb("gfin", [128, 1024], F32, sF)
                ost = sb("ost", [128, 2, 1024], F32, sF)
                junkF = sb("junkF", [128, 1024], BF16, sF)

                load_weight_bf16(Wo, w_out, 1024, wstg, "Wo", None)
                Wok = [("Wo", dc) for dc in range(8)]
                P.add("sp", lambda e: e.dma_start(out=gfin[:], in_=g_fin.partition_broadcast(128)),
                      w=["gfin"], dma="gfin")
                cnt = dict(w=0, x=0, o=0, g=0)
                groups = [list(range(s0, min(s0 + 4, NFC))) for s0 in range(0, NFC, 4)]

                def f_wout(tb, t):
                    xb = tb % 2
                    yb = 2 * (t % 2)
                    tok0 = tb * 512 + t * 128
                    P.add("sp", lambda e: e.dma_start(out=x1[:, xb, t, :], in_=x_own[tok0:tok0 + 128, :]),
                          w=[("x1", xb, t)], dma=("x1", xb, t))
                    for hf in range(2):
                        for c in range(8):
                            src = QT[:, c, tok0:tok0 + 128] if c < 4 else mixbT[:, c - 4, tok0:tok0 + 128]
                            P.add("pe", lambda e, hf=hf, c=c, src=src: e.matmul(
                                bank(yb + hf), lhsT=src, rhs=Wo[:, c, hf * 512:(hf + 1) * 512],
                                start=(c == 0), stop=(c == 7)), r=Wok, w=[("ps", yb + hf)])
                    P.add("dve", lambda e: e.tensor_tensor(out=x1[:, xb, t, :], in0=bank(yb, 2),
                                                          in1=x1[:, xb, t, :], op=ALU.add),
                          r=[("ps", yb), ("ps", yb + 1), ("x1", xb, t)], w=[("x1", xb, t)])
                    P.add("act", lambda e: e.activation(out=junkF[:, :], in_=x1[:, xb, t, :], func=AF.Square,
                                                       accum_out=ssF[:, xb, t, 0:1]),
                          r=[("x1", xb, t)], w=["junkF", ("ssF", xb, t)])
                    P.add("act", lambda e: e.activation(out=ssF[:, xb, t, 1:2], in_=ssF[:, xb, t, 0:1],
                                                       func=AF.Sqrt, scale=1.0 / 1024, bias=epsA[:]),
                          r=[("ssF", xb, t), "epsA"], w=[("ssF", xb, t)])
                    P.add("dve", lambda e: e.reciprocal(out=ssF[:, xb, t, 2:3], in_=ssF[:, xb, t, 1:2]),
                          r=[("ssF", xb, t)], w=[("ssF", xb, t)])
                    xsl = t % 2
                    P.add("dve", lambda e: e.tensor_scalar(
                        out=xsF[:, xsl, :], in0=x1[:, xb, t, :], scalar1=ssF[:, xb, t, 2:3], scalar2=None,
                        op0=ALU.mult),
                        r=[("x1", xb, t), ("ssF", xb, t)], w=[("xsF", xsl)])

                def f_tr(tb, t):
                    xb = tb % 2
                    xsl = t % 2
                    pbk = 2 * (t % 2) + 1
                    pb = bankb(pbk)
                    for dc in range(8):
                        P.add("pe", lambda e, dc=dc: e.transpose(
                            pb[:, dc * 128:(dc + 1) * 128], xsF[:, xsl, dc * 128:(dc + 1) * 128], identb[:]),
                            r=[("xsF", xsl), "identb"], w=[("ps", pbk)])
                    P.add("dve", lambda e: e.tensor_tensor(
                        out=h2T[:, xb, :, t * 128:(t + 1) * 128],
                        in0=pb[:, 0:1024].rearrange("p (c t) -> p c t", t=128),
                        in1=gffn[:, :].unsqueeze(2).to_broadcast([128, 8, 128]), op=ALU.mult),
                        r=[("ps", pbk), "gffn"], w=[("h2T", xb, t)])

                def f_prologue(tb):
                    for t in range(4):
                        f_wout(tb, t)
                        if t >= 1:
                            f_tr(tb, t - 1)
                    f_tr(tb, 3)

                def f_group(tb, grp):
                    xb = tb % 2
                    h2k = [("h2T", xb, t) for t in range(4)]
                    asl = cnt["g"] % 2
                    cnt["g"] += 1
                    wsl = {}
                    for fc in grp:
                        s_ = cnt["w"] % 6
                        cnt["w"] += 1
                        wsl[fc] = s_
                        P.add("sp", lambda e, s_=s_, fc=fc: e.dma_start(
                            out=wgu[:, s_].rearrange("p a c f -> p (a c f)"),
                            in_=wgu_s[fc].rearrange("p a c f -> p (a c f)")),
                            w=[("wgu", s_)], dma=("wgu", s_))
                        P.add("sp", lambda e, s_=s_, fc=fc: e.dma_start(out=wdr[:, s_, :], in_=wd_s[fc]),
                              w=[("wdr", s_)], dma=("wdr", s_))
                    for k, fc in enumerate(grp):
                        s_ = wsl[fc]
                        gb = 4 + 2 * (k % 2)
                        for a_ in range(2):
                            for dc in range(8):
                                P.add("pe", lambda e, a_=a_, dc=dc, s_=s_, gb=gb: e.matmul(
                                    bank(gb + a_), lhsT=wgu[:, s_, a_, dc, :], rhs=h2T[:, xb, dc, :],
                                    start=(dc == 0), stop=(dc == 7)),
                                    r=h2k + [("wgu", s_)], w=[("ps", gb + a_)])
                        P.add("act", lambda e, gb=gb, k=k: e.activation(out=sg[:, k % 2, :], in_=bank(gb), func=AF.Silu),
                              r=[("ps", gb)], w=[("sg", k % 2)])
                        P.add("dve", lambda e, gb=gb, k=k, asl=asl: e.tensor_tensor(
                            out=actT[:, asl, k, :], in0=bank(gb + 1), in1=sg[:, k % 2, :], op=ALU.mult),
                            r=[("ps", gb + 1), ("sg", k % 2)], w=[("actT", asl, k)])
                    ak = [("actT", asl, k) for k in range(len(grp))]
                    for t in range(4):
                        for hf in range(2):
                            db = (2 * t + hf) % 4
                            for k, fc in enumerate(grp):
                                s_ = wsl[fc]
                                P.add("pe", lambda e, t=t, hf=hf, k=k, s_=s_, db=db, asl=asl, n=len(grp): e.matmul(
                                    bank(db), lhsT=actT[:, asl, k, t * 128:(t + 1) * 128],
                                    rhs=wdr[:, s_, hf * 512:(hf + 1) * 512], start=(k == 0), stop=(k == n - 1)),
                                    r=ak + [("wdr", s_)], w=[("ps", db)])
                            P.add("dve", lambda e, t=t, hf=hf, db=db: e.tensor_tensor(
                                out=x1[:, xb, t, hf * 512:(hf + 1) * 512], in0=bank(db),
                                in1=x1[:, xb, t, hf * 512:(hf + 1) * 512], op=ALU.add),
                                r=[("ps", db), ("x1", xb, t)], w=[("x1", xb, t)])

                def f_epilogue(tb):
                    xb = tb % 2
                    for t in range(4):
                        P.add("act", lambda e, t=t: e.activation(out=junkF[:, :], in_=x1[:, xb, t, :], func=AF.Square,
                                                                accum_out=ssF[:, xb, t, 4:5]),
                              r=[("x1", xb, t)], w=["junkF", ("ssF", xb, t)])
                    for t in range(4):
                        P.add("act", lambda e, t=t: e.activation(out=ssF[:, xb, t, 5:6], in_=ssF[:, xb, t, 4:5],
                                                                func=AF.Sqrt, scale=1.0 / 1024, bias=epsA[:]),
                              r=[("ssF", xb, t), "epsA"], w=[("ssF", xb, t)])
                    for t in range(4):
                        osl = cnt["o"] % 2
                        cnt["o"] += 1
                        tok0 = tb * 512 + t * 128
                        P.add("dve", lambda e, t=t: e.reciprocal(out=ssF[:, xb, t, 6:7], in_=ssF[:, xb, t, 5:6]),
                              r=[("ssF", xb, t)], w=[("ssF", xb, t)])
                        P.add("dve", lambda e, t=t, osl=osl: e.scalar_tensor_tensor(
                            out=ost[:, osl, :], in0=x1[:, xb, t, :], scalar=ssF[:, xb, t, 6:7], in1=gfin[:, :],
                            op0=ALU.mult, op1=ALU.mult),
                            r=[("x1", xb, t), ("ssF", xb, t), "gfin"], w=[("ost", osl)])
                        P.add("pool", lambda e, osl=osl, tok0=tok0: e.dma_start(out=out[tok0:tok0 + 128, :],
                                                                             in_=ost[:, osl, :]),
                              r=[("ost", osl)], dma=("ost", osl))

                f_prologue(0)
                for tb in range(8):
                    for gi, grp in enumerate(groups):
                        f_group(tb, grp)
                        if tb + 1 < 8:
                            if gi < 4:
                                f_wout(tb + 1, gi)
                            if 1 <= gi <= 4:
                                f_tr(tb + 1, gi - 1)
                    f_epilogue(tb)
                P.barrier()
                P.flush()
    return nc


def _rope_tables(tok):
    row = (tok // 64).astype(np.float32)
    col = (tok % 64).astype(np.float32)
    inv = (np.float32(10000.0) ** (-np.arange(0, 32, 2, dtype=np.float32) / np.float32(32))).astype(np.float32)
    ar = (row[:, None] * inv[None, :]).astype(np.float32)
    ac = (col[:, None] * inv[None, :]).astype(np.float32)
    n = tok.shape[0]
    C = np.empty((n, 2, 2, 16), np.float32)
    S = np.empty((n, 2, 2, 16), np.float32)
    for a, ang in enumerate((ar, ac)):
        c = np.cos(ang).astype(np.float32)
        s = np.sin(ang).astype(np.float32)
        C[:, a, 0] = c
        C[:, a, 1] = c
        S[:, a, 0] = -s
        S[:, a, 1] = s
    return np.concatenate([C.reshape(n, 64), S.reshape(n, 64)], axis=1)


def _bias_table(rpb, own_row0, j, w0, nch):
    p = np.arange(128)
    ci = np.arange(nch)
    e = w0 + 2 * ci[:, None] + (p[None, :] // 64)
    kr = own_row0 - 4 + e
    kc = np.broadcast_to(p[None, :] % 64, kr.shape)
    q = np.arange(128)
    r = own_row0 + 2 * j + q // 64
    c = q % 64
    rs = np.clip(r - 4, 0, 120)
    cs = np.clip(c - 8, 0, 48)
    KR = kr[:, :, None]
    KC = kc[:, :, None]
    valid = (KR >= 0) & (KR <= 127) & (KR >= rs[None, None, :]) & (KR < rs[None, None, :] + 8) \
        & (KC >= cs[None, None, :]) & (KC < cs[None, None, :] + 16)
    dr = np.clip(KR - r[None, None, :] + 7, 0, 14)
    dc = np.clip(KC - c[None, None, :] + 15, 0, 30)
    vals = rpb[:, dr, dc]
    tab = np.where(valid[None], vals, np.float32(NEG)).astype(np.float32)
    return np.ascontiguousarray(tab.transpose(0, 2, 1, 3)).reshape(8, 128, nch * 128)


def prep_inputs(inputs):
    f = lambda a: np.ascontiguousarray(np.asarray(a, dtype=np.float32))
    x = f(inputs["x"])
    w_in = f(inputs["w_in"])[0]
    qperm = np.array([(4 * g + i) * 64 + d for i in range(4) for g in range(2) for d in range(64)])
    w_qa = w_in[:, 0:512][:, qperm]
    w_ka = w_in[:, 512:640]
    w_va = w_in[:, 640:768]
    w_qb = w_in[:, 768:1280]
    w_kb = w_in[:, 1280:1792]
    w_vb = w_in[:, 1792:2304]
    w_a = f(np.concatenate([w_qa, w_ka, w_va], axis=1))
    w_b = f(np.concatenate([w_qb, w_kb, w_vb], axis=1))
    w_out = f(inputs["w_out"])[0]
    w_out_p = f(np.concatenate([w_out[0:512][qperm], w_out[512:1024]], axis=0))
    g_oa = f(inputs["out_norm_a"])[0][qperm].reshape(1, 512)
    g_ob = f(inputs["out_norm_b"])[0].reshape(1, 512)
    g_mix = f(f(inputs["norm_mix"])[0].reshape(8, 128).T)
    g_ffn = f(f(inputs["norm_ffn"])[0].reshape(8, 128).T)
    g_fin = f(inputs["norm_final"]).reshape(1, 1024)
    g_qk = f(np.concatenate([np.tile(f(inputs["q_norm_a"])[0], 8), np.tile(f(inputs["k_norm_a"])[0], 2)])).reshape(1, 640)
    rpb = f(inputs["rpb_b"])[0]
    w_g = f(inputs["w_gate"])[0]
    w_u = f(inputs["w_up"])[0]
    w_d = f(inputs["w_down"])[0]
    ident = np.eye(128, dtype=np.float32)
    bias_i = _bias_table(rpb, 0, 5, 10, 5)
    bias_i = f(bias_i.transpose(1, 0, 2).reshape(128, 8 * 640))
    in_maps = []
    for c in range(8):
        b, hf = c // 2, c % 2
        t0 = 4096 * hf
        o0 = 4096 * (1 - hf)
        row0 = 64 * hf
        x_own = x[b, t0:t0 + 4096]
        x_oth = x[b, o0:o0 + 4096]
        x_ext = np.zeros((72, 64, 1024), np.float32)
        for e in range(72):
            gr = row0 - 4 + e
            if 0 <= gr < 128:
                x_ext[e] = x[b, gr * 64:(gr + 1) * 64]
        tok = np.concatenate([np.arange(t0, t0 + 4096), np.arange(o0, o0 + 4096)])
        rope = _rope_tables(tok).reshape(64, 128, 128)
        bd = np.stack([_bias_table(rpb, row0, 0, 0, 6), _bias_table(rpb, row0, 1, 0, 6),
                       _bias_table(rpb, row0, 30, 60, 6), _bias_table(rpb, row0, 31, 60, 6)])
        in_maps.append(dict(
            x_own=f(x_own), x_oth=f(x_oth), x_ext=f(x_ext.reshape(4608, 1024)),
            w_a=w_a, w_b=w_b, w_out=w_out_p, w_g=w_g, w_u=w_u, w_d=w_d,
            g_mix=g_mix, g_ffn=g_ffn, g_qk=g_qk, g_oa=g_oa, g_ob=g_ob, g_fin=g_fin,
            rope=f(rope), bias_i=bias_i, bias_bd=f(bd), ident=ident))
    return in_maps


def kernel(**inputs):
    in_maps = prep_inputs(inputs)
    nc = build_program("full")
    res = run_bass_kernel_spmd(nc, in_maps, core_ids=list(range(8)))
    out = np.empty((4, 8192, 1024), np.float32)
    for c in range(8):
        b, hf = c // 2, c % 2
        out[b, 4096 * hf:4096 * (hf + 1)] = np.asarray(res.results[c]["out"], dtype=np.float32)
    return out
```

```python
from contextlib import ExitStack

import numpy as np
import ml_dtypes

import concourse.bass as bass
import concourse.mybir as mybir
from concourse.bass_utils import run_bass_kernel_spmd

F32 = mybir.dt.float32
BF16 = mybir.dt.bfloat16
ALU = mybir.AluOpType
AF = mybir.ActivationFunctionType
AX = mybir.AxisListType

EPS = 1e-6
NEG = -30000.0
D_FF = 2816
NFC = 22
ENGS = ("pe", "act", "dve", "pool", "sp")


class Prog:
    def __init__(self, nc, stack):
        self.nc = nc
        self.stack = stack
        self.sems = {}
        self.cnt = {}
        self.seen = {e: {} for e in ENGS}
        self._reset()

    def _reset(self):
        self.insts = []
        self.lw = {}
        self.rd = {}

    def sem(self, key):
        if key not in self.sems:
            self.sems[key] = self.stack.enter_context(self.nc.semaphore("s%d" % len(self.sems)))
            self.cnt[key] = 0
        return self.sems[key]

    def add(self, eng, fn, r=(), w=(), dma=None):
        idx = len(self.insts)
        deps = set()
        for k in r:
            if k in self.lw:
                deps.add(self.lw[k])
        for k in w:
            if k in self.lw:
                deps.add(self.lw[k])
            for d in self.rd.get(k, {}).values():
                deps.add(d)
        deps.discard(idx)
        self.insts.append(dict(eng=eng, fn=fn, deps=deps, dma=dma))
        stream = ("dma", dma) if dma is not None else eng
        for k in r:
            self.rd.setdefault(k, {})[stream] = idx
        for k in w:
            self.lw[k] = idx
            self.rd[k] = {}
        return idx

    def barrier(self):
        last = {}
        for i, ins in enumerate(self.insts):
            st = ("dma", ins["dma"]) if ins["dma"] is not None else ins["eng"]
            if ins["fn"] is not None:
                last[st] = i
        deps = set(last.values())
        for e in ENGS:
            self.insts.append(dict(eng=e, fn=None, deps=set(deps), dma=None))

    def flush(self):
        insts = self.insts
        signaled = set()
        for i, ins in enumerate(insts):
            best = {}
            for d in ins["deps"]:
                p = insts[d]
                if p["dma"] is not None:
                    st = ("dma", p["dma"])
                else:
                    st = p["eng"]
                    if st == ins["eng"] and ins["dma"] is None and st == "pe":
                        continue
                if st not in best or best[st] < d:
                    best[st] = d
            ins["rdeps"] = sorted(best.values())
            for d in ins["rdeps"]:
                signaled.add(d)
        tok = {}
        for i, ins in enumerate(insts):
            if ins["fn"] is None:
                continue
            if ins["dma"] is not None:
                key = ("dma", ins["dma"])
                self.sem(key)
                self.cnt[key] += 16
                tok[i] = (key, self.cnt[key])
            elif i in signaled:
                key = ins["eng"]
                self.sem(key)
                self.cnt[key] += 1
                tok[i] = (key, self.cnt[key])
        per = {e: [] for e in ENGS}
        for i, ins in enumerate(insts):
            per[ins["eng"]].append(i)

        def mk(en):
            def body(e):
                seen = self.seen[en]
                for i in per[en]:
                    ins = insts[i]
                    for d in ins["rdeps"]:
                        key, val = tok[d]
                        if seen.get(key, 0) < val:
                            e.wait_ge(self.sems[key], val)
                            seen[key] = val
                    if ins["fn"] is not None:
                        bi = ins["fn"](e)
                        if i in tok:
                            bi.then_inc(self.sems[tok[i][0]], 16 if ins["dma"] is not None else 1)
            return body

        with self.nc.Block() as block:
            block.tensor(mk("pe"))
            block.scalar(mk("act"))
            block.vector(mk("dve"))
            block.gpsimd(mk("pool"))
            block.sync(mk("sp"))
        self._reset()


def build_program(stage="full"):
    nc = bass.Bass("TRN2", target_bir_lowering=False)

    def din(name, shape, dt=F32):
        return nc.dram_tensor(name, list(shape), dt, kind="ExternalInput").ap()

    x_own = din("x_own", [4096, 1024])
    x_oth = din("x_oth", [4096, 1024])
    x_ext = din("x_ext", [4608, 1024])
    w_a = din("w_a", [1024, 768])
    w_b = din("w_b", [1024, 1536])
    w_out = din("w_out", [1024, 1024])
    w_g = din("w_g", [1024, D_FF])
    w_u = din("w_u", [1024, D_FF])
    w_d = din("w_d", [D_FF, 1024])
    g_mix = din("g_mix", [128, 8])
    g_ffn = din("g_ffn", [128, 8])
    g_qk = din("g_qk", [1, 640])
    g_oa = din("g_oa", [1, 512])
    g_ob = din("g_ob", [1, 512])
    g_fin = din("g_fin", [1, 1024])
    rope = din("rope", [64, 128, 128])
    bias_i = din("bias_i", [128, 8 * 640])
    bias_bd = din("bias_bd", [4, 8, 128, 768])
    ident = din("ident", [128, 128])
    out = nc.dram_tensor("out", [4096, 1024], F32, kind="ExternalOutput").ap()
    wgu_s = nc.dram_tensor("wgu_s", [NFC, 128, 2, 8, 128], BF16, kind="Internal").ap()
    wd_s = nc.dram_tensor("wd_s", [NFC, 128, 1024], BF16, kind="Internal").ap()
    dbg = None
    if stage != "full":
        dbg = nc.dram_tensor("dbg", [128, 8, 4096], BF16, kind="ExternalOutput").ap()

    with ExitStack() as top:
        P = Prog(nc, top)
        E = top.enter_context

        def sb(name, shape, dt=F32, st=None):
            return (st or top).enter_context(nc.sbuf_tensor(name, list(shape), dt))

        ps = E(nc.psum_tensor("ps", [128, 4096], F32))

        def bank(i, n=1):
            return ps[:, i * 512:(i + n) * 512]

        def bankb(i):
            return ps[:, i * 512:(i + 1) * 512].bitcast(BF16)

        identf = sb("identf", [128, 128])
        identb = sb("identb", [128, 128], BF16)
        gmix = sb("gmix", [128, 8])
        gffn = sb("gffn", [128, 8])
        epsA = sb("epsA", [128, 1])
        epsB = sb("epsB", [128, 1])
        mixbT = sb("mixbT", [128, 4, 4096], BF16)

        P.add("sp", lambda e: e.dma_start(out=identf[:], in_=ident), w=["identf"], dma="c0")
        P.add("sp", lambda e: e.dma_start(out=gmix[:], in_=g_mix), w=["gmix"], dma="c1")
        P.add("sp", lambda e: e.dma_start(out=gffn[:], in_=g_ffn), w=["gffn"], dma="c2")
        P.add("dve", lambda e: e.tensor_copy(out=identb[:], in_=identf[:]), r=["identf"], w=["identb"])
        P.add("dve", lambda e: e.memset(epsA[:], EPS), w=["epsA"])
        P.add("dve", lambda e: e.memset(epsB[:], 64.0 * EPS), w=["epsB"])

        def rms_tile(src_ap, xt, xs, ss, slot, pbank, hT_dst, hT_key, gtile, gkey, cnt):
            P.add("sp", lambda e: e.dma_start(out=xt[:, slot, :], in_=src_ap),
                  w=[("xt", slot)], dma=("xt", slot))
            P.add("act", lambda e: e.activation(out=xs[:, slot, :], in_=xt[:, slot, :], func=AF.Square,
                                               accum_out=ss[:, slot, 0:1]),
                  r=[("xt", slot)], w=[("xs", slot), ("ss", slot)])
            P.add("act", lambda e: e.activation(out=ss[:, slot, 1:2], in_=ss[:, slot, 0:1], func=AF.Sqrt,
                                               scale=1.0 / 1024, bias=epsA[:]),
                  r=[("ss", slot), "epsA"], w=[("ss", slot)])
            P.add("dve", lambda e: e.reciprocal(out=ss[:, slot, 2:3], in_=ss[:, slot, 1:2]),
                  r=[("ss", slot)], w=[("ss", slot)])
            P.add("dve", lambda e: e.tensor_scalar(out=xs[:, slot, :], in0=xt[:, slot, :],
                                                  scalar1=ss[:, slot, 2:3], scalar2=None, op0=ALU.mult),
                  r=[("xt", slot), ("ss", slot)], w=[("xs", slot)])
            pb = bankb(pbank)
            for dc in range(8):
                P.add("pe", lambda e, dc=dc: e.transpose(pb[:, dc * 128:(dc + 1) * 128],
                                                         xs[:, slot, dc * 128:(dc + 1) * 128], identb[:]),
                      r=[("xs", slot), "identb"], w=[("ps", pbank)])
            P.add("dve", lambda e: e.tensor_tensor(
                out=hT_dst, in0=pb[:, 0:1024].rearrange("p (c t) -> p c t", t=128),
                in1=gtile[:, :].unsqueeze(2).to_broadcast([128, 8, 128]), op=ALU.mult),
                r=[("ps", pbank), gkey], w=[hT_key])

        def load_weight_bf16(dst, src, ncols, wstg, keyname, gname):
            for dc in range(8):
                s = dc % 2
                P.add("sp", lambda e, dc=dc, s=s: e.dma_start(out=wstg[:, s, 0:ncols],
                                                             in_=src[dc * 128:(dc + 1) * 128, :]),
                      w=[("wstg", s)], dma=("wstg", s))
                if dc % 2 == 0:
                    P.add("dve", lambda e, dc=dc, s=s: e.tensor_copy(out=dst[:, dc, :], in_=wstg[:, s, 0:ncols]),
                          r=[("wstg", s)], w=[(keyname, dc)])
                else:
                    P.add("act", lambda e, dc=dc, s=s: e.activation(out=dst[:, dc, :], in_=wstg[:, s, 0:ncols],
                                                                   func=AF.Copy),
                          r=[("wstg", s)], w=[(keyname, dc)])

        with ExitStack() as sB:
            WB = sb("WB", [128, 8, 1536], BF16, sB)
            wstg = sb("wstg", [128, 2, 1536], F32, sB)
            cstg = sb("cstg", [128, 2, 1408], F32, sB)
            cbf = sb("cbf", [128, 2, 1408], BF16, sB)
            xt = sb("xt", [128, 2, 1024], F32, sB)
            xs = sb("xs", [128, 4, 1024], BF16, sB)
            ss = sb("ss", [128, 4, 4], F32, sB)
            hT = sb("hT", [128, 2, 8, 512], BF16, sB)
            QbT = sb("QbT", [128, 4, 1024], BF16, sB)
            KbT = sb("KbT", [128, 2, 4, 1536], BF16, sB)
            Vb = sb("Vb", [128, 12, 8, 65], BF16, sB)
            biasI = sb("biasI", [128, 8, 640], F32, sB)
            biasD = sb("biasD", [128, 2, 768], F32, sB)
            tmpS = sb("tmpS", [128, 2, 768], F32, sB)
            PT = sb("PT", [128, 2, 768], BF16, sB)
            OTs = sb("OTs", [65, 2, 512], F32, sB)
            gob = sb("gob", [128, 512], F32, sB)
            ob = sb("ob", [128, 512], F32, sB)
            obn = sb("obn", [128, 512], BF16, sB)
            junk = sb("junkb", [128, 512], BF16, sB)
            st2 = sb("st2", [128, 16], F32, sB)

            if stage in ("full", "B"):
                load_weight_bf16(WB, w_b, 1536, wstg, "WB", None)
                WBk = [("WB", dc) for dc in range(8)]
            if stage in ("full", "F"):
                jobs = []
                for which, src in ((0, w_g), (1, w_u)):
                    for dc in range(8):
                        for hf in range(2):
                            jobs.append(("gu", which, dc, hf, src))
                for fc in range(NFC):
                    jobs.append(("d", fc))
                for n, job in enumerate(jobs):
                    s = n % 2
                    if job[0] == "gu":
                        _, which, dc, hf, src = job
                        P.add("pool", lambda e, s=s, dc=dc, hf=hf, src=src: e.dma_start(
                            out=cstg[:, s, :], in_=src[dc * 128:(dc + 1) * 128, hf * 1408:(hf + 1) * 1408]),
                            r=([("WB", 7)] if n < 2 else []), w=[("cstg", s)], dma=("cstg", s))
                        P.add("pool", lambda e, s=s: e.tensor_copy(out=cbf[:, s, :], in_=cstg[:, s, :]),
                              r=[("cstg", s)], w=[("cbf", s)])
                        P.add("pool", lambda e, s=s, which=which, dc=dc, hf=hf: e.dma_start(
                            out=wgu_s[hf * 11:(hf + 1) * 11, :, which, dc, :].rearrange("f p c -> p f c"),
                            in_=cbf[:, s, :].rearrange("p (f c) -> p f c", c=128)),
                            r=[("cbf", s)], dma=("cbfo", s))
                    else:
                        fc = job[1]
                        P.add("pool", lambda e, s=s, fc=fc: e.dma_start(
                            out=cstg[:, s, 0:1024], in_=w_d[fc * 128:(fc + 1) * 128, :]),
                            w=[("cstg", s)], dma=("cstg", s))
                        P.add("pool", lambda e, s=s: e.tensor_copy(out=cbf[:, s, 0:1024], in_=cstg[:, s, 0:1024]),
                              r=[("cstg", s)], w=[("cbf", s)])
                        P.add("pool", lambda e, s=s, fc=fc: e.dma_start(out=wd_s[fc], in_=cbf[:, s, 0:1024]),
                              r=[("cbf", s)], dma=("cbfo", s))

            if stage in ("full", "B"):
                P.add("act", lambda e: e.dma_start(out=biasI[:].rearrange("p h n -> p (h n)"), in_=bias_i),
                      w=["biasI"], dma="biasI")
                P.add("act", lambda e: e.dma_start(out=gob[:], in_=g_ob.partition_broadcast(128)),
                      w=["gob"], dma="gob")
                P.add("dve", lambda e: e.memset(Vb[:].rearrange("p a h c -> p (a h c)"), 1.0), w=["Vb"])
                P.add("dve", lambda e: e.memset(KbT[:].rearrange("p a f k -> p (a f k)"), 0.0), w=["KbTz"])
                state = dict(tcount=0, bd_n=0)

                def emit_Ra(gb):
                    qt, bt = gb // 3, gb % 3
                    for t in range(4):
                        et = 8 * qt + 4 * bt + t
                        tc = state["tcount"]
                        state["tcount"] += 1
                        xsl = tc % 2
                        P.add("sp", lambda e, et=et, xsl=xsl: e.dma_start(out=xt[:, xsl, :],
                                                                         in_=x_ext[et * 128:(et + 1) * 128, :]),
                              w=[("xt", xsl)], dma=("xt", xsl))
                        P.add("act", lambda e, t=t, xsl=xsl: e.activation(out=xs[:, t, :], in_=xt[:, xsl, :], func=AF.Square,
                                                                       accum_out=ss[:, t, 0:1]),
                              r=[("xt", xsl)], w=[("xs", t), ("ss", t)])
                        P.add("act", lambda e, t=t: e.activation(out=ss[:, t, 1:2], in_=ss[:, t, 0:1], func=AF.Sqrt,
                                                                scale=1.0 / 1024, bias=epsA[:]),
                              r=[("ss", t), "epsA"], w=[("ss", t)])
                        P.add("dve", lambda e, t=t: e.reciprocal(out=ss[:, t, 2:3], in_=ss[:, t, 1:2]),
                              r=[("ss", t)], w=[("ss", t)])
                        P.add("dve", lambda e, t=t, xsl=xsl: e.tensor_scalar(
                            out=xs[:, t, :], in0=xt[:, xsl, :], scalar1=ss[:, t, 2:3], scalar2=None, op0=ALU.mult),
                            r=[("xt", xsl), ("ss", t)], w=[("xs", t)])

                def emit_Rb(gb):
                    hb = gb % 2
                    for t in range(4):
                        pbank = t % 2
                        pb = bankb(pbank)
                        for dc in range(8):
                            P.add("pe", lambda e, dc=dc, t=t, pb=pb: e.transpose(
                                pb[:, dc * 128:(dc + 1) * 128], xs[:, t, dc * 128:(dc + 1) * 128], identb[:]),
                                r=[("xs", t), "identb"], w=[("ps", pbank)])
                        P.add("dve", lambda e, t=t, pb=pb, pbank=pbank: e.tensor_tensor(
                            out=hT[:, hb, :, t * 128:(t + 1) * 128],
                            in0=pb[:, 0:1024].rearrange("p (c t) -> p c t", t=128),
                            in1=gmix[:, :].unsqueeze(2).to_broadcast([128, 8, 128]), op=ALU.mult),
                            r=[("ps", pbank), "gmix"], w=[("hT", hb, t)])

                def emit_M(gb):
                    qt, bt = gb // 3, gb % 3
                    hb = gb % 2
                    hTk = [("hT", hb, t) for t in range(4)]
                    for fc in range(4):
                        pbk = 2 + (fc % 2)
                        for dc in range(8):
                            P.add("pe", lambda e, fc=fc, dc=dc, pbk=pbk: e.matmul(
                                bank(pbk), lhsT=WB[:, dc, 512 + fc * 128:512 + (fc + 1) * 128],
                                rhs=hT[:, hb, dc, :], start=(dc == 0), stop=(dc == 7)),
                                r=hTk + WBk, w=[("ps", pbk)])
                        for par in range(2):
                            P.add("act", lambda e, fc=fc, pbk=pbk, bt=bt, par=par: e.activation(
                                out=KbT[par * 64:(par + 1) * 64, par, fc, bt * 512:(bt + 1) * 512],
                                in_=bank(pbk)[par * 64:(par + 1) * 64, :], func=AF.Copy),
                                r=[("ps", pbk), "KbTz"], w=[("KbT", bt, par)])
                    lo, hi = {0: (256, 512), 1: (0, 512), 2: (0, 256)}[bt]
                    qoff = {0: 0, 1: 256, 2: 768}[bt]
                    n = hi - lo
                    for fc in range(4):
                        pbk = 4 + (fc % 2)
                        for dc in range(8):
                            P.add("pe", lambda e, fc=fc, dc=dc, pbk=pbk, lo=lo, hi=hi, n=n: e.matmul(
                                bank(pbk)[:, 0:n], lhsT=WB[:, dc, fc * 128:(fc + 1) * 128],
                                rhs=hT[:, hb, dc, lo:hi], start=(dc == 0), stop=(dc == 7)),
                                r=hTk + WBk, w=[("ps", pbk)])
                        P.add("act", lambda e, fc=fc, pbk=pbk, n=n, qoff=qoff: e.activation(
                            out=QbT[:, fc, qoff:qoff + n], in_=bank(pbk)[:, 0:n], func=AF.Copy),
                            r=[("ps", pbk)], w=[("QbT", bt)])
                    for t in range(4):
                        pbk = 6 + (t % 2)
                        ch = 4 * bt + t
                        for dc in range(8):
                            P.add("pe", lambda e, t=t, dc=dc, pbk=pbk: e.matmul(
                                bank(pbk), lhsT=hT[:, hb, dc, t * 128:(t + 1) * 128],
                                rhs=WB[:, dc, 1024:1536], start=(dc == 0), stop=(dc == 7)),
                                r=hTk + WBk, w=[("ps", pbk)])
                        P.add("dve", lambda e, pbk=pbk, ch=ch: e.tensor_copy(
                            out=Vb[:, ch, :, 0:64], in_=bank(pbk).rearrange("p (h d) -> p h d", d=64)),
                            r=[("ps", pbk)], w=["Vb"])

                Kk = [("KbT", b, par) for b in range(3) for par in range(2)]
                Qk = [("QbT", b) for b in range(3)]

                def emit_attention(qt):
                    steps = [(jl, h) for jl in range(8) for h in range(8)]
                    info = {}

                    def blockinfo(jl):
                        j = 8 * qt + jl
                        if j in (0, 1):
                            return j, list(range(0, 6)), True
                        if j in (30, 31):
                            return j, list(range(6, 12)), True
                        return j, list(range(jl, jl + 5)), False

                    def emit_st(n):
                        jl, h = steps[n]
                        j, chunks, border = blockinfo(jl)
                        W = len(chunks) * 128
                        fc = h // 2
                        pb0 = (h % 2) * 64
                        sbk = 2 + 2 * (n % 2)
                        if border:
                            bslot = state["bd_n"] % 2
                            state["bd_n"] += 1
                            bidx = {0: 0, 1: 1, 30: 2, 31: 3}[j]
                            P.add("sp", lambda e, bslot=bslot, bidx=bidx, h=h: e.dma_start(
                                out=biasD[:, bslot, :], in_=bias_bd[bidx, h]),
                                w=[("biasD", bslot)], dma=("biasD", bslot))
                            info[n] = (biasD[:, bslot, 0:W], ("biasD", bslot))
                        else:
                            info[n] = (biasI[:, h, 0:W], "biasI")
                        for ci, cl in enumerate(chunks):
                            P.add("pe", lambda e, ci=ci, cl=cl, fc=fc, pb0=pb0, sbk=sbk, jl=jl: e.matmul(
                                ps[:, sbk * 512 + ci * 128: sbk * 512 + (ci + 1) * 128],
                                lhsT=KbT[:, pb0 // 64, fc, cl * 128:(cl + 1) * 128],
                                rhs=QbT[:, fc, jl * 128:(jl + 1) * 128], start=True, stop=True),
                                r=Kk + Qk, w=[("ps", sbk), ("ps", sbk + 1)])

                    def group_epilogue(grp, h):
                        osl = grp % 2
                        obk = 0
                        for q4 in range(4):
                            P.add("pe", lambda e, q4=q4, obk=obk, osl=osl: e.transpose(
                                ps[:, obk * 512 + q4 * 65: obk * 512 + (q4 + 1) * 65],
                                OTs[0:65, osl, q4 * 128:(q4 + 1) * 128], identf[0:65, 0:65]),
                                r=[("OTs", osl), "identf"], w=[("ps", obk)])
                        g0 = 0 if h == 3 else 4
                        src3 = ps[:, obk * 512: obk * 512 + 260].rearrange("p (h c) -> p h c", c=65)
                        P.add("dve", lambda e, src3=src3, g0=g0: e.reciprocal(
                            out=st2[:, g0:g0 + 4].unsqueeze(2), in_=src3[:, :, 64:65]),
                            r=[("ps", obk)], w=[("st2", g0)])
                        P.add("dve", lambda e, src3=src3, g0=g0: e.tensor_tensor(
                            out=ob[:, g0 * 64:(g0 + 4) * 64].rearrange("p (h d) -> p h d", d=64),
                            in0=src3[:, :, 0:64],
                            in1=st2[:, g0:g0 + 4].unsqueeze(2).to_broadcast([128, 4, 64]), op=ALU.mult),
                            r=[("ps", obk), ("st2", g0)], w=[("ob", g0)])

                    def block_epilogue(j):
                        P.add("act", lambda e: e.activation(out=junk[:, :], in_=ob[:, :], func=AF.Square,
                                                           accum_out=st2[:, 8:9]),
                              r=[("ob", 0), ("ob", 4)], w=["junk", ("st2", 8)])
                        P.add("act", lambda e: e.activation(out=st2[:, 9:10], in_=st2[:, 8:9], func=AF.Sqrt,
                                                           scale=1.0 / 512, bias=epsA[:]),
                              r=[("st2", 8), "epsA"], w=[("st2", 8)])
                        P.add("dve", lambda e: e.reciprocal(out=st2[:, 10:11], in_=st2[:, 9:10]),
                              r=[("st2", 8)], w=[("st2", 8)])
                        P.add("dve", lambda e: e.scalar_tensor_tensor(
                            out=obn[:, :], in0=ob[:, :], scalar=st2[:, 10:11], in1=gob[:, :],
                            op0=ALU.mult, op1=ALU.mult),
                            r=[("ob", 0), ("ob", 4), ("st2", 8), "gob"], w=["obn"])
                        pb = bankb(1)
                        for c in range(4):
                            P.add("pe", lambda e, c=c, pb=pb: e.transpose(
                                pb[:, c * 128:(c + 1) * 128], obn[:, c * 128:(c + 1) * 128], identb[:]),
                                r=["obn", "identb"], w=[("ps", 1)])
                        P.add("dve", lambda e, j=j, pb=pb: e.tensor_copy(
                            out=mixbT[:, :, j * 128:(j + 1) * 128],
                            in_=pb[:, 0:512].rearrange("p (c t) -> p c t", t=128)),
                            r=[("ps", 1)], w=[("mixbT", j)])

                    pending = []
                    NS = len(steps)

                    def sinfo(n):
                        jl, h = steps[n]
                        j, chunks, border = blockinfo(jl)
                        return jl, h, j, chunks, len(chunks) * 128, n % 2, 2 + 2 * (n % 2)

                    def emit_add(n):
                        jl, h, j, chunks, W, sslot, sbk = sinfo(n)
                        bsrc, bkey = info[n]
                        P.add("dve", lambda e: e.scalar_tensor_tensor(
                            out=tmpS[:, sslot, 0:W], in0=ps[:, sbk * 512: sbk * 512 + W], scalar=0.125,
                            in1=bsrc, op0=ALU.mult, op1=ALU.add),
                            r=[("ps", sbk), ("ps", sbk + 1), bkey], w=[("tmpS", sslot)])

                    def emit_exp(n):
                        jl, h, j, chunks, W, sslot, sbk = sinfo(n)
                        P.add("act", lambda e: e.activation(
                            out=PT[:, sslot, 0:W], in_=tmpS[:, sslot, 0:W], func=AF.Exp),
                            r=[("tmpS", sslot)], w=[("PT", sslot)])

                    def emit_pv(n):
                        jl, h, j, chunks, W, sslot, sbk = sinfo(n)
                        nch = len(chunks)
                        grp = n // 4
                        otb = 6 + (grp % 2)
                        hh = h % 4
                        for ci, cl in enumerate(chunks):
                            P.add("pe", lambda e, ci=ci, cl=cl: e.matmul(
                                ps[0:65, otb * 512 + hh * 128: otb * 512 + (hh + 1) * 128],
                                lhsT=Vb[:, cl, h, :], rhs=PT[:, sslot, ci * 128:(ci + 1) * 128],
                                start=(ci == 0), stop=(ci == nch - 1)),
                                r=["Vb", ("PT", sslot)], w=[("ps", otb)])
                        if hh == 3:
                            osl = grp % 2
                            P.add("act", lambda e: e.activation(
                                out=OTs[:, osl, :], in_=ps[0:65, otb * 512:(otb + 1) * 512], func=AF.Copy),
                                r=[("ps", otb)], w=[("OTs", osl)])
                            pending.append((n + 2, lambda: group_epilogue(grp, h)))
                            if h == 7:
                                pending.append((n + 3, lambda: block_epilogue(j)))
                        while pending and pending[0][0] <= n:
                            pending.pop(0)[1]()

                    for tau in range(NS + 3):
                        if 0 <= tau - 3 < NS:
                            emit_pv(tau - 3)
                        if 0 <= tau - 2 < NS:
                            emit_exp(tau - 2)
                        if 0 <= tau - 1 < NS:
                            emit_add(tau - 1)
                        if tau < NS:
                            emit_st(tau)
                    for _, fn in pending:
                        fn()

                NB = 12
                emit_Ra(0)
                emit_Rb(0)
                emit_Ra(1)
                emit_Rb(1)
                for gb in range(NB):
                    if gb + 2 < NB:
                        emit_Ra(gb + 2)
                    emit_M(gb)
                    if gb + 2 < NB:
                        emit_Rb(gb + 2)
                    if gb % 3 == 2:
                        emit_attention(gb // 3)
            P.barrier()
            if stage == "B":
                for c in range(4):
                    P.add("sp", lambda e, c=c: e.dma_start(out=dbg[:, c, :], in_=mixbT[:, c, :]), dma=("dbg", c))
                P.barrier()
            if stage == "A":
                for c in range(4):
                    P.add("sp", lambda e, c=c: e.dma_start(out=dbg[:, 4 + c, :], in_=mixbT[:, c, :]), dma=("dbg", 4 + c))
                P.barrier()
            P.flush()

        if stage == "B":
            return nc

        with ExitStack() as sQ:
            QT = sb("QT", [128, 4, 4096], BF16, sQ)
            with ExitStack() as sA:
                WA = sb("WA", [128, 8, 768], BF16, sA)
                wstg = sb("wstgA", [128, 2, 768], F32, sA)
                xt = sb("xtA", [128, 2, 1024], F32, sA)
                xs = sb("xsA", [128, 2, 1024], BF16, sA)
                ss = sb("ssA", [128, 2, 4], F32, sA)
                hT = sb("hTA", [128, 2, 8, 128], BF16, sA)
                KT = sb("KT", [128, 2, 8192], BF16, sA)
                Va = sb("Va", [128, 64, 2, 65], BF16, sA)
                rp = sb("rp", [128, 5, 128], F32, sA)
                gqk = sb("gqk", [128, 640], F32, sA)
                goa = sb("goa", [128, 512], F32, sA)
                sq = sb("sq", [128, 640], F32, sA)
                yv = sb("yv", [128, 2, 640], F32, sA)
                t1 = sb("t1", [128, 640], F32, sA)
                t2 = sb("t2", [128, 640], F32, sA)
                zf = sb("zf", [128, 2, 640], BF16, sA)
                st = sb("stA", [128, 2, 32], F32, sA)
                PTa = sb("PTa", [128, 4, 1024], BF16, sA)
                OTa = sb("OTa", [65, 2, 512], F32, sA)
                oa = sb("oa", [128, 4, 512], F32, sA)
                oan = sb("oan", [128, 2, 512], BF16, sA)
                junk = sb("junkA", [128, 512], BF16, sA)
                st3 = sb("st3", [128, 16], F32, sA)

                if stage in ("full", "A"):
                    load_weight_bf16(WA, w_a, 768, wstg, "WA", None)
                    WAk = [("WA", dc) for dc in range(8)]
                    P.add("sp", lambda e: e.dma_start(out=gqk[:], in_=g_qk.partition_broadcast(128)),
                          w=["gqk"], dma="gqk")
                    P.add("sp", lambda e: e.dma_start(out=goa[:], in_=g_oa.partition_broadcast(128)),
                          w=["goa"], dma="goa")
                    P.add("dve", lambda e: e.memset(Va[:].rearrange("p a h c -> p (a h c)"), 1.0), w=["Va"])
                    P.add("pool", lambda e: e.memset(KT[:].rearrange("p g k -> p (g k)"), 0.0), w=["KTz"])
                    NRP = 5

                    def a_stage(k, ti):
                        own = ti < 32
                        slot = ti % 2
                        rslot = ti % NRP
                        rb = 2 + 2 * slot
                        hk = [("hTA", slot)]
                        c0 = 0 if own else 512
                        nh = 10 if own else 2
                        Wc = nh * 64
                        reg = ps[:, rb * 512 + c0: rb * 512 + c0 + Wc]
                        pk = [("ps", rb), ("ps", rb + 1)] if own else [("ps", rb + 1)]
                        sl = slice(c0, c0 + Wc)
                        if k == 0:
                            src = x_own[ti * 128:(ti + 1) * 128, :] if own else x_oth[(ti - 32) * 128:(ti - 31) * 128, :]
                            P.add("sp", lambda e: e.dma_start(out=rp[:, rslot, :], in_=rope[ti]),
                                  w=[("rp", rslot)], dma=("rp", rslot))
                            P.add("sp", lambda e: e.dma_start(out=xt[:, slot, :], in_=src),
                                  w=[("xt", slot)], dma=("xt", slot))
                            P.add("act", lambda e: e.activation(out=xs[:, slot, :], in_=xt[:, slot, :], func=AF.Square,
                                                               accum_out=ss[:, slot, 0:1]),
                                  r=[("xt", slot)], w=[("xs", slot), ("ss", slot)])
                            P.add("act", lambda e: e.activation(out=ss[:, slot, 1:2], in_=ss[:, slot, 0:1], func=AF.Sqrt,
                                                               scale=1.0 / 1024, bias=epsA[:]),
                                  r=[("ss", slot), "epsA"], w=[("ss", slot)])
                            P.add("dve", lambda e: e.reciprocal(out=ss[:, slot, 2:3], in_=ss[:, slot, 1:2]),
                                  r=[("ss", slot)], w=[("ss", slot)])
                            P.add("dve", lambda e: e.tensor_scalar(out=xs[:, slot, :], in0=xt[:, slot, :],
                                                                  scalar1=ss[:, slot, 2:3], scalar2=None, op0=ALU.mult),
                                  r=[("xt", slot), ("ss", slot)], w=[("xs", slot)])
                        elif k == 1:
                            pb = bankb(slot)
                            for dc in range(8):
                                P.add("pe", lambda e, dc=dc: e.transpose(pb[:, dc * 128:(dc + 1) * 128],
                                                                         xs[:, slot, dc * 128:(dc + 1) * 128], identb[:]),
                                      r=[("xs", slot), "identb"], w=[("ps", slot)])
                            P.add("dve", lambda e: e.tensor_tensor(
                                out=hT[:, slot, :, :], in0=pb[:, 0:1024].rearrange("p (c t) -> p c t", t=128),
                                in1=gmix[:, :].unsqueeze(2).to_broadcast([128, 8, 128]), op=ALU.mult),
                                r=[("ps", slot), "gmix"], w=[("hTA", slot)])
                        elif k == 2:
                            if own:
                                for dc in range(8):
                                    P.add("pe", lambda e, dc=dc: e.matmul(
                                        bank(rb), lhsT=hT[:, slot, dc, :], rhs=WA[:, dc, 0:512],
                                        start=(dc == 0), stop=(dc == 7)), r=hk + WAk, w=[("ps", rb)])
                            for dc in range(8):
                                P.add("pe", lambda e, dc=dc: e.matmul(
                                    bank(rb + 1)[:, 0:256], lhsT=hT[:, slot, dc, :], rhs=WA[:, dc, 512:768],
                                    start=(dc == 0), stop=(dc == 7)), r=hk + WAk, w=[("ps", rb + 1)])
                        elif k == 3:
                            g_ap = gqk[:, c0:c0 + Wc]
                            P.add("act", lambda e: e.activation(out=sq[:, sl], in_=reg, func=AF.Square),
                                  r=pk, w=["sq"])
                            P.add("dve", lambda e: e.tensor_tensor(
                                out=yv[:, slot, sl], in0=reg, in1=g_ap, op=ALU.mult), r=pk + ["gqk"], w=[("yv", slot)])
                            P.add("dve", lambda e: e.tensor_copy(
                                out=Va[:, ti, :, 0:64],
                                in_=ps[:, (rb + 1) * 512 + 128:(rb + 1) * 512 + 256].rearrange("p (h d) -> p h d", d=64)),
                                r=[("ps", rb + 1)], w=[("Va", ti)])
                            P.add("dve", lambda e: e.tensor_reduce(
                                out=st[:, slot, 0:nh], in_=sq[:, sl].rearrange("p (h d) -> p h d", d=64),
                                axis=AX.X, op=ALU.add), r=["sq"], w=[("stA", slot)])
                            if own:
                                P.add("act", lambda e: e.activation(
                                    out=st[:, slot, 10:18], in_=st[:, slot, 0:8], func=AF.Sqrt, scale=1.0, bias=epsB[:]),
                                    r=[("stA", slot), "epsB"], w=[("stA", slot)])
                                P.add("act", lambda e: e.activation(
                                    out=st[:, slot, 18:20], in_=st[:, slot, 8:10], func=AF.Sqrt, scale=1.0 / 64,
                                    bias=epsA[:]), r=[("stA", slot), "epsA"], w=[("stA", slot)])
                                P.add("dve", lambda e: e.reciprocal(out=st[:, slot, 20:30], in_=st[:, slot, 10:20]),
                                      r=[("stA", slot)], w=[("stA", slot)])
                            else:
                                P.add("act", lambda e: e.activation(
                                    out=st[:, slot, 10:12], in_=st[:, slot, 0:2], func=AF.Sqrt, scale=1.0 / 64,
                                    bias=epsA[:]), r=[("stA", slot), "epsA"], w=[("stA", slot)])
                                P.add("dve", lambda e: e.reciprocal(out=st[:, slot, 20:22], in_=st[:, slot, 10:12]),
                                      r=[("stA", slot)], w=[("stA", slot)])
                        elif k == 4:
                            y5 = yv[:, slot, sl].rearrange("p (h a t s) -> p h a t s", a=2, t=2, s=16)
                            t25 = t2[:, sl].rearrange("p (h a t s) -> p h a t s", a=2, t=2, s=16)
                            C3 = rp[:, rslot, 0:64].unsqueeze(1).to_broadcast([128, nh, 64])
                            S4 = rp[:, rslot, 64:128].rearrange("p (a t s) -> p a t s", a=2, t=2)
                            P.add("dve", lambda e: e.tensor_tensor(
                                out=t1[:, sl].rearrange("p (h d) -> p h d", d=64),
                                in0=yv[:, slot, sl].rearrange("p (h d) -> p h d", d=64), in1=C3, op=ALU.mult),
                                r=[("yv", slot), ("rp", rslot)], w=["t1"])
                            for tt in range(2):
                                P.add("pool", lambda e, tt=tt: e.tensor_tensor(
                                    out=t25[:, :, :, tt, :], in0=y5[:, :, :, 1 - tt, :],
                                    in1=S4[:, :, tt, :].unsqueeze(1).to_broadcast([128, nh, 2, 16]), op=ALU.mult),
                                    r=[("yv", slot), ("rp", rslot)], w=[("t2", tt)])
                            P.add("dve", lambda e: e.tensor_tensor(out=t1[:, sl], in0=t1[:, sl], in1=t2[:, sl],
                                                                  op=ALU.add),
                                  r=["t1", ("t2", 0), ("t2", 1)], w=["t1"])
                            P.add("dve", lambda e: e.tensor_tensor(
                                out=zf[:, slot, sl].rearrange("p (h d) -> p h d", d=64),
                                in0=t1[:, sl].rearrange("p (h d) -> p h d", d=64),
                                in1=st[:, slot, 20:20 + nh].unsqueeze(2).to_broadcast([128, nh, 64]), op=ALU.mult),
                                r=["t1", ("stA", slot)], w=[("zf", slot)])
                        elif k == 5:
                            tb = 6 + slot
                            pb = bankb(tb)
                            cs = range(0, 5) if own else range(4, 5)
                            for c in cs:
                                P.add("pe", lambda e, c=c: e.transpose(
                                    pb[:, c * 128:(c + 1) * 128], zf[:, slot, c * 128:(c + 1) * 128], identb[:]),
                                    r=[("zf", slot), "identb"], w=[("ps", tb)])
                            if own:
                                P.add("act", lambda e: e.activation(
                                    out=QT[:, :, ti * 128:(ti + 1) * 128],
                                    in_=pb[:, 0:512].rearrange("p (c t) -> p c t", t=128), func=AF.Copy),
                                    r=[("ps", tb)], w=[("QT", ti // 4)])
                            for g in range(2):
                                P.add("act", lambda e, g=g: e.activation(
                                    out=KT[g * 64:(g + 1) * 64, g, ti * 128:(ti + 1) * 128],
                                    in_=pb[g * 64:(g + 1) * 64, 512:640], func=AF.Copy),
                                    r=[("ps", tb), "KTz"], w=[("KT", ti, g)])

                    NST = 6
                    for tau in range(64 + NST - 1):
                        for k in reversed(range(NST)):
                            ti = tau - k
                            if 0 <= ti < 64:
                                a_stage(k, ti)

                    KTk = [("KT", ti, g) for ti in range(64) for g in range(2)]
                    Vak = [("Va", ti) for ti in range(64)] + ["Va"]
                    steps = [(qb, i, g, kc2) for qb in range(8) for i in range(4) for g in range(2)
                             for kc2 in range(32)]
                    NS = len(steps)

                    def emit_st(n):
                        qb, i, g, kc2 = steps[n]
                        p0 = g * 64
                        sb2 = 2 * (n % 3)
                        for u in range(2):
                            kc = 2 * kc2 + u
                            P.add("pe", lambda e, u=u, kc=kc, sb2=sb2, g=g, i=i, qb=qb: e.matmul(
                                bank(sb2 + u), lhsT=KT[:, g, kc * 128:(kc + 1) * 128],
                                rhs=QT[:, i, qb * 512:(qb + 1) * 512], start=True, stop=True),
                                r=KTk + [("QT", qb)], w=[("ps", sb2 + u)])

                    def head_epilogue(hidx, qb, i, g):
                        osl = hidx % 2
                        fo = i * 128 + g * 64
                        for hp in range(2):
                            for q2 in range(2):
                                q4 = 2 * hp + q2
                                P.add("pe", lambda e, q4=q4, q2=q2, osl=osl: e.transpose(
                                    ps[:, 7 * 512 + q2 * 65: 7 * 512 + (q2 + 1) * 65],
                                    OTa[0:65, osl, q4 * 128:(q4 + 1) * 128], identf[0:65, 0:65]),
                                    r=[("OTa", osl), "identf"], w=[("ps", 7)])
                            src3 = ps[:, 7 * 512: 7 * 512 + 130].rearrange("p (t c) -> p t c", c=65)
                            P.add("dve", lambda e, src3=src3, hp=hp: e.reciprocal(
                                out=st3[:, 2 * hp:2 * hp + 2].unsqueeze(2), in_=src3[:, :, 64:65]),
                                r=[("ps", 7)], w=[("st3", hp)])
                            P.add("dve", lambda e, src3=src3, fo=fo, hp=hp: e.tensor_tensor(
                                out=oa[:, 2 * hp:2 * hp + 2, fo:fo + 64], in0=src3[:, :, 0:64],
                                in1=st3[:, 2 * hp:2 * hp + 2].unsqueeze(2).to_broadcast([128, 2, 64]), op=ALU.mult),
                                r=[("ps", 7), ("st3", hp)], w=["oa"])

                    def qb_epilogue(qb):
                        for t in range(4):
                            P.add("act", lambda e, t=t: e.activation(out=junk[:, :], in_=oa[:, t, :], func=AF.Square,
                                                                    accum_out=st3[:, 8 + t:9 + t]),
                                  r=["oa"], w=["junkA", ("st3b", t)])
                        for t in range(4):
                            P.add("act", lambda e, t=t: e.activation(out=st3[:, 8 + t:9 + t], in_=st3[:, 8 + t:9 + t],
                                                                    func=AF.Sqrt, scale=1.0 / 512, bias=epsA[:]),
                                  r=[("st3b", t), "epsA"], w=[("st3b", t)])
                        for t in range(4):
                            P.add("dve", lambda e, t=t: e.reciprocal(out=st3[:, 12 + t:13 + t], in_=st3[:, 8 + t:9 + t]),
                                  r=[("st3b", t)], w=[("st3c", t)])
                            P.add("dve", lambda e, t=t: e.scalar_tensor_tensor(
                                out=oan[:, t % 2, :], in0=oa[:, t, :], scalar=st3[:, 12 + t:13 + t], in1=goa[:, :],
                                op0=ALU.mult, op1=ALU.mult), r=["oa", ("st3c", t), "goa"], w=[("oan", t % 2)])
                            pb = bankb(7)
                            for c in range(4):
                                P.add("pe", lambda e, c=c, pb=pb, t=t: e.transpose(
                                    pb[:, c * 128:(c + 1) * 128], oan[:, t % 2, c * 128:(c + 1) * 128], identb[:]),
                                    r=[("oan", t % 2), "identb"], w=[("ps", 7)])
                            tok0 = qb * 512 + t * 128
                            P.add("dve", lambda e, pb=pb, tok0=tok0: e.tensor_copy(
                                out=QT[:, :, tok0:tok0 + 128],
                                in_=pb[:, 0:512].rearrange("p (c t) -> p c t", t=128)),
                                r=[("ps", 7)], w=[("QT", qb)])

                    pending = []
                    emit_st(0)
                    emit_st(1)
                    for n in range(NS):
                        qb, i, g, kc2 = steps[n]
                        hidx = n // 32
                        ob_ = 6
                        osl = hidx % 2
                        pslot = n % 4
                        sb2 = 2 * (n % 3)
                        if n + 2 < NS:
                            emit_st(n + 2)
                        P.add("act", lambda e, sb2=sb2, pslot=pslot: e.activation(
                            out=PTa[:, pslot, :], in_=bank(sb2, 2), func=AF.Exp),
                            r=[("ps", sb2), ("ps", sb2 + 1)], w=[("PTa", pslot)])
                        for u in range(2):
                            kc = 2 * kc2 + u
                            P.add("pe", lambda e, u=u, kc=kc, pslot=pslot, g=g, ob_=ob_: e.matmul(
                                ps[0:65, ob_ * 512:(ob_ + 1) * 512], lhsT=Va[:, kc, g, :],
                                rhs=PTa[:, pslot, u * 512:(u + 1) * 512],
                                start=(kc == 0), stop=(kc == 63)),
                                r=Vak + [("PTa", pslot)], w=[("ps", ob_)])
                        if kc2 == 31:
                            P.add("dve", lambda e, ob_=ob_, osl=osl: e.tensor_copy(
                                out=OTa[:, osl, :], in_=ps[0:65, ob_ * 512:(ob_ + 1) * 512]),
                                r=[("ps", ob_)], w=[("OTa", osl)])
                            pending.append((n + 3, lambda hidx=hidx, qb=qb, i=i, g=g: head_epilogue(hidx, qb, i, g)))
                            if i == 3 and g == 1:
                                pending.append((n + 6, lambda qb=qb: qb_epilogue(qb)))
                        while pending and pending[0][0] <= n:
                            pending.pop(0)[1]()
                    for _, fn in pending:
                        fn()
                P.barrier()
                if stage == "A":
                    for c in range(4):
                        P.add("sp", lambda e, c=c: e.dma_start(out=dbg[:, c, :], in_=QT[:, c, :]), dma=("dbg", c))
                    P.barrier()
                P.flush()

            if stage == "A":
                return nc

            with ExitStack() as sF:
                Wo = sb("Wo", [128, 8, 1024], BF16, sF)
                wstg = sb("wstgF", [128, 2, 1024], F32, sF)
                x1 = sb("x1", [128, 2, 4, 1024], F32, sF)
                xsF = sb("xsF", [128, 2, 1024], BF16, sF)
                ssF = sb("ssF", [128, 2, 4, 8], F32, sF)
                h2T = sb("h2T", [128, 2, 8, 512], BF16, sF)
                actT = sb("actT", [128, 2, 4, 512], BF16, sF)
                wgu = sb("wgu", [128, 6, 2, 8, 128], BF16, sF)
                wdr = sb("wdr", [128, 6, 1024], BF16, sF)
                sg = sb("sg", [128, 2, 512], F32, sF)
                gfin = sb("gfin", [128, 1024], F32, sF)
                ost = sb("ost", [128, 2, 1024], F32, sF)
                junkF = sb("junkF", [128, 1024], BF16, sF)

                load_weight_bf16(Wo, w_out, 1024, wstg, "Wo", None)
                Wok = [("Wo", dc) for dc in range(8)]
                P.add("sp", lambda e: e.dma_start(out=gfin[:], in_=g_fin.partition_broadcast(128)),
                      w=["gfin"], dma="gfin")
                cnt = dict(w=0, x=0, o=0, g=0)
                groups = [list(range(s0, min(s0 + 4, NFC))) for s0 in range(0, NFC, 4)]

                def f_wout(tb, t):
                    xb = tb % 2
                    yb = 2 * (t % 2)
                    tok0 = tb * 512 + t * 128
                    P.add("sp", lambda e: e.dma_start(out=x1[:, xb, t, :], in_=x_own[tok0:tok0 + 128, :]),
                          w=[("x1", xb, t)], dma=("x1", xb, t))
                    for hf in range(2):
                        for c in range(8):
                            src = QT[:, c, tok0:tok0 + 128] if c < 4 else mixbT[:, c - 4, tok0:tok0 + 128]
                            P.add("pe", lambda e, hf=hf, c=c, src=src: e.matmul(
                                bank(yb + hf), lhsT=src, rhs=Wo[:, c, hf * 512:(hf + 1) * 512],
                                start=(c == 0), stop=(c == 7)), r=Wok, w=[("ps", yb + hf)])
                    P.add("dve", lambda e: e.tensor_tensor(out=x1[:, xb, t, :], in0=bank(yb, 2),
                                                          in1=x1[:, xb, t, :], op=ALU.add),
                          r=[("ps", yb), ("ps", yb + 1), ("x1", xb, t)], w=[("x1", xb, t)])
                    P.add("act", lambda e: e.activation(out=junkF[:, :], in_=x1[:, xb, t, :], func=AF.Square,
                                                       accum_out=ssF[:, xb, t, 0:1]),
                          r=[("x1", xb, t)], w=["junkF", ("ssF", xb, t)])
                    P.add("act", lambda e: e.activation(out=ssF[:, xb, t, 1:2], in_=ssF[:, xb, t, 0:1],
                                                       func=AF.Sqrt, scale=1.0 / 1024, bias=epsA[:]),
                          r=[("ssF", xb, t), "epsA"], w=[("ssF", xb, t)])
                    P.add("dve", lambda e: e.reciprocal(out=ssF[:, xb, t, 2:3], in_=ssF[:, xb, t, 1:2]),
                          r=[("ssF", xb, t)], w=[("ssF", xb, t)])
                    xsl = t % 2
                    P.add("dve", lambda e: e.tensor_scalar(
                        out=xsF[:, xsl, :], in0=x1[:, xb, t, :], scalar1=ssF[:, xb, t, 2:3], scalar2=None,
                        op0=ALU.mult),
                        r=[("x1", xb, t), ("ssF", xb, t)], w=[("xsF", xsl)])

                def f_tr(tb, t):
                    xb = tb % 2
                    xsl = t % 2
                    pbk = 2 * (t % 2) + 1
                    pb = bankb(pbk)
                    for dc in range(8):
                        P.add("pe", lambda e, dc=dc: e.transpose(
                            pb[:, dc * 128:(dc + 1) * 128], xsF[:, xsl, dc * 128:(dc + 1) * 128], identb[:]),
                            r=[("xsF", xsl), "identb"], w=[("ps", pbk)])
                    P.add("dve", lambda e: e.tensor_tensor(
                        out=h2T[:, xb, :, t * 128:(t + 1) * 128],
                        in0=pb[:, 0:1024].rearrange("p (c t) -> p c t", t=128),
                        in1=gffn[:, :].unsqueeze(2).to_broadcast([128, 8, 128]), op=ALU.mult),
                        r=[("ps", pbk), "gffn"], w=[("h2T", xb, t)])

                def f_prologue(tb):
                    for t in range(4):
                        f_wout(tb, t)
                        if t >= 1:
                            f_tr(tb, t - 1)
                    f_tr(tb, 3)

                def f_group(tb, grp):
                    xb = tb % 2
                    h2k = [("h2T", xb, t) for t in range(4)]
                    asl = cnt["g"] % 2
                    cnt["g"] += 1
                    wsl = {}
                    for fc in grp:
                        s_ = cnt["w"] % 6
                        cnt["w"] += 1
                        wsl[fc] = s_
                        P.add("sp", lambda e, s_=s_, fc=fc: e.dma_start(
                            out=wgu[:, s_].rearrange("p a c f -> p (a c f)"),
                            in_=wgu_s[fc].rearrange("p a c f -> p (a c f)")),
                            w=[("wgu", s_)], dma=("wgu", s_))
                        P.add("sp", lambda e, s_=s_, fc=fc: e.dma_start(out=wdr[:, s_, :], in_=wd_s[fc]),
                              w=[("wdr", s_)], dma=("wdr", s_))
                    for k, fc in enumerate(grp):
                        s_ = wsl[fc]
                        gb = 4 + 2 * (k % 2)
                        for a_ in range(2):
                            for dc in range(8):
                                P.add("pe", lambda e, a_=a_, dc=dc, s_=s_, gb=gb: e.matmul(
                                    bank(gb + a_), lhsT=wgu[:, s_, a_, dc, :], rhs=h2T[:, xb, dc, :],
                                    start=(dc == 0), stop=(dc == 7)),
                                    r=h2k + [("wgu", s_)], w=[("ps", gb + a_)])
                        P.add("act", lambda e, gb=gb, k=k: e.activation(out=sg[:, k % 2, :], in_=bank(gb), func=AF.Silu),
                              r=[("ps", gb)], w=[("sg", k % 2)])
                        P.add("dve", lambda e, gb=gb, k=k, asl=asl: e.tensor_tensor(
                            out=actT[:, asl, k, :], in0=bank(gb + 1), in1=sg[:, k % 2, :], op=ALU.mult),
                            r=[("ps", gb + 1), ("sg", k % 2)], w=[("actT", asl, k)])
                    ak = [("actT", asl, k) for k in range(len(grp))]
                    for t in range(4):
                        for hf in range(2):
                            db = (2 * t + hf) % 4
                            for k, fc in enumerate(grp):
                                s_ = wsl[fc]
                                P.add("pe", lambda e, t=t, hf=hf, k=k, s_=s_, db=db, asl=asl, n=len(grp): e.matmul(
                                    bank(db), lhsT=actT[:, asl, k, t * 128:(t + 1) * 128],
                                    rhs=wdr[:, s_, hf * 512:(hf + 1) * 512], start=(k == 0), stop=(k == n - 1)),
                                    r=ak + [("wdr", s_)], w=[("ps", db)])
                            P.add("dve", lambda e, t=t, hf=hf, db=db: e.tensor_tensor(
                                out=x1[:, xb, t, hf * 512:(hf + 1) * 512], in0=bank(db),
                                in1=x1[:, xb, t, hf * 512:(hf + 1) * 512], op=ALU.add),
                                r=[("ps", db), ("x1", xb, t)], w=[("x1", xb, t)])

                def f_epilogue(tb):
                    xb = tb % 2
                    for t in range(4):
                        P.add("act", lambda e, t=t: e.activation(out=junkF[:, :], in_=x1[:, xb, t, :], func=AF.Square,
                                                                accum_out=ssF[:, xb, t, 4:5]),
                              r=[("x1", xb, t)], w=["junkF", ("ssF", xb, t)])
                    for t in range(4):
                        P.add("act", lambda e, t=t: e.activation(out=ssF[:, xb, t, 5:6], in_=ssF[:, xb, t, 4:5],
                                                                func=AF.Sqrt, scale=1.0 / 1024, bias=epsA[:]),
                              r=[("ssF", xb, t), "epsA"], w=[("ssF", xb, t)])
                    for t in range(4):
                        osl = cnt["o"] % 2
                        cnt["o"] += 1
                        tok0 = tb * 512 + t * 128
                        P.add("dve", lambda e, t=t: e.reciprocal(out=ssF[:, xb, t, 6:7], in_=ssF[:, xb, t, 5:6]),
                              r=[("ssF", xb, t)], w=[("ssF", xb, t)])
                        P.add("dve", lambda e, t=t, osl=osl: e.scalar_tensor_tensor(
                            out=ost[:, osl, :], in0=x1[:, xb, t, :], scalar=ssF[:, xb, t, 6:7], in1=gfin[:, :],
                            op0=ALU.mult, op1=ALU.mult),
                            r=[("x1", xb, t), ("ssF", xb, t), "gfin"], w=[("ost", osl)])
                        P.add("pool", lambda e, osl=osl, tok0=tok0: e.dma_start(out=out[tok0:tok0 + 128, :],
                                                                             in_=ost[:, osl, :]),
                              r=[("ost", osl)], dma=("ost", osl))

                f_prologue(0)
                for tb in range(8):
                    for gi, grp in enumerate(groups):
                        f_group(tb, grp)
                        if tb + 1 < 8:
                            if gi < 4:
                                f_wout(tb + 1, gi)
                            if 1 <= gi <= 4:
                                f_tr(tb + 1, gi - 1)
                    f_epilogue(tb)
                P.barrier()
                P.flush()
    return nc


def _rope_tables(tok):
    row = (tok // 64).astype(np.float32)
    col = (tok % 64).astype(np.float32)
    inv = (np.float32(10000.0) ** (-np.arange(0, 32, 2, dtype=np.float32) / np.float32(32))).astype(np.float32)
    ar = (row[:, None] * inv[None, :]).astype(np.float32)
    ac = (col[:, None] * inv[None, :]).astype(np.float32)
    n = tok.shape[0]
    C = np.empty((n, 2, 2, 16), np.float32)
    S = np.empty((n, 2, 2, 16), np.float32)
    for a, ang in enumerate((ar, ac)):
        c = np.cos(ang).astype(np.float32)
        s = np.sin(ang).astype(np.float32)
        C[:, a, 0] = c
        C[:, a, 1] = c
        S[:, a, 0] = -s
        S[:, a, 1] = s
    return np.concatenate([C.reshape(n, 64), S.reshape(n, 64)], axis=1)


def _bias_table(rpb, own_row0, j, w0, nch):
    p = np.arange(128)
    ci = np.arange(nch)
    e = w0 + 2 * ci[:, None] + (p[None, :] // 64)
    kr = own_row0 - 4 + e
    kc = np.broadcast_to(p[None, :] % 64, kr.shape)
    q = np.arange(128)
    r = own_row0 + 2 * j + q // 64
    c = q % 64
    rs = np.clip(r - 4, 0, 120)
    cs = np.clip(c - 8, 0, 48)
    KR = kr[:, :, None]
    KC = kc[:, :, None]
    valid = (KR >= 0) & (KR <= 127) & (KR >= rs[None, None, :]) & (KR < rs[None, None, :] + 8) \
        & (KC >= cs[None, None, :]) & (KC < cs[None, None, :] + 16)
    dr = np.clip(KR - r[None, None, :] + 7, 0, 14)
    dc = np.clip(KC - c[None, None, :] + 15, 0, 30)
    vals = rpb[:, dr, dc]
    tab = np.where(valid[None], vals, np.float32(NEG)).astype(np.float32)
    return np.ascontiguousarray(tab.transpose(0, 2, 1, 3)).reshape(8, 128, nch * 128)


def prep_inputs(inputs):
    f = lambda a: np.ascontiguousarray(np.asarray(a, dtype=np.float32))
    x = f(inputs["x"])
    w_in = f(inputs["w_in"])[0]
    qperm = np.array([(4 * g + i) * 64 + d for i in range(4) for g in range(2) for d in range(64)])
    w_qa = w_in[:, 0:512][:, qperm]
    w_ka = w_in[:, 512:640]
    w_va = w_in[:, 640:768]
    w_qb = w_in[:, 768:1280]
    w_kb = w_in[:, 1280:1792]
    w_vb = w_in[:, 1792:2304]
    w_a = f(np.concatenate([w_qa, w_ka, w_va], axis=1))
    w_b = f(np.concatenate([w_qb, w_kb, w_vb], axis=1))
    w_out = f(inputs["w_out"])[0]
    w_out_p = f(np.concatenate([w_out[0:512][qperm], w_out[512:1024]], axis=0))
    g_oa = f(inputs["out_norm_a"])[0][qperm].reshape(1, 512)
    g_ob = f(inputs["out_norm_b"])[0].reshape(1, 512)
    g_mix = f(f(inputs["norm_mix"])[0].reshape(8, 128).T)
    g_ffn = f(f(inputs["norm_ffn"])[0].reshape(8, 128).T)
    g_fin = f(inputs["norm_final"]).reshape(1, 1024)
    g_qk = f(np.concatenate([np.tile(f(inputs["q_norm_a"])[0], 8), np.tile(f(inputs["k_norm_a"])[0], 2)])).reshape(1, 640)
    rpb = f(inputs["rpb_b"])[0]
    w_g = f(inputs["w_gate"])[0]
    w_u = f(inputs["w_up"])[0]
    w_d = f(inputs["w_down"])[0]
    ident = np.eye(128, dtype=np.float32)
    bias_i = _bias_table(rpb, 0, 5, 10, 5)
    bias_i = f(bias_i.transpose(1, 0, 2).reshape(128, 8 * 640))
    in_maps = []
    for c in range(8):
        b, hf = c // 2, c % 2
        t0 = 4096 * hf
        o0 = 4096 * (1 - hf)
        row0 = 64 * hf
        x_own = x[b, t0:t0 + 4096]
        x_oth = x[b, o0:o0 + 4096]
        x_ext = np.zeros((72, 64, 1024), np.float32)
        for e in range(72):
            gr = row0 - 4 + e
            if 0 <= gr < 128:
                x_ext[e] = x[b, gr * 64:(gr + 1) * 64]
        tok = np.concatenate([np.arange(t0, t0 + 4096), np.arange(o0, o0 + 4096)])
        rope = _rope_tables(tok).reshape(64, 128, 128)
        bd = np.stack([_bias_table(rpb, row0, 0, 0, 6), _bias_table(rpb, row0, 1, 0, 6),
                       _bias_table(rpb, row0, 30, 60, 6), _bias_table(rpb, row0, 31, 60, 6)])
        in_maps.append(dict(
            x_own=f(x_own), x_oth=f(x_oth), x_ext=f(x_ext.reshape(4608, 1024)),
            w_a=w_a, w_b=w_b, w_out=w_out_p, w_g=w_g, w_u=w_u, w_d=w_d,
            g_mix=g_mix, g_ffn=g_ffn, g_qk=g_qk, g_oa=g_oa, g_ob=g_ob, g_fin=g_fin,
            rope=f(rope), bias_i=bias_i, bias_bd=f(bd), ident=ident))
    return in_maps


def kernel(**inputs):
    in_maps = prep_inputs(inputs)
    nc = build_program("full")
    res = run_bass_kernel_spmd(nc, in_maps, core_ids=list(range(8)))
    out = np.empty((4, 8192, 1024), np.float32)
    for c in range(8):
        b, hf = c // 2, c % 2
        out[b, 4096 * hf:4096 * (hf + 1)] = np.asarray(res.results[c]["out"], dtype=np.float32)
    return out
```

```python
from contextlib import ExitStack

import numpy as np
import ml_dtypes

import concourse.bass as bass
import concourse.mybir as mybir
from concourse.bass_utils import run_bass_kernel_spmd

F32 = mybir.dt.float32
BF16 = mybir.dt.bfloat16
ALU = mybir.AluOpType
AF = mybir.ActivationFunctionType
AX = mybir.AxisListType

EPS = 1e-6
NEG = -30000.0
D_FF = 2816
NFC = 22
ENGS = ("pe", "act", "dve", "pool", "sp")


class Prog:
    def __init__(self, nc, stack):
        self.nc = nc
        self.stack = stack
        self.sems = {}
        self.cnt = {}
        self.seen = {e: {} for e in ENGS}
        self._reset()

    def _reset(self):
        self.insts = []
        self.lw = {}
        self.rd = {}

    def sem(self, key):
        if key not in self.sems:
            self.sems[key] = self.stack.enter_context(self.nc.semaphore("s%d" % len(self.sems)))
            self.cnt[key] = 0
        return self.sems[key]

    def add(self, eng, fn, r=(), w=(), dma=None):
        idx = len(self.insts)
        deps = set()
        for k in r:
            if k in self.lw:
                deps.add(self.lw[k])
        for k in w:
            if k in self.lw:
                deps.add(self.lw[k])
            for d in self.rd.get(k, {}).values():
                deps.add(d)
        deps.discard(idx)
        self.insts.append(dict(eng=eng, fn=fn, deps=deps, dma=dma))
        stream = ("dma", dma) if dma is not None else eng
        for k in r:
            self.rd.setdefault(k, {})[stream] = idx
        for k in w:
            self.lw[k] = idx
            self.rd[k] = {}
        return idx

    def barrier(self):
        last = {}
        for i, ins in enumerate(self.insts):
            st = ("dma", ins["dma"]) if ins["dma"] is not None else ins["eng"]
            if ins["fn"] is not None:
                last[st] = i
        deps = set(last.values())
        for e in ENGS:
            self.insts.append(dict(eng=e, fn=None, deps=set(deps), dma=None))

    def flush(self):
        insts = self.insts
        signaled = set()
        for i, ins in enumerate(insts):
            best = {}
            for d in ins["deps"]:
                p = insts[d]
                if p["dma"] is not None:
                    st = ("dma", p["dma"])
                else:
                    st = p["eng"]
                    if st == ins["eng"] and ins["dma"] is None and st == "pe":
                        continue
                if st not in best or best[st] < d:
                    best[st] = d
            ins["rdeps"] = sorted(best.values())
            for d in ins["rdeps"]:
                signaled.add(d)
        tok = {}
        for i, ins in enumerate(insts):
            if ins["fn"] is None:
                continue
            if ins["dma"] is not None:
                key = ("dma", ins["dma"])
                self.sem(key)
                self.cnt[key] += 16
                tok[i] = (key, self.cnt[key])
            elif i in signaled:
                key = ins["eng"]
                self.sem(key)
                self.cnt[key] += 1
                tok[i] = (key, self.cnt[key])
        per = {e: [] for e in ENGS}
        for i, ins in enumerate(insts):
            per[ins["eng"]].append(i)

        def mk(en):
            def body(e):
                seen = self.seen[en]
                for i in per[en]:
                    ins = insts[i]
                    for d in ins["rdeps"]:
                        key, val = tok[d]
                        if seen.get(key, 0) < val:
                            e.wait_ge(self.sems[key], val)
                            seen[key] = val
                    if ins["fn"] is not None:
                        bi = ins["fn"](e)
                        if i in tok:
                            bi.then_inc(self.sems[tok[i][0]], 16 if ins["dma"] is not None else 1)
            return body

        with self.nc.Block() as block:
            block.tensor(mk("pe"))
            block.scalar(mk("act"))
            block.vector(mk("dve"))
            block.gpsimd(mk("pool"))
            block.sync(mk("sp"))
        self._reset()


def build_program(stage="full"):
    nc = bass.Bass("TRN2", target_bir_lowering=False)

    def din(name, shape, dt=F32):
        return nc.dram_tensor(name, list(shape), dt, kind="ExternalInput").ap()

    x_own = din("x_own", [4096, 1024])
    x_oth = din("x_oth", [4096, 1024])
    x_ext = din("x_ext", [4608, 1024])
    w_a = din("w_a", [1024, 768])
    w_b = din("w_b", [1024, 1536])
    w_out = din("w_out", [1024, 1024])
    w_g = din("w_g", [1024, D_FF])
    w_u = din("w_u", [1024, D_FF])
    w_d = din("w_d", [D_FF, 1024])
    g_mix = din("g_mix", [128, 8])
    g_ffn = din("g_ffn", [128, 8])
    g_qk = din("g_qk", [1, 640])
    g_oa = din("g_oa", [1, 512])
    g_ob = din("g_ob", [1, 512])
    g_fin = din("g_fin", [1, 1024])
    rope = din("rope", [64, 128, 128])
    bias_i = din("bias_i", [128, 8 * 640])
    bias_bd = din("bias_bd", [4, 8, 128, 768])
    ident = din("ident", [128, 128])
    out = nc.dram_tensor("out", [4096, 1024], F32, kind="ExternalOutput").ap()
    wgu_s = nc.dram_tensor("wgu_s", [NFC, 128, 2, 8, 128], BF16, kind="Internal").ap()
    wd_s = nc.dram_tensor("wd_s", [NFC, 128, 1024], BF16, kind="Internal").ap()
    dbg = None
    if stage != "full":
        dbg = nc.dram_tensor("dbg", [128, 8, 4096], BF16, kind="ExternalOutput").ap()

    with ExitStack() as top:
        P = Prog(nc, top)
        E = top.enter_context

        def sb(name, shape, dt=F32, st=None):
            return (st or top).enter_context(nc.sbuf_tensor(name, list(shape), dt))

        ps = E(nc.psum_tensor("ps", [128, 4096], F32))

        def bank(i, n=1):
            return ps[:, i * 512:(i + n) * 512]

        def bankb(i):
            return ps[:, i * 512:(i + 1) * 512].bitcast(BF16)

        identf = sb("identf", [128, 128])
        identb = sb("identb", [128, 128], BF16)
        gmix = sb("gmix", [128, 8])
        gffn = sb("gffn", [128, 8])
        epsA = sb("epsA", [128, 1])
        epsB = sb("epsB", [128, 1])
        mixbT = sb("mixbT", [128, 4, 4096], BF16)

        P.add("sp", lambda e: e.dma_start(out=identf[:], in_=ident), w=["identf"], dma="c0")
        P.add("sp", lambda e: e.dma_start(out=gmix[:], in_=g_mix), w=["gmix"], dma="c1")
        P.add("sp", lambda e: e.dma_start(out=gffn[:], in_=g_ffn), w=["gffn"], dma="c2")
        P.add("dve", lambda e: e.tensor_copy(out=identb[:], in_=identf[:]), r=["identf"], w=["identb"])
        P.add("dve", lambda e: e.memset(epsA[:], EPS), w=["epsA"])
        P.add("dve", lambda e: e.memset(epsB[:], 64.0 * EPS), w=["epsB"])

        def rms_tile(src_ap, xt, xs, ss, slot, pbank, hT_dst, hT_key, gtile, gkey, cnt):
            P.add("sp", lambda e: e.dma_start(out=xt[:, slot, :], in_=src_ap),
                  w=[("xt", slot)], dma=("xt", slot))
            P.add("act", lambda e: e.activation(out=xs[:, slot, :], in_=xt[:, slot, :], func=AF.Square,
                                               accum_out=ss[:, slot, 0:1]),
                  r=[("xt", slot)], w=[("xs", slot), ("ss", slot)])
            P.add("act", lambda e: e.activation(out=ss[:, slot, 1:2], in_=ss[:, slot, 0:1], func=AF.Sqrt,
                                               scale=1.0 / 1024, bias=epsA[:]),
                  r=[("ss", slot), "epsA"], w=[("ss", slot)])
            P.add("dve", lambda e: e.reciprocal(out=ss[:, slot, 2:3], in_=ss[:, slot, 1:2]),
                  r=[("ss", slot)], w=[("ss", slot)])
            P.add("dve", lambda e: e.tensor_scalar(out=xs[:, slot, :], in0=xt[:, slot, :],
                                                  scalar1=ss[:, slot, 2:3], scalar2=None, op0=ALU.mult),
                  r=[("xt", slot), ("ss", slot)], w=[("xs", slot)])
            pb = bankb(pbank)
            for dc in range(8):
                P.add("pe", lambda e, dc=dc: e.transpose(pb[:, dc * 128:(dc + 1) * 128],
                                                         xs[:, slot, dc * 128:(dc + 1) * 128], identb[:]),
                      r=[("xs", slot), "identb"], w=[("ps", pbank)])
            P.add("dve", lambda e: e.tensor_tensor(
                out=hT_dst, in0=pb[:, 0:1024].rearrange("p (c t) -> p c t", t=128),
                in1=gtile[:, :].unsqueeze(2).to_broadcast([128, 8, 128]), op=ALU.mult),
                r=[("ps", pbank), gkey], w=[hT_key])

        def load_weight_bf16(dst, src, ncols, wstg, keyname, gname):
            for dc in range(8):
                s = dc % 2
                P.add("sp", lambda e, dc=dc, s=s: e.dma_start(out=wstg[:, s, 0:ncols],
                                                             in_=src[dc * 128:(dc + 1) * 128, :]),
                      w=[("wstg", s)], dma=("wstg", s))
                if dc % 2 == 0:
                    P.add("dve", lambda e, dc=dc, s=s: e.tensor_copy(out=dst[:, dc, :], in_=wstg[:, s, 0:ncols]),
                          r=[("wstg", s)], w=[(keyname, dc)])
                else:
                    P.add("act", lambda e, dc=dc, s=s: e.activation(out=dst[:, dc, :], in_=wstg[:, s, 0:ncols],
                                                                   func=AF.Copy),
                          r=[("wstg", s)], w=[(keyname, dc)])

        with ExitStack() as sB:
            WB = sb("WB", [128, 8, 1536], BF16, sB)
            wstg = sb("wstg", [128, 2, 1536], F32, sB)
            cstg = sb("cstg", [128, 2, 1408], F32, sB)
            cbf = sb("cbf", [128, 2, 1408], BF16, sB)
            xt = sb("xt", [128, 2, 1024], F32, sB)
            xs = sb("xs", [128, 4, 1024], BF16, sB)
            ss = sb("ss", [128, 4, 4], F32, sB)
            hT = sb("hT", [128, 2, 8, 512], BF16, sB)
            QbT = sb("QbT", [128, 4, 1024], BF16, sB)
            KbT = sb("KbT", [128, 4, 1536], BF16, sB)
            Vb = sb("Vb", [128, 12, 8, 65], BF16, sB)
            biasI = sb("biasI", [128, 8, 640], F32, sB)
            biasD = sb("biasD", [128, 2, 768], F32, sB)
            tmpS = sb("tmpS", [128, 2, 768], F32, sB)
            PT = sb("PT", [128, 2, 768], BF16, sB)
            OTs = sb("OTs", [65, 2, 512], F32, sB)
            gob = sb("gob", [128, 512], F32, sB)
            ob = sb("ob", [128, 512], F32, sB)
            obn = sb("obn", [128, 512], BF16, sB)
            junk = sb("junkb", [128, 512], BF16, sB)
            st2 = sb("st2", [128, 16], F32, sB)

            if stage in ("full", "B"):
                load_weight_bf16(WB, w_b, 1536, wstg, "WB", None)
                WBk = [("WB", dc) for dc in range(8)]
            if stage in ("full", "F"):
                jobs = []
                for which, src in ((0, w_g), (1, w_u)):
                    for dc in range(8):
                        for hf in range(2):
                            jobs.append(("gu", which, dc, hf, src))
                for fc in range(NFC):
                    jobs.append(("d", fc))
                for n, job in enumerate(jobs):
                    s = n % 2
                    if job[0] == "gu":
                        _, which, dc, hf, src = job
                        P.add("pool", lambda e, s=s, dc=dc, hf=hf, src=src: e.dma_start(
                            out=cstg[:, s, :], in_=src[dc * 128:(dc + 1) * 128, hf * 1408:(hf + 1) * 1408]),
                            r=([("WB", 7)] if n < 2 else []), w=[("cstg", s)], dma=("cstg", s))
                        P.add("pool", lambda e, s=s: e.tensor_copy(out=cbf[:, s, :], in_=cstg[:, s, :]),
                              r=[("cstg", s)], w=[("cbf", s)])
                        P.add("pool", lambda e, s=s, which=which, dc=dc, hf=hf: e.dma_start(
                            out=wgu_s[hf * 11:(hf + 1) * 11, :, which, dc, :].rearrange("f p c -> p f c"),
                            in_=cbf[:, s, :].rearrange("p (f c) -> p f c", c=128)),
                            r=[("cbf", s)], dma=("cbfo", s))
                    else:
                        fc = job[1]
                        P.add("pool", lambda e, s=s, fc=fc: e.dma_start(
                            out=cstg[:, s, 0:1024], in_=w_d[fc * 128:(fc + 1) * 128, :]),
                            w=[("cstg", s)], dma=("cstg", s))
                        P.add("pool", lambda e, s=s: e.tensor_copy(out=cbf[:, s, 0:1024], in_=cstg[:, s, 0:1024]),
                              r=[("cstg", s)], w=[("cbf", s)])
                        P.add("pool", lambda e, s=s, fc=fc: e.dma_start(out=wd_s[fc], in_=cbf[:, s, 0:1024]),
                              r=[("cbf", s)], dma=("cbfo", s))

            if stage in ("full", "B"):
                P.add("act", lambda e: e.dma_start(out=biasI[:].rearrange("p h n -> p (h n)"), in_=bias_i),
                      w=["biasI"], dma="biasI")
                P.add("act", lambda e: e.dma_start(out=gob[:], in_=g_ob.partition_broadcast(128)),
                      w=["gob"], dma="gob")
                P.add("dve", lambda e: e.memset(Vb[:].rearrange("p a h c -> p (a h c)"), 1.0), w=["Vb"])
                state = dict(tcount=0, bd_n=0)

                def emit_Ra(gb):
                    qt, bt = gb // 3, gb % 3
                    for t in range(4):
                        et = 8 * qt + 4 * bt + t
                        tc = state["tcount"]
                        state["tcount"] += 1
                        xsl = tc % 2
                        P.add("sp", lambda e, et=et, xsl=xsl: e.dma_start(out=xt[:, xsl, :],
                                                                         in_=x_ext[et * 128:(et + 1) * 128, :]),
                              w=[("xt", xsl)], dma=("xt", xsl))
                        P.add("act", lambda e, t=t, xsl=xsl: e.activation(out=xs[:, t, :], in_=xt[:, xsl, :], func=AF.Square,
                                                                       accum_out=ss[:, t, 0:1]),
                              r=[("xt", xsl)], w=[("xs", t), ("ss", t)])
                        P.add("act", lambda e, t=t: e.activation(out=ss[:, t, 1:2], in_=ss[:, t, 0:1], func=AF.Sqrt,
                                                                scale=1.0 / 1024, bias=epsA[:]),
                              r=[("ss", t), "epsA"], w=[("ss", t)])
                        P.add("dve", lambda e, t=t: e.reciprocal(out=ss[:, t, 2:3], in_=ss[:, t, 1:2]),
                              r=[("ss", t)], w=[("ss", t)])
                        P.add("dve", lambda e, t=t, xsl=xsl: e.tensor_scalar(
                            out=xs[:, t, :], in0=xt[:, xsl, :], scalar1=ss[:, t, 2:3], scalar2=None, op0=ALU.mult),
                            r=[("xt", xsl), ("ss", t)], w=[("xs", t)])

                def emit_Rb(gb):
                    hb = gb % 2
                    for t in range(4):
                        pbank = t % 2
                        pb = bankb(pbank)
                        for dc in range(8):
                            P.add("pe", lambda e, dc=dc, t=t, pb=pb: e.transpose(
                                pb[:, dc * 128:(dc + 1) * 128], xs[:, t, dc * 128:(dc + 1) * 128], identb[:]),
                                r=[("xs", t), "identb"], w=[("ps", pbank)])
                        P.add("dve", lambda e, t=t, pb=pb, pbank=pbank: e.tensor_tensor(
                            out=hT[:, hb, :, t * 128:(t + 1) * 128],
                            in0=pb[:, 0:1024].rearrange("p (c t) -> p c t", t=128),
                            in1=gmix[:, :].unsqueeze(2).to_broadcast([128, 8, 128]), op=ALU.mult),
                            r=[("ps", pbank), "gmix"], w=[("hT", hb, t)])

                def emit_M(gb):
                    qt, bt = gb // 3, gb % 3
                    hb = gb % 2
                    hTk = [("hT", hb, t) for t in range(4)]
                    for fc in range(4):
                        pbk = 2 + (fc % 2)
                        for dc in range(8):
                            P.add("pe", lambda e, fc=fc, dc=dc, pbk=pbk: e.matmul(
                                bank(pbk), lhsT=WB[:, dc, 512 + fc * 128:512 + (fc + 1) * 128],
                                rhs=hT[:, hb, dc, :], start=(dc == 0), stop=(dc == 7)),
                                r=hTk + WBk, w=[("ps", pbk)])
                        P.add("act", lambda e, fc=fc, pbk=pbk, bt=bt: e.activation(
                            out=KbT[:, fc, bt * 512:(bt + 1) * 512], in_=bank(pbk), func=AF.Copy),
                            r=[("ps", pbk)], w=[("KbT", bt)])
                    lo, hi = {0: (256, 512), 1: (0, 512), 2: (0, 256)}[bt]
                    qoff = {0: 0, 1: 256, 2: 768}[bt]
                    n = hi - lo
                    for fc in range(4):
                        pbk = 4 + (fc % 2)
                        for dc in range(8):
                            P.add("pe", lambda e, fc=fc, dc=dc, pbk=pbk, lo=lo, hi=hi, n=n: e.matmul(
                                bank(pbk)[:, 0:n], lhsT=WB[:, dc, fc * 128:(fc + 1) * 128],
                                rhs=hT[:, hb, dc, lo:hi], start=(dc == 0), stop=(dc == 7)),
                                r=hTk + WBk, w=[("ps", pbk)])
                        P.add("act", lambda e, fc=fc, pbk=pbk, n=n, qoff=qoff: e.activation(
                            out=QbT[:, fc, qoff:qoff + n], in_=bank(pbk)[:, 0:n], func=AF.Copy),
                            r=[("ps", pbk)], w=[("QbT", bt)])
                    for t in range(4):
                        pbk = 6 + (t % 2)
                        ch = 4 * bt + t
                        for dc in range(8):
                            P.add("pe", lambda e, t=t, dc=dc, pbk=pbk: e.matmul(
                                bank(pbk), lhsT=hT[:, hb, dc, t * 128:(t + 1) * 128],
                                rhs=WB[:, dc, 1024:1536], start=(dc == 0), stop=(dc == 7)),
                                r=hTk + WBk, w=[("ps", pbk)])
                        P.add("dve", lambda e, pbk=pbk, ch=ch: e.tensor_copy(
                            out=Vb[:, ch, :, 0:64], in_=bank(pbk).rearrange("p (h d) -> p h d", d=64)),
                            r=[("ps", pbk)], w=["Vb"])

                Kk = [("KbT", b) for b in range(3)]
                Qk = [("QbT", b) for b in range(3)]

                def emit_attention(qt):
                    steps = [(jl, h) for jl in range(8) for h in range(8)]
                    info = {}

                    def blockinfo(jl):
                        j = 8 * qt + jl
                        if j in (0, 1):
                            return j, list(range(0, 6)), True
                        if j in (30, 31):
                            return j, list(range(6, 12)), True
                        return j, list(range(jl, jl + 5)), False

                    def emit_st(n):
                        jl, h = steps[n]
                        j, chunks, border = blockinfo(jl)
                        W = len(chunks) * 128
                        fc = h // 2
                        pb0 = (h % 2) * 64
                        sbk = 2 + 2 * (n % 2)
                        if border:
                            bslot = state["bd_n"] % 2
                            state["bd_n"] += 1
                            bidx = {0: 0, 1: 1, 30: 2, 31: 3}[j]
                            P.add("sp", lambda e, bslot=bslot, bidx=bidx, h=h: e.dma_start(
                                out=biasD[:, bslot, :], in_=bias_bd[bidx, h]),
                                w=[("biasD", bslot)], dma=("biasD", bslot))
                            info[n] = (biasD[:, bslot, 0:W], ("biasD", bslot))
                        else:
                            info[n] = (biasI[:, h, 0:W], "biasI")
                        for ci, cl in enumerate(chunks):
                            P.add("pe", lambda e, ci=ci, cl=cl, fc=fc, pb0=pb0, sbk=sbk, jl=jl: e.matmul(
                                ps[:, sbk * 512 + ci * 128: sbk * 512 + (ci + 1) * 128],
                                lhsT=KbT[pb0:pb0 + 64, fc, cl * 128:(cl + 1) * 128],
                                rhs=QbT[pb0:pb0 + 64, fc, jl * 128:(jl + 1) * 128], start=True, stop=True),
                                r=Kk + Qk, w=[("ps", sbk), ("ps", sbk + 1)])

                    def group_epilogue(grp, h):
                        osl = grp % 2
                        obk = 0
                        for q4 in range(4):
                            P.add("pe", lambda e, q4=q4, obk=obk, osl=osl: e.transpose(
                                ps[:, obk * 512 + q4 * 65: obk * 512 + (q4 + 1) * 65],
                                OTs[0:65, osl, q4 * 128:(q4 + 1) * 128], identf[0:65, 0:65]),
                                r=[("OTs", osl), "identf"], w=[("ps", obk)])
                        g0 = 0 if h == 3 else 4
                        src3 = ps[:, obk * 512: obk * 512 + 260].rearrange("p (h c) -> p h c", c=65)
                        P.add("dve", lambda e, src3=src3, g0=g0: e.reciprocal(
                            out=st2[:, g0:g0 + 4].unsqueeze(2), in_=src3[:, :, 64:65]),
                            r=[("ps", obk)], w=[("st2", g0)])
                        P.add("dve", lambda e, src3=src3, g0=g0: e.tensor_tensor(
                            out=ob[:, g0 * 64:(g0 + 4) * 64].rearrange("p (h d) -> p h d", d=64),
                            in0=src3[:, :, 0:64],
                            in1=st2[:, g0:g0 + 4].unsqueeze(2).to_broadcast([128, 4, 64]), op=ALU.mult),
                            r=[("ps", obk), ("st2", g0)], w=[("ob", g0)])

                    def block_epilogue(j):
                        P.add("act", lambda e: e.activation(out=junk[:, :], in_=ob[:, :], func=AF.Square,
                                                           accum_out=st2[:, 8:9]),
                              r=[("ob", 0), ("ob", 4)], w=["junk", ("st2", 8)])
                        P.add("act", lambda e: e.activation(out=st2[:, 9:10], in_=st2[:, 8:9], func=AF.Ln,
                                                           scale=1.0 / 512, bias=epsA[:]),
                              r=[("st2", 8), "epsA"], w=[("st2", 8)])
                        P.add("act", lambda e: e.activation(out=st2[:, 10:11], in_=st2[:, 9:10], func=AF.Exp, scale=-0.5),
                              r=[("st2", 8)], w=[("st2", 8)])
                        P.add("dve", lambda e: e.scalar_tensor_tensor(
                            out=obn[:, :], in0=ob[:, :], scalar=st2[:, 10:11], in1=gob[:, :],
                            op0=ALU.mult, op1=ALU.mult),
                            r=[("ob", 0), ("ob", 4), ("st2", 8), "gob"], w=["obn"])
                        pb = bankb(1)
                        for c in range(4):
                            P.add("pe", lambda e, c=c, pb=pb: e.transpose(
                                pb[:, c * 128:(c + 1) * 128], obn[:, c * 128:(c + 1) * 128], identb[:]),
                                r=["obn", "identb"], w=[("ps", 1)])
                        P.add("dve", lambda e, j=j, pb=pb: e.tensor_copy(
                            out=mixbT[:, :, j * 128:(j + 1) * 128],
                            in_=pb[:, 0:512].rearrange("p (c t) -> p c t", t=128)),
                            r=[("ps", 1)], w=[("mixbT", j)])

                    pending = []
                    NS = len(steps)

                    def sinfo(n):
                        jl, h = steps[n]
                        j, chunks, border = blockinfo(jl)
                        return jl, h, j, chunks, len(chunks) * 128, n % 2, 2 + 2 * (n % 2)

                    def emit_add(n):
                        jl, h, j, chunks, W, sslot, sbk = sinfo(n)
                        bsrc, bkey = info[n]
                        P.add("dve", lambda e: e.scalar_tensor_tensor(
                            out=tmpS[:, sslot, 0:W], in0=ps[:, sbk * 512: sbk * 512 + W], scalar=0.125,
                            in1=bsrc, op0=ALU.mult, op1=ALU.add),
                            r=[("ps", sbk), ("ps", sbk + 1), bkey], w=[("tmpS", sslot)])

                    def emit_exp(n):
                        jl, h, j, chunks, W, sslot, sbk = sinfo(n)
                        P.add("act", lambda e: e.activation(
                            out=PT[:, sslot, 0:W], in_=tmpS[:, sslot, 0:W], func=AF.Exp),
                            r=[("tmpS", sslot)], w=[("PT", sslot)])

                    def emit_pv(n):
                        jl, h, j, chunks, W, sslot, sbk = sinfo(n)
                        nch = len(chunks)
                        grp = n // 4
                        otb = 6 + (grp % 2)
                        hh = h % 4
                        for ci, cl in enumerate(chunks):
                            P.add("pe", lambda e, ci=ci, cl=cl: e.matmul(
                                ps[0:65, otb * 512 + hh * 128: otb * 512 + (hh + 1) * 128],
                                lhsT=Vb[:, cl, h, :], rhs=PT[:, sslot, ci * 128:(ci + 1) * 128],
                                start=(ci == 0), stop=(ci == nch - 1)),
                                r=["Vb", ("PT", sslot)], w=[("ps", otb)])
                        if hh == 3:
                            osl = grp % 2
                            P.add("act", lambda e: e.activation(
                                out=OTs[:, osl, :], in_=ps[0:65, otb * 512:(otb + 1) * 512], func=AF.Copy),
                                r=[("ps", otb)], w=[("OTs", osl)])
                            pending.append((n + 2, lambda: group_epilogue(grp, h)))
                            if h == 7:
                                pending.append((n + 3, lambda: block_epilogue(j)))
                        while pending and pending[0][0] <= n:
                            pending.pop(0)[1]()

                    for tau in range(NS + 3):
                        if 0 <= tau - 3 < NS:
                            emit_pv(tau - 3)
                        if 0 <= tau - 2 < NS:
                            emit_exp(tau - 2)
                        if 0 <= tau - 1 < NS:
                            emit_add(tau - 1)
                        if tau < NS:
                            emit_st(tau)
                    for _, fn in pending:
                        fn()

                NB = 12
                emit_Ra(0)
                emit_Rb(0)
                emit_Ra(1)
                emit_Rb(1)
                for gb in range(NB):
                    if gb + 2 < NB:
                        emit_Ra(gb + 2)
                    emit_M(gb)
                    if gb + 2 < NB:
                        emit_Rb(gb + 2)
                    if gb % 3 == 2:
                        emit_attention(gb // 3)
            P.barrier()
            if stage == "B":
                for c in range(4):
                    P.add("sp", lambda e, c=c: e.dma_start(out=dbg[:, c, :], in_=mixbT[:, c, :]), dma=("dbg", c))
                P.barrier()
            if stage == "A":
                for c in range(4):
                    P.add("sp", lambda e, c=c: e.dma_start(out=dbg[:, 4 + c, :], in_=mixbT[:, c, :]), dma=("dbg", 4 + c))
                P.barrier()
            P.flush()

        if stage == "B":
            return nc

        with ExitStack() as sQ:
            QT = sb("QT", [128, 4, 4096], BF16, sQ)
            with ExitStack() as sA:
                WA = sb("WA", [128, 8, 768], BF16, sA)
                wstg = sb("wstgA", [128, 2, 768], F32, sA)
                xt = sb("xtA", [128, 2, 1024], F32, sA)
                xs = sb("xsA", [128, 2, 1024], BF16, sA)
                ss = sb("ssA", [128, 2, 4], F32, sA)
                hT = sb("hTA", [128, 2, 8, 128], BF16, sA)
                KT = sb("KT", [128, 2, 8192], BF16, sA)
                Va = sb("Va", [128, 64, 2, 65], BF16, sA)
                rp = sb("rp", [128, 5, 128], F32, sA)
                gqk = sb("gqk", [128, 640], F32, sA)
                goa = sb("goa", [128, 512], F32, sA)
                sq = sb("sq", [128, 640], F32, sA)
                yv = sb("yv", [128, 2, 640], F32, sA)
                t1 = sb("t1", [128, 640], F32, sA)
                t2 = sb("t2", [128, 640], F32, sA)
                zf = sb("zf", [128, 2, 640], BF16, sA)
                st = sb("stA", [128, 2, 32], F32, sA)
                PTa = sb("PTa", [128, 4, 1024], BF16, sA)
                OTa = sb("OTa", [65, 2, 512], F32, sA)
                oa = sb("oa", [128, 4, 512], F32, sA)
                oan = sb("oan", [128, 2, 512], BF16, sA)
                junk = sb("junkA", [128, 512], BF16, sA)
                st3 = sb("st3", [128, 16], F32, sA)

                if stage in ("full", "A"):
                    load_weight_bf16(WA, w_a, 768, wstg, "WA", None)
                    WAk = [("WA", dc) for dc in range(8)]
                    P.add("sp", lambda e: e.dma_start(out=gqk[:], in_=g_qk.partition_broadcast(128)),
                          w=["gqk"], dma="gqk")
                    P.add("sp", lambda e: e.dma_start(out=goa[:], in_=g_oa.partition_broadcast(128)),
                          w=["goa"], dma="goa")
                    P.add("dve", lambda e: e.memset(Va[:].rearrange("p a h c -> p (a h c)"), 1.0), w=["Va"])
                    P.add("pool", lambda e: e.memset(KT[:].rearrange("p g k -> p (g k)"), 0.0), w=["KTz"])
                    NRP = 5

                    def a_stage(k, ti):
                        own = ti < 32
                        slot = ti % 2
                        rslot = ti % NRP
                        rb = 2 + 2 * slot
                        hk = [("hTA", slot)]
                        c0 = 0 if own else 512
                        nh = 10 if own else 2
                        Wc = nh * 64
                        reg = ps[:, rb * 512 + c0: rb * 512 + c0 + Wc]
                        pk = [("ps", rb), ("ps", rb + 1)] if own else [("ps", rb + 1)]
                        sl = slice(c0, c0 + Wc)
                        if k == 0:
                            src = x_own[ti * 128:(ti + 1) * 128, :] if own else x_oth[(ti - 32) * 128:(ti - 31) * 128, :]
                            P.add("sp", lambda e: e.dma_start(out=rp[:, rslot, :], in_=rope[ti]),
                                  w=[("rp", rslot)], dma=("rp", rslot))
                            P.add("sp", lambda e: e.dma_start(out=xt[:, slot, :], in_=src),
                                  w=[("xt", slot)], dma=("xt", slot))
                            P.add("act", lambda e: e.activation(out=xs[:, slot, :], in_=xt[:, slot, :], func=AF.Square,
                                                               accum_out=ss[:, slot, 0:1]),
                                  r=[("xt", slot)], w=[("xs", slot), ("ss", slot)])
                            P.add("act", lambda e: e.activation(out=ss[:, slot, 1:2], in_=ss[:, slot, 0:1], func=AF.Sqrt,
                                                               scale=1.0 / 1024, bias=epsA[:]),
                                  r=[("ss", slot), "epsA"], w=[("ss", slot)])
                            P.add("dve", lambda e: e.reciprocal(out=ss[:, slot, 2:3], in_=ss[:, slot, 1:2]),
                                  r=[("ss", slot)], w=[("ss", slot)])
                            P.add("dve", lambda e: e.tensor_scalar(out=xs[:, slot, :], in0=xt[:, slot, :],
                                                                  scalar1=ss[:, slot, 2:3], scalar2=None, op0=ALU.mult),
                                  r=[("xt", slot), ("ss", slot)], w=[("xs", slot)])
                        elif k == 1:
                            pb = bankb(slot)
                            for dc in range(8):
                                P.add("pe", lambda e, dc=dc: e.transpose(pb[:, dc * 128:(dc + 1) * 128],
                                                                         xs[:, slot, dc * 128:(dc + 1) * 128], identb[:]),
                                      r=[("xs", slot), "identb"], w=[("ps", slot)])
                            P.add("dve", lambda e: e.tensor_tensor(
                                out=hT[:, slot, :, :], in0=pb[:, 0:1024].rearrange("p (c t) -> p c t", t=128),
                                in1=gmix[:, :].unsqueeze(2).to_broadcast([128, 8, 128]), op=ALU.mult),
                                r=[("ps", slot), "gmix"], w=[("hTA", slot)])
                        elif k == 2:
                            if own:
                                for dc in range(8):
                                    P.add("pe", lambda e, dc=dc: e.matmul(
                                        bank(rb), lhsT=hT[:, slot, dc, :], rhs=WA[:, dc, 0:512],
                                        start=(dc == 0), stop=(dc == 7)), r=hk + WAk, w=[("ps", rb)])
                            for dc in range(8):
                                P.add("pe", lambda e, dc=dc: e.matmul(
                                    bank(rb + 1)[:, 0:256], lhsT=hT[:, slot, dc, :], rhs=WA[:, dc, 512:768],
                                    start=(dc == 0), stop=(dc == 7)), r=hk + WAk, w=[("ps", rb + 1)])
                        elif k == 3:
                            g_ap = gqk[:, c0:c0 + Wc]
                            P.add("act", lambda e: e.activation(out=sq[:, sl], in_=reg, func=AF.Square),
                                  r=pk, w=["sq"])
                            P.add("dve", lambda e: e.tensor_tensor(
                                out=yv[:, slot, sl], in0=reg, in1=g_ap, op=ALU.mult), r=pk + ["gqk"], w=[("yv", slot)])
                            P.add("dve", lambda e: e.tensor_copy(
                                out=Va[:, ti, :, 0:64],
                                in_=ps[:, (rb + 1) * 512 + 128:(rb + 1) * 512 + 256].rearrange("p (h d) -> p h d", d=64)),
                                r=[("ps", rb + 1)], w=[("Va", ti)])
                            P.add("dve", lambda e: e.tensor_reduce(
                                out=st[:, slot, 0:nh], in_=sq[:, sl].rearrange("p (h d) -> p h d", d=64),
                                axis=AX.X, op=ALU.add), r=["sq"], w=[("stA", slot)])
                            if own:
                                P.add("act", lambda e: e.activation(
                                    out=st[:, slot, 10:18], in_=st[:, slot, 0:8], func=AF.Sqrt, scale=1.0, bias=epsB[:]),
                                    r=[("stA", slot), "epsB"], w=[("stA", slot)])
                                P.add("act", lambda e: e.activation(
                                    out=st[:, slot, 18:20], in_=st[:, slot, 8:10], func=AF.Sqrt, scale=1.0 / 64,
                                    bias=epsA[:]), r=[("stA", slot), "epsA"], w=[("stA", slot)])
                                P.add("dve", lambda e: e.reciprocal(out=st[:, slot, 20:30], in_=st[:, slot, 10:20]),
                                      r=[("stA", slot)], w=[("stA", slot)])
                            else:
                                P.add("act", lambda e: e.activation(
                                    out=st[:, slot, 10:12], in_=st[:, slot, 0:2], func=AF.Sqrt, scale=1.0 / 64,
                                    bias=epsA[:]), r=[("stA", slot), "epsA"], w=[("stA", slot)])
                                P.add("dve", lambda e: e.reciprocal(out=st[:, slot, 20:22], in_=st[:, slot, 10:12]),
                                      r=[("stA", slot)], w=[("stA", slot)])
                        elif k == 4:
                            y5 = yv[:, slot, sl].rearrange("p (h a t s) -> p h a t s", a=2, t=2, s=16)
                            t25 = t2[:, sl].rearrange("p (h a t s) -> p h a t s", a=2, t=2, s=16)
                            C3 = rp[:, rslot, 0:64].unsqueeze(1).to_broadcast([128, nh, 64])
                            S4 = rp[:, rslot, 64:128].rearrange("p (a t s) -> p a t s", a=2, t=2)
                            P.add("dve", lambda e: e.tensor_tensor(
                                out=t1[:, sl].rearrange("p (h d) -> p h d", d=64),
                                in0=yv[:, slot, sl].rearrange("p (h d) -> p h d", d=64), in1=C3, op=ALU.mult),
                                r=[("yv", slot), ("rp", rslot)], w=["t1"])
                            for tt in range(2):
                                P.add("pool", lambda e, tt=tt: e.tensor_tensor(
                                    out=t25[:, :, :, tt, :], in0=y5[:, :, :, 1 - tt, :],
                                    in1=S4[:, :, tt, :].unsqueeze(1).to_broadcast([128, nh, 2, 16]), op=ALU.mult),
                                    r=[("yv", slot), ("rp", rslot)], w=[("t2", tt)])
                            P.add("dve", lambda e: e.tensor_tensor(out=t1[:, sl], in0=t1[:, sl], in1=t2[:, sl],
                                                                  op=ALU.add),
                                  r=["t1", ("t2", 0), ("t2", 1)], w=["t1"])
                            P.add("dve", lambda e: e.tensor_tensor(
                                out=zf[:, slot, sl].rearrange("p (h d) -> p h d", d=64),
                                in0=t1[:, sl].rearrange("p (h d) -> p h d", d=64),
                                in1=st[:, slot, 20:20 + nh].unsqueeze(2).to_broadcast([128, nh, 64]), op=ALU.mult),
                                r=["t1", ("stA", slot)], w=[("zf", slot)])
                        elif k == 5:
                            tb = 6 + slot
                            pb = bankb(tb)
                            cs = range(0, 5) if own else range(4, 5)
                            for c in cs:
                                P.add("pe", lambda e, c=c: e.transpose(
                                    pb[:, c * 128:(c + 1) * 128], zf[:, slot, c * 128:(c + 1) * 128], identb[:]),
                                    r=[("zf", slot), "identb"], w=[("ps", tb)])
                            if own:
                                P.add("act", lambda e: e.activation(
                                    out=QT[:, :, ti * 128:(ti + 1) * 128],
                                    in_=pb[:, 0:512].rearrange("p (c t) -> p c t", t=128), func=AF.Copy),
                                    r=[("ps", tb)], w=[("QT", ti // 4)])
                            for g in range(2):
                                P.add("act", lambda e, g=g: e.activation(
                                    out=KT[g * 64:(g + 1) * 64, g, ti * 128:(ti + 1) * 128],
                                    in_=pb[g * 64:(g + 1) * 64, 512:640], func=AF.Copy),
                                    r=[("ps", tb), "KTz"], w=[("KT", ti, g)])

                    NST = 6
                    for tau in range(64 + NST - 1):
                        for k in reversed(range(NST)):
                            ti = tau - k
                            if 0 <= ti < 64:
                                a_stage(k, ti)

                    KTk = [("KT", ti, g) for ti in range(64) for g in range(2)]
                    Vak = [("Va", ti) for ti in range(64)] + ["Va"]
                    steps = [(qb, i, g, kc2) for qb in range(8) for i in range(4) for g in range(2)
                             for kc2 in range(32)]
                    NS = len(steps)

                    def emit_st(n):
                        qb, i, g, kc2 = steps[n]
                        p0 = g * 64
                        sb2 = 2 * (n % 3)
                        for u in range(2):
                            kc = 2 * kc2 + u
                            P.add("pe", lambda e, u=u, kc=kc, sb2=sb2, g=g, i=i, qb=qb: e.matmul(
                                bank(sb2 + u), lhsT=KT[:, g, kc * 128:(kc + 1) * 128],
                                rhs=QT[:, i, qb * 512:(qb + 1) * 512], start=True, stop=True),
                                r=KTk + [("QT", qb)], w=[("ps", sb2 + u)])

                    def head_epilogue(hidx, qb, i, g):
                        osl = hidx % 2
                        fo = i * 128 + g * 64
                        for hp in range(2):
                            for q2 in range(2):
                                q4 = 2 * hp + q2
                                P.add("pe", lambda e, q4=q4, q2=q2, osl=osl: e.transpose(
                                    ps[:, 7 * 512 + q2 * 65: 7 * 512 + (q2 + 1) * 65],
                                    OTa[0:65, osl, q4 * 128:(q4 + 1) * 128], identf[0:65, 0:65]),
                                    r=[("OTa", osl), "identf"], w=[("ps", 7)])
                            src3 = ps[:, 7 * 512: 7 * 512 + 130].rearrange("p (t c) -> p t c", c=65)
                            P.add("dve", lambda e, src3=src3, hp=hp: e.reciprocal(
                                out=st3[:, 2 * hp:2 * hp + 2].unsqueeze(2), in_=src3[:, :, 64:65]),
                                r=[("ps", 7)], w=[("st3", hp)])
                            P.add("dve", lambda e, src3=src3, fo=fo, hp=hp: e.tensor_tensor(
                                out=oa[:, 2 * hp:2 * hp + 2, fo:fo + 64], in0=src3[:, :, 0:64],
                                in1=st3[:, 2 * hp:2 * hp + 2].unsqueeze(2).to_broadcast([128, 2, 64]), op=ALU.mult),
                                r=[("ps", 7), ("st3", hp)], w=["oa"])

                    def qb_epilogue(qb):
                        for t in range(4):
                            P.add("act", lambda e, t=t: e.activation(out=junk[:, :], in_=oa[:, t, :], func=AF.Square,
                                                                    accum_out=st3[:, 8 + t:9 + t]),
                                  r=["oa"], w=["junkA", ("st3b", t)])
                        for t in range(4):
                            P.add("act", lambda e, t=t: e.activation(out=st3[:, 8 + t:9 + t], in_=st3[:, 8 + t:9 + t],
                                                                    func=AF.Sqrt, scale=1.0 / 512, bias=epsA[:]),
                                  r=[("st3b", t), "epsA"], w=[("st3b", t)])
                        for t in range(4):
                            P.add("dve", lambda e, t=t: e.reciprocal(out=st3[:, 12 + t:13 + t], in_=st3[:, 8 + t:9 + t]),
                                  r=[("st3b", t)], w=[("st3c", t)])
                            P.add("dve", lambda e, t=t: e.scalar_tensor_tensor(
                                out=oan[:, t % 2, :], in0=oa[:, t, :], scalar=st3[:, 12 + t:13 + t], in1=goa[:, :],
                                op0=ALU.mult, op1=ALU.mult), r=["oa", ("st3c", t), "goa"], w=[("oan", t % 2)])
                            pb = bankb(7)
                            for c in range(4):
                                P.add("pe", lambda e, c=c, pb=pb, t=t: e.transpose(
                                    pb[:, c * 128:(c + 1) * 128], oan[:, t % 2, c * 128:(c + 1) * 128], identb[:]),
                                    r=[("oan", t % 2), "identb"], w=[("ps", 7)])
                            tok0 = qb * 512 + t * 128
                            P.add("dve", lambda e, pb=pb, tok0=tok0: e.tensor_copy(
                                out=QT[:, :, tok0:tok0 + 128],
                                in_=pb[:, 0:512].rearrange("p (c t) -> p c t", t=128)),
                                r=[("ps", 7)], w=[("QT", qb)])

                    pending = []
                    emit_st(0)
                    emit_st(1)
                    for n in range(NS):
                        qb, i, g, kc2 = steps[n]
                        hidx = n // 32
                        ob_ = 6
                        osl = hidx % 2
                        pslot = n % 4
                        sb2 = 2 * (n % 3)
                        if n + 2 < NS:
                            emit_st(n + 2)
                        P.add("act", lambda e, sb2=sb2, pslot=pslot: e.activation(
                            out=PTa[:, pslot, :], in_=bank(sb2, 2), func=AF.Exp),
                            r=[("ps", sb2), ("ps", sb2 + 1)], w=[("PTa", pslot)])
                        for u in range(2):
                            kc = 2 * kc2 + u
                            P.add("pe", lambda e, u=u, kc=kc, pslot=pslot, g=g, ob_=ob_: e.matmul(
                                ps[0:65, ob_ * 512:(ob_ + 1) * 512], lhsT=Va[:, kc, g, :],
                                rhs=PTa[:, pslot, u * 512:(u + 1) * 512],
                                start=(kc == 0), stop=(kc == 63)),
                                r=Vak + [("PTa", pslot)], w=[("ps", ob_)])
                        if kc2 == 31:
                            P.add("dve", lambda e, ob_=ob_, osl=osl: e.tensor_copy(
                                out=OTa[:, osl, :], in_=ps[0:65, ob_ * 512:(ob_ + 1) * 512]),
                                r=[("ps", ob_)], w=[("OTa", osl)])
                            pending.append((n + 3, lambda hidx=hidx, qb=qb, i=i, g=g: head_epilogue(hidx, qb, i, g)))
                            if i == 3 and g == 1:
                                pending.append((n + 6, lambda qb=qb: qb_epilogue(qb)))
                        while pending and pending[0][0] <= n:
                            pending.pop(0)[1]()
                    for _, fn in pending:
                        fn()
                P.barrier()
                if stage == "A":
                    for c in range(4):
                        P.add("sp", lambda e, c=c: e.dma_start(out=dbg[:, c, :], in_=QT[:, c, :]), dma=("dbg", c))
                    P.barrier()
                P.flush()

            if stage == "A":
                return nc

            with ExitStack() as sF:
                Wo = sb("Wo", [128, 8, 1024], BF16, sF)
                wstg = sb("wstgF", [128, 2, 1024], F32, sF)
                x1 = sb("x1", [128, 2, 4, 1024], F32, sF)
                xsF = sb("xsF", [128, 2, 1024], BF16, sF)
                ssF = sb("ssF", [128, 2, 4, 8], F32, sF)
                h2T = sb("h2T", [128, 2, 8, 512], BF16, sF)
                actT = sb("actT", [128, 2, 4, 512], BF16, sF)
                wgu = sb("wgu", [128, 6, 2, 8, 128], BF16, sF)
                wdr = sb("wdr", [128, 6, 1024], BF16, sF)
                sg = sb("sg", [128, 2, 512], F32, sF)
                gfin = sb("gfin", [128, 1024], F32, sF)
                ost = sb("ost", [128, 2, 1024], F32, sF)
                junkF = sb("junkF", [128, 1024], BF16, sF)

                load_weight_bf16(Wo, w_out, 1024, wstg, "Wo", None)
                Wok = [("Wo", dc) for dc in range(8)]
                P.add("sp", lambda e: e.dma_start(out=gfin[:], in_=g_fin.partition_broadcast(128)),
                      w=["gfin"], dma="gfin")
                cnt = dict(w=0, x=0, o=0, g=0)
                groups = [list(range(s0, min(s0 + 4, NFC))) for s0 in range(0, NFC, 4)]

                def f_wout(tb, t):
                    xb = tb % 2
                    yb = 2 * (t % 2)
                    tok0 = tb * 512 + t * 128
                    P.add("sp", lambda e: e.dma_start(out=x1[:, xb, t, :], in_=x_own[tok0:tok0 + 128, :]),
                          w=[("x1", xb, t)], dma=("x1", xb, t))
                    for hf in range(2):
                        for c in range(8):
                            src = QT[:, c, tok0:tok0 + 128] if c < 4 else mixbT[:, c - 4, tok0:tok0 + 128]
                            P.add("pe", lambda e, hf=hf, c=c, src=src: e.matmul(
                                bank(yb + hf), lhsT=src, rhs=Wo[:, c, hf * 512:(hf + 1) * 512],
                                start=(c == 0), stop=(c == 7)), r=Wok, w=[("ps", yb + hf)])
                    P.add("dve", lambda e: e.tensor_tensor(out=x1[:, xb, t, :], in0=bank(yb, 2),
                                                          in1=x1[:, xb, t, :], op=ALU.add),
                          r=[("ps", yb), ("ps", yb + 1), ("x1", xb, t)], w=[("x1", xb, t)])
                    P.add("act", lambda e: e.activation(out=junkF[:, :], in_=x1[:, xb, t, :], func=AF.Square,
                                                       accum_out=ssF[:, xb, t, 0:1]),
                          r=[("x1", xb, t)], w=["junkF", ("ssF", xb, t)])
                    P.add("act", lambda e: e.activation(out=ssF[:, xb, t, 1:2], in_=ssF[:, xb, t, 0:1],
                                                       func=AF.Sqrt, scale=1.0 / 1024, bias=epsA[:]),
                          r=[("ssF", xb, t), "epsA"], w=[("ssF", xb, t)])
                    P.add("dve", lambda e: e.reciprocal(out=ssF[:, xb, t, 2:3], in_=ssF[:, xb, t, 1:2]),
                          r=[("ssF", xb, t)], w=[("ssF", xb, t)])
                    xsl = t % 2
                    P.add("dve", lambda e: e.tensor_scalar(
                        out=xsF[:, xsl, :], in0=x1[:, xb, t, :], scalar1=ssF[:, xb, t, 2:3], scalar2=None,
                        op0=ALU.mult),
                        r=[("x1", xb, t), ("ssF", xb, t)], w=[("xsF", xsl)])

                def f_tr(tb, t):
                    xb = tb % 2
                    xsl = t % 2
                    pbk = 2 * (t % 2) + 1
                    pb = bankb(pbk)
                    for dc in range(8):
                        P.add("pe", lambda e, dc=dc: e.transpose(
                            pb[:, dc * 128:(dc + 1) * 128], xsF[:, xsl, dc * 128:(dc + 1) * 128], identb[:]),
                            r=[("xsF", xsl), "identb"], w=[("ps", pbk)])
                    P.add("dve", lambda e: e.tensor_tensor(
                        out=h2T[:, xb, :, t * 128:(t + 1) * 128],
                        in0=pb[:, 0:1024].rearrange("p (c t) -> p c t", t=128),
                        in1=gffn[:, :].unsqueeze(2).to_broadcast([128, 8, 128]), op=ALU.mult),
                        r=[("ps", pbk), "gffn"], w=[("h2T", xb, t)])

                def f_prologue(tb):
                    for t in range(4):
                        f_wout(tb, t)
                        if t >= 1:
                            f_tr(tb, t - 1)
                    f_tr(tb, 3)

                def f_group(tb, grp):
                    xb = tb % 2
                    h2k = [("h2T", xb, t) for t in range(4)]
                    asl = cnt["g"] % 2
                    cnt["g"] += 1
                    wsl = {}
                    for fc in grp:
                        s_ = cnt["w"] % 6
                        cnt["w"] += 1
                        wsl[fc] = s_
                        P.add("sp", lambda e, s_=s_, fc=fc: e.dma_start(
                            out=wgu[:, s_].rearrange("p a c f -> p (a c f)"),
                            in_=wgu_s[fc].rearrange("p a c f -> p (a c f)")),
                            w=[("wgu", s_)], dma=("wgu", s_))
                        P.add("sp", lambda e, s_=s_, fc=fc: e.dma_start(out=wdr[:, s_, :], in_=wd_s[fc]),
                              w=[("wdr", s_)], dma=("wdr", s_))
                    for k, fc in enumerate(grp):
                        s_ = wsl[fc]
                        gb = 4 + 2 * (k % 2)
                        for a_ in range(2):
                            for dc in range(8):
                                P.add("pe", lambda e, a_=a_, dc=dc, s_=s_, gb=gb: e.matmul(
                                    bank(gb + a_), lhsT=wgu[:, s_, a_, dc, :], rhs=h2T[:, xb, dc, :],
                                    start=(dc == 0), stop=(dc == 7)),
                                    r=h2k + [("wgu", s_)], w=[("ps", gb + a_)])
                        P.add("act", lambda e, gb=gb, k=k: e.activation(out=sg[:, k % 2, :], in_=bank(gb), func=AF.Silu),
                              r=[("ps", gb)], w=[("sg", k % 2)])
                        P.add("dve", lambda e, gb=gb, k=k, asl=asl: e.tensor_tensor(
                            out=actT[:, asl, k, :], in0=bank(gb + 1), in1=sg[:, k % 2, :], op=ALU.mult),
                            r=[("ps", gb + 1), ("sg", k % 2)], w=[("actT", asl, k)])
                    ak = [("actT", asl, k) for k in range(len(grp))]
                    for t in range(4):
                        for hf in range(2):
                            db = (2 * t + hf) % 4
                            for k, fc in enumerate(grp):
                                s_ = wsl[fc]
                                P.add("pe", lambda e, t=t, hf=hf, k=k, s_=s_, db=db, asl=asl, n=len(grp): e.matmul(
                                    bank(db), lhsT=actT[:, asl, k, t * 128:(t + 1) * 128],
                                    rhs=wdr[:, s_, hf * 512:(hf + 1) * 512], start=(k == 0), stop=(k == n - 1)),
                                    r=ak + [("wdr", s_)], w=[("ps", db)])
                            P.add("dve", lambda e, t=t, hf=hf, db=db: e.tensor_tensor(
                                out=x1[:, xb, t, hf * 512:(hf + 1) * 512], in0=bank(db),
                                in1=x1[:, xb, t, hf * 512:(hf + 1) * 512], op=ALU.add),
                                r=[("ps", db), ("x1", xb, t)], w=[("x1", xb, t)])

                def f_epilogue(tb):
                    xb = tb % 2
                    for t in range(4):
                        P.add("act", lambda e, t=t: e.activation(out=junkF[:, :], in_=x1[:, xb, t, :], func=AF.Square,
                                                                accum_out=ssF[:, xb, t, 4:5]),
                              r=[("x1", xb, t)], w=["junkF", ("ssF", xb, t)])
                    for t in range(4):
                        P.add("act", lambda e, t=t: e.activation(out=ssF[:, xb, t, 5:6], in_=ssF[:, xb, t, 4:5],
                                                                func=AF.Sqrt, scale=1.0 / 1024, bias=epsA[:]),
                              r=[("ssF", xb, t), "epsA"], w=[("ssF", xb, t)])
                    for t in range(4):
                        osl = cnt["o"] % 2
                        cnt["o"] += 1
                        tok0 = tb * 512 + t * 128
                        P.add("dve", lambda e, t=t: e.reciprocal(out=ssF[:, xb, t, 6:7], in_=ssF[:, xb, t, 5:6]),
                              r=[("ssF", xb, t)], w=[("ssF", xb, t)])
                        P.add("dve", lambda e, t=t, osl=osl: e.scalar_tensor_tensor(
                            out=ost[:, osl, :], in0=x1[:, xb, t, :], scalar=ssF[:, xb, t, 6:7], in1=gfin[:, :],
                            op0=ALU.mult, op1=ALU.mult),
                            r=[("x1", xb, t), ("ssF", xb, t), "gfin"], w=[("ost", osl)])
                        P.add("pool", lambda e, osl=osl, tok0=tok0: e.dma_start(out=out[tok0:tok0 + 128, :],
                                                                             in_=ost[:, osl, :]),
                              r=[("ost", osl)], dma=("ost", osl))

                f_prologue(0)
                for tb in range(8):
                    for gi, grp in enumerate(groups):
                        f_group(tb, grp)
                        if tb + 1 < 8:
                            if gi < 4:
                                f_wout(tb + 1, gi)
                            if 1 <= gi <= 4:
                                f_tr(tb + 1, gi - 1)
                    f_epilogue(tb)
                P.barrier()
                P.flush()
    return nc


def _rope_tables(tok):
    row = (tok // 64).astype(np.float32)
    col = (tok % 64).astype(np.float32)
    inv = (np.float32(10000.0) ** (-np.arange(0, 32, 2, dtype=np.float32) / np.float32(32))).astype(np.float32)
    ar = (row[:, None] * inv[None, :]).astype(np.float32)
    ac = (col[:, None] * inv[None, :]).astype(np.float32)
    n = tok.shape[0]
    C = np.empty((n, 2, 2, 16), np.float32)
    S = np.empty((n, 2, 2, 16), np.float32)
    for a, ang in enumerate((ar, ac)):
        c = np.cos(ang).astype(np.float32)
        s = np.sin(ang).astype(np.float32)
        C[:, a, 0] = c
        C[:, a, 1] = c
        S[:, a, 0] = -s
        S[:, a, 1] = s
    return np.concatenate([C.reshape(n, 64), S.reshape(n, 64)], axis=1)


def _bias_table(rpb, own_row0, j, w0, nch):
    p = np.arange(128)
    ci = np.arange(nch)
    e = w0 + 2 * ci[:, None] + (p[None, :] // 64)
    kr = own_row0 - 4 + e
    kc = np.broadcast_to(p[None, :] % 64, kr.shape)
    q = np.arange(128)
    r = own_row0 + 2 * j + q // 64
    c = q % 64
    rs = np.clip(r - 4, 0, 120)
    cs = np.clip(c - 8, 0, 48)
    KR = kr[:, :, None]
    KC = kc[:, :, None]
    valid = (KR >= 0) & (KR <= 127) & (KR >= rs[None, None, :]) & (KR < rs[None, None, :] + 8) \
        & (KC >= cs[None, None, :]) & (KC < cs[None, None, :] + 16)
    dr = np.clip(KR - r[None, None, :] + 7, 0, 14)
    dc = np.clip(KC - c[None, None, :] + 15, 0, 30)
    vals = rpb[:, dr, dc]
    tab = np.where(valid[None], vals, np.float32(NEG)).astype(np.float32)
    return np.ascontiguousarray(tab.transpose(0, 2, 1, 3)).reshape(8, 128, nch * 128)


def prep_inputs(inputs):
    f = lambda a: np.ascontiguousarray(np.asarray(a, dtype=np.float32))
    x = f(inputs["x"])
    w_in = f(inputs["w_in"])[0]
    qperm = np.array([(4 * g + i) * 64 + d for i in range(4) for g in range(2) for d in range(64)])
    w_qa = w_in[:, 0:512][:, qperm]
    w_ka = w_in[:, 512:640]
    w_va = w_in[:, 640:768]
    w_qb = w_in[:, 768:1280]
    w_kb = w_in[:, 1280:1792]
    w_vb = w_in[:, 1792:2304]
    w_a = f(np.concatenate([w_qa, w_ka, w_va], axis=1))
    w_b = f(np.concatenate([w_qb, w_kb, w_vb], axis=1))
    w_out = f(inputs["w_out"])[0]
    w_out_p = f(np.concatenate([w_out[0:512][qperm], w_out[512:1024]], axis=0))
    g_oa = f(inputs["out_norm_a"])[0][qperm].reshape(1, 512)
    g_ob = f(inputs["out_norm_b"])[0].reshape(1, 512)
    g_mix = f(f(inputs["norm_mix"])[0].reshape(8, 128).T)
    g_ffn = f(f(inputs["norm_ffn"])[0].reshape(8, 128).T)
    g_fin = f(inputs["norm_final"]).reshape(1, 1024)
    g_qk = f(np.concatenate([np.tile(f(inputs["q_norm_a"])[0], 8), np.tile(f(inputs["k_norm_a"])[0], 2)])).reshape(1, 640)
    rpb = f(inputs["rpb_b"])[0]
    w_g = f(inputs["w_gate"])[0]
    w_u = f(inputs["w_up"])[0]
    w_d = f(inputs["w_down"])[0]
    ident = np.eye(128, dtype=np.float32)
    bias_i = _bias_table(rpb, 0, 5, 10, 5)
    bias_i = f(bias_i.transpose(1, 0, 2).reshape(128, 8 * 640))
    in_maps = []
    for c in range(8):
        b, hf = c // 2, c % 2
        t0 = 4096 * hf
        o0 = 4096 * (1 - hf)
        row0 = 64 * hf
        x_own = x[b, t0:t0 + 4096]
        x_oth = x[b, o0:o0 + 4096]
        x_ext = np.zeros((72, 64, 1024), np.float32)
        for e in range(72):
            gr = row0 - 4 + e
            if 0 <= gr < 128:
                x_ext[e] = x[b, gr * 64:(gr + 1) * 64]
        tok = np.concatenate([np.arange(t0, t0 + 4096), np.arange(o0, o0 + 4096)])
        rope = _rope_tables(tok).reshape(64, 128, 128)
        bd = np.stack([_bias_table(rpb, row0, 0, 0, 6), _bias_table(rpb, row0, 1, 0, 6),
                       _bias_table(rpb, row0, 30, 60, 6), _bias_table(rpb, row0, 31, 60, 6)])
        in_maps.append(dict(
            x_own=f(x_own), x_oth=f(x_oth), x_ext=f(x_ext.reshape(4608, 1024)),
            w_a=w_a, w_b=w_b, w_out=w_out_p, w_g=w_g, w_u=w_u, w_d=w_d,
            g_mix=g_mix, g_ffn=g_ffn, g_qk=g_qk, g_oa=g_oa, g_ob=g_ob, g_fin=g_fin,
            rope=f(rope), bias_i=bias_i, bias_bd=f(bd), ident=ident))
    return in_maps


def kernel(**inputs):
    in_maps = prep_inputs(inputs)
    nc = build_program("full")
    res = run_bass_kernel_spmd(nc, in_maps, core_ids=list(range(8)))
    out = np.empty((4, 8192, 1024), np.float32)
    for c in range(8):
        b, hf = c // 2, c % 2
        out[b, 4096 * hf:4096 * (hf + 1)] = np.asarray(res.results[c]["out"], dtype=np.float32)
    return out
```

```python
from contextlib import ExitStack

import numpy as np
import ml_dtypes

import concourse.bass as bass
import concourse.mybir as mybir
from concourse.bass_utils import run_bass_kernel_spmd

F32 = mybir.dt.float32
BF16 = mybir.dt.bfloat16
ALU = mybir.AluOpType
AF = mybir.ActivationFunctionType
AX = mybir.AxisListType

EPS = 1e-6
NEG = -30000.0
D_FF = 2816
NFC = 22
ENGS = ("pe", "act", "dve", "pool", "sp")


class Prog:
    def __init__(self, nc, stack):
        self.nc = nc
        self.stack = stack
        self.sems = {}
        self.cnt = {}
        self.seen = {e: {} for e in ENGS}
        self._reset()

    def _reset(self):
        self.insts = []
        self.lw = {}
        self.rd = {}

    def sem(self, key):
        if key not in self.sems:
            self.sems[key] = self.stack.enter_context(self.nc.semaphore("s%d" % len(self.sems)))
            self.cnt[key] = 0
        return self.sems[key]

    def add(self, eng, fn, r=(), w=(), dma=None):
        idx = len(self.insts)
        deps = set()
        for k in r:
            if k in self.lw:
                deps.add(self.lw[k])
        for k in w:
            if k in self.lw:
                deps.add(self.lw[k])
            for d in self.rd.get(k, {}).values():
                deps.add(d)
        deps.discard(idx)
        self.insts.append(dict(eng=eng, fn=fn, deps=deps, dma=dma))
        stream = ("dma", dma) if dma is not None else eng
        for k in r:
            self.rd.setdefault(k, {})[stream] = idx
        for k in w:
            self.lw[k] = idx
            self.rd[k] = {}
        return idx

    def barrier(self):
        last = {}
        for i, ins in enumerate(self.insts):
            st = ("dma", ins["dma"]) if ins["dma"] is not None else ins["eng"]
            if ins["fn"] is not None:
                last[st] = i
        deps = set(last.values())
        for e in ENGS:
            self.insts.append(dict(eng=e, fn=None, deps=set(deps), dma=None))

    def flush(self):
        insts = self.insts
        signaled = set()
        for i, ins in enumerate(insts):
            best = {}
            for d in ins["deps"]:
                p = insts[d]
                if p["dma"] is not None:
                    st = ("dma", p["dma"])
                else:
                    st = p["eng"]
                    if st == ins["eng"] and ins["dma"] is None and st == "pe":
                        continue
                if st not in best or best[st] < d:
                    best[st] = d
            ins["rdeps"] = sorted(best.values())
            for d in ins["rdeps"]:
                signaled.add(d)
        tok = {}
        for i, ins in enumerate(insts):
            if ins["fn"] is None:
                continue
            if ins["dma"] is not None:
                key = ("dma", ins["dma"])
                self.sem(key)
                self.cnt[key] += 16
                tok[i] = (key, self.cnt[key])
            elif i in signaled:
                key = ins["eng"]
                self.sem(key)
                self.cnt[key] += 1
                tok[i] = (key, self.cnt[key])
        per = {e: [] for e in ENGS}
        for i, ins in enumerate(insts):
            per[ins["eng"]].append(i)

        def mk(en):
            def body(e):
                seen = self.seen[en]
                for i in per[en]:
                    ins = insts[i]
                    for d in ins["rdeps"]:
                        key, val = tok[d]
                        if seen.get(key, 0) < val:
                            e.wait_ge(self.sems[key], val)
                            seen[key] = val
                    if ins["fn"] is not None:
                        bi = ins["fn"](e)
                        if i in tok:
                            bi.then_inc(self.sems[tok[i][0]], 16 if ins["dma"] is not None else 1)
            return body

        with self.nc.Block() as block:
            block.tensor(mk("pe"))
            block.scalar(mk("act"))
            block.vector(mk("dve"))
            block.gpsimd(mk("pool"))
            block.sync(mk("sp"))
        self._reset()


def build_program(stage="full"):
    nc = bass.Bass("TRN2", target_bir_lowering=False)

    def din(name, shape, dt=F32):
        return nc.dram_tensor(name, list(shape), dt, kind="ExternalInput").ap()

    x_own = din("x_own", [4096, 1024])
    x_oth = din("x_oth", [4096, 1024])
    x_ext = din("x_ext", [4608, 1024])
    w_a = din("w_a", [1024, 768])
    w_b = din("w_b", [1024, 1536])
    w_out = din("w_out", [1024, 1024])
    w_g = din("w_g", [1024, D_FF])
    w_u = din("w_u", [1024, D_FF])
    w_d = din("w_d", [D_FF, 1024])
    g_mix = din("g_mix", [128, 8])
    g_ffn = din("g_ffn", [128, 8])
    g_qk = din("g_qk", [1, 640])
    g_oa = din("g_oa", [1, 512])
    g_ob = din("g_ob", [1, 512])
    g_fin = din("g_fin", [1, 1024])
    rope = din("rope", [64, 128, 128])
    bias_i = din("bias_i", [128, 8 * 640])
    bias_bd = din("bias_bd", [4, 8, 128, 768])
    ident = din("ident", [128, 128])
    out = nc.dram_tensor("out", [4096, 1024], F32, kind="ExternalOutput").ap()
    wgu_s = nc.dram_tensor("wgu_s", [NFC, 128, 2, 8, 128], BF16, kind="Internal").ap()
    wd_s = nc.dram_tensor("wd_s", [NFC, 128, 1024], BF16, kind="Internal").ap()
    dbg = None
    if stage != "full":
        dbg = nc.dram_tensor("dbg", [128, 8, 4096], BF16, kind="ExternalOutput").ap()

    with ExitStack() as top:
        P = Prog(nc, top)
        E = top.enter_context

        def sb(name, shape, dt=F32, st=None):
            return (st or top).enter_context(nc.sbuf_tensor(name, list(shape), dt))

        ps = E(nc.psum_tensor("ps", [128, 4096], F32))

        def bank(i, n=1):
            return ps[:, i * 512:(i + n) * 512]

        def bankb(i):
            return ps[:, i * 512:(i + 1) * 512].bitcast(BF16)

        identf = sb("identf", [128, 128])
        identb = sb("identb", [128, 128], BF16)
        gmix = sb("gmix", [128, 8])
        gffn = sb("gffn", [128, 8])
        epsA = sb("epsA", [128, 1])
        epsB = sb("epsB", [128, 1])
        mixbT = sb("mixbT", [128, 4, 4096], BF16)

        P.add("sp", lambda e: e.dma_start(out=identf[:], in_=ident), w=["identf"], dma="c0")
        P.add("sp", lambda e: e.dma_start(out=gmix[:], in_=g_mix), w=["gmix"], dma="c1")
        P.add("sp", lambda e: e.dma_start(out=gffn[:], in_=g_ffn), w=["gffn"], dma="c2")
        P.add("dve", lambda e: e.tensor_copy(out=identb[:], in_=identf[:]), r=["identf"], w=["identb"])
        P.add("dve", lambda e: e.memset(epsA[:], EPS), w=["epsA"])
        P.add("dve", lambda e: e.memset(epsB[:], 64.0 * EPS), w=["epsB"])

        def rms_tile(src_ap, xt, xs, ss, slot, pbank, hT_dst, hT_key, gtile, gkey, cnt):
            P.add("sp", lambda e: e.dma_start(out=xt[:, slot, :], in_=src_ap),
                  w=[("xt", slot)], dma=("xt", slot))
            P.add("act", lambda e: e.activation(out=xs[:, slot, :], in_=xt[:, slot, :], func=AF.Square,
                                               accum_out=ss[:, slot, 0:1]),
                  r=[("xt", slot)], w=[("xs", slot), ("ss", slot)])
            P.add("act", lambda e: e.activation(out=ss[:, slot, 1:2], in_=ss[:, slot, 0:1], func=AF.Sqrt,
                                               scale=1.0 / 1024, bias=epsA[:]),
                  r=[("ss", slot), "epsA"], w=[("ss", slot)])
            P.add("dve", lambda e: e.reciprocal(out=ss[:, slot, 2:3], in_=ss[:, slot, 1:2]),
                  r=[("ss", slot)], w=[("ss", slot)])
            P.add("dve", lambda e: e.tensor_scalar(out=xs[:, slot, :], in0=xt[:, slot, :],
                                                  scalar1=ss[:, slot, 2:3], scalar2=None, op0=ALU.mult),
                  r=[("xt", slot), ("ss", slot)], w=[("xs", slot)])
            pb = bankb(pbank)
            for dc in range(8):
                P.add("pe", lambda e, dc=dc: e.transpose(pb[:, dc * 128:(dc + 1) * 128],
                                                         xs[:, slot, dc * 128:(dc + 1) * 128], identb[:]),
                      r=[("xs", slot), "identb"], w=[("ps", pbank)])
            P.add("dve", lambda e: e.tensor_tensor(
                out=hT_dst, in0=pb[:, 0:1024].rearrange("p (c t) -> p c t", t=128),
                in1=gtile[:, :].unsqueeze(2).to_broadcast([128, 8, 128]), op=ALU.mult),
                r=[("ps", pbank), gkey], w=[hT_key])

        def load_weight_bf16(dst, src, ncols, wstg, keyname, gname):
            for dc in range(8):
                s = dc % 2
                P.add("sp", lambda e, dc=dc, s=s: e.dma_start(out=wstg[:, s, 0:ncols],
                                                             in_=src[dc * 128:(dc + 1) * 128, :]),
                      w=[("wstg", s)], dma=("wstg", s))
                if dc % 2 == 0:
                    P.add("dve", lambda e, dc=dc, s=s: e.tensor_copy(out=dst[:, dc, :], in_=wstg[:, s, 0:ncols]),
                          r=[("wstg", s)], w=[(keyname, dc)])
                else:
                    P.add("act", lambda e, dc=dc, s=s: e.activation(out=dst[:, dc, :], in_=wstg[:, s, 0:ncols],
                                                                   func=AF.Copy),
                          r=[("wstg", s)], w=[(keyname, dc)])

        with ExitStack() as sB:
            WB = sb("WB", [128, 8, 1536], BF16, sB)
            wstg = sb("wstg", [128, 2, 1536], F32, sB)
            cstg = sb("cstg", [128, 2, 1408], F32, sB)
            cbf = sb("cbf", [128, 2, 1408], BF16, sB)
            xt = sb("xt", [128, 2, 1024], F32, sB)
            xs = sb("xs", [128, 4, 1024], BF16, sB)
            ss = sb("ss", [128, 4, 4], F32, sB)
            hT = sb("hT", [128, 2, 8, 512], BF16, sB)
            QbT = sb("QbT", [128, 4, 1024], BF16, sB)
            KbT = sb("KbT", [128, 4, 1536], BF16, sB)
            Vb = sb("Vb", [128, 12, 8, 65], BF16, sB)
            biasI = sb("biasI", [128, 8, 640], F32, sB)
            biasD = sb("biasD", [128, 2, 768], F32, sB)
            tmpS = sb("tmpS", [128, 2, 768], F32, sB)
            PT = sb("PT", [128, 2, 768], BF16, sB)
            OTs = sb("OTs", [65, 2, 512], F32, sB)
            gob = sb("gob", [128, 512], F32, sB)
            ob = sb("ob", [128, 512], F32, sB)
            obn = sb("obn", [128, 512], BF16, sB)
            junk = sb("junkb", [128, 512], BF16, sB)
            st2 = sb("st2", [128, 16], F32, sB)

            if stage in ("full", "B"):
                load_weight_bf16(WB, w_b, 1536, wstg, "WB", None)
                WBk = [("WB", dc) for dc in range(8)]
            if stage in ("full", "F"):
                jobs = []
                for which, src in ((0, w_g), (1, w_u)):
                    for dc in range(8):
                        for hf in range(2):
                            jobs.append(("gu", which, dc, hf, src))
                for fc in range(NFC):
                    jobs.append(("d", fc))
                for n, job in enumerate(jobs):
                    s = n % 2
                    if job[0] == "gu":
                        _, which, dc, hf, src = job
                        P.add("pool", lambda e, s=s, dc=dc, hf=hf, src=src: e.dma_start(
                            out=cstg[:, s, :], in_=src[dc * 128:(dc + 1) * 128, hf * 1408:(hf + 1) * 1408]),
                            r=([("WB", 7)] if n < 2 else []), w=[("cstg", s)], dma=("cstg", s))
                        P.add("pool", lambda e, s=s: e.tensor_copy(out=cbf[:, s, :], in_=cstg[:, s, :]),
                              r=[("cstg", s)], w=[("cbf", s)])
                        P.add("pool", lambda e, s=s, which=which, dc=dc, hf=hf: e.dma_start(
                            out=wgu_s[hf * 11:(hf + 1) * 11, :, which, dc, :].rearrange("f p c -> p f c"),
                            in_=cbf[:, s, :].rearrange("p (f c) -> p f c", c=128)),
                            r=[("cbf", s)], dma=("cbfo", s))
                    else:
                        fc = job[1]
                        P.add("pool", lambda e, s=s, fc=fc: e.dma_start(
                            out=cstg[:, s, 0:1024], in_=w_d[fc * 128:(fc + 1) * 128, :]),
                            w=[("cstg", s)], dma=("cstg", s))
                        P.add("pool", lambda e, s=s: e.tensor_copy(out=cbf[:, s, 0:1024], in_=cstg[:, s, 0:1024]),
                              r=[("cstg", s)], w=[("cbf", s)])
                        P.add("pool", lambda e, s=s, fc=fc: e.dma_start(out=wd_s[fc], in_=cbf[:, s, 0:1024]),
                              r=[("cbf", s)], dma=("cbfo", s))

            if stage in ("full", "B"):
                P.add("act", lambda e: e.dma_start(out=biasI[:].rearrange("p h n -> p (h n)"), in_=bias_i),
                      w=["biasI"], dma="biasI")
                P.add("act", lambda e: e.dma_start(out=gob[:], in_=g_ob.partition_broadcast(128)),
                      w=["gob"], dma="gob")
                P.add("dve", lambda e: e.memset(Vb[:].rearrange("p a h c -> p (a h c)"), 1.0), w=["Vb"])
                state = dict(tcount=0, bd_n=0)

                def emit_Ra(gb):
                    qt, bt = gb // 3, gb % 3
                    for t in range(4):
                        et = 8 * qt + 4 * bt + t
                        tc = state["tcount"]
                        state["tcount"] += 1
                        xsl = tc % 2
                        P.add("sp", lambda e, et=et, xsl=xsl: e.dma_start(out=xt[:, xsl, :],
                                                                         in_=x_ext[et * 128:(et + 1) * 128, :]),
                              w=[("xt", xsl)], dma=("xt", xsl))
                        P.add("act", lambda e, t=t, xsl=xsl: e.activation(out=xs[:, t, :], in_=xt[:, xsl, :], func=AF.Square,
                                                                       accum_out=ss[:, t, 0:1]),
                              r=[("xt", xsl)], w=[("xs", t), ("ss", t)])
                        P.add("act", lambda e, t=t: e.activation(out=ss[:, t, 1:2], in_=ss[:, t, 0:1], func=AF.Sqrt,
                                                                scale=1.0 / 1024, bias=epsA[:]),
                              r=[("ss", t), "epsA"], w=[("ss", t)])
                        P.add("dve", lambda e, t=t: e.reciprocal(out=ss[:, t, 2:3], in_=ss[:, t, 1:2]),
                              r=[("ss", t)], w=[("ss", t)])
                        P.add("dve", lambda e, t=t, xsl=xsl: e.tensor_scalar(
                            out=xs[:, t, :], in0=xt[:, xsl, :], scalar1=ss[:, t, 2:3], scalar2=None, op0=ALU.mult),
                            r=[("xt", xsl), ("ss", t)], w=[("xs", t)])

                def emit_Rb(gb):
                    hb = gb % 2
                    for t in range(4):
                        pbank = t % 2
                        pb = bankb(pbank)
                        for dc in range(8):
                            P.add("pe", lambda e, dc=dc, t=t, pb=pb: e.transpose(
                                pb[:, dc * 128:(dc + 1) * 128], xs[:, t, dc * 128:(dc + 1) * 128], identb[:]),
                                r=[("xs", t), "identb"], w=[("ps", pbank)])
                        P.add("dve", lambda e, t=t, pb=pb, pbank=pbank: e.tensor_tensor(
                            out=hT[:, hb, :, t * 128:(t + 1) * 128],
                            in0=pb[:, 0:1024].rearrange("p (c t) -> p c t", t=128),
                            in1=gmix[:, :].unsqueeze(2).to_broadcast([128, 8, 128]), op=ALU.mult),
                            r=[("ps", pbank), "gmix"], w=[("hT", hb, t)])

                def emit_M(gb):
                    qt, bt = gb // 3, gb % 3
                    hb = gb % 2
                    hTk = [("hT", hb, t) for t in range(4)]
                    for fc in range(4):
                        pbk = 2 + (fc % 2)
                        for dc in range(8):
                            P.add("pe", lambda e, fc=fc, dc=dc, pbk=pbk: e.matmul(
                                bank(pbk), lhsT=WB[:, dc, 512 + fc * 128:512 + (fc + 1) * 128],
                                rhs=hT[:, hb, dc, :], start=(dc == 0), stop=(dc == 7)),
                                r=hTk + WBk, w=[("ps", pbk)])
                        P.add("act", lambda e, fc=fc, pbk=pbk, bt=bt: e.activation(
                            out=KbT[:, fc, bt * 512:(bt + 1) * 512], in_=bank(pbk), func=AF.Copy),
                            r=[("ps", pbk)], w=[("KbT", bt)])
                    lo, hi = {0: (256, 512), 1: (0, 512), 2: (0, 256)}[bt]
                    qoff = {0: 0, 1: 256, 2: 768}[bt]
                    n = hi - lo
                    for fc in range(4):
                        pbk = 4 + (fc % 2)
                        for dc in range(8):
                            P.add("pe", lambda e, fc=fc, dc=dc, pbk=pbk, lo=lo, hi=hi, n=n: e.matmul(
                                bank(pbk)[:, 0:n], lhsT=WB[:, dc, fc * 128:(fc + 1) * 128],
                                rhs=hT[:, hb, dc, lo:hi], start=(dc == 0), stop=(dc == 7)),
                                r=hTk + WBk, w=[("ps", pbk)])
                        P.add("act", lambda e, fc=fc, pbk=pbk, n=n, qoff=qoff: e.activation(
                            out=QbT[:, fc, qoff:qoff + n], in_=bank(pbk)[:, 0:n], func=AF.Copy),
                            r=[("ps", pbk)], w=[("QbT", bt)])
                    for t in range(4):
                        pbk = 6 + (t % 2)
                        ch = 4 * bt + t
                        for dc in range(8):
                            P.add("pe", lambda e, t=t, dc=dc, pbk=pbk: e.matmul(
                                bank(pbk), lhsT=hT[:, hb, dc, t * 128:(t + 1) * 128],
                                rhs=WB[:, dc, 1024:1536], start=(dc == 0), stop=(dc == 7)),
                                r=hTk + WBk, w=[("ps", pbk)])
                        P.add("dve", lambda e, pbk=pbk, ch=ch: e.tensor_copy(
                            out=Vb[:, ch, :, 0:64], in_=bank(pbk).rearrange("p (h d) -> p h d", d=64)),
                            r=[("ps", pbk)], w=["Vb"])

                Kk = [("KbT", b) for b in range(3)]
                Qk = [("QbT", b) for b in range(3)]

                def emit_attention(qt):
                    steps = [(jl, h) for jl in range(8) for h in range(8)]
                    info = {}

                    def blockinfo(jl):
                        j = 8 * qt + jl
                        if j in (0, 1):
                            return j, list(range(0, 6)), True
                        if j in (30, 31):
                            return j, list(range(6, 12)), True
                        return j, list(range(jl, jl + 5)), False

                    def emit_st(n):
                        jl, h = steps[n]
                        j, chunks, border = blockinfo(jl)
                        W = len(chunks) * 128
                        fc = h // 2
                        pb0 = (h % 2) * 64
                        sbk = 2 + 2 * (n % 2)
                        if border:
                            bslot = state["bd_n"] % 2
                            state["bd_n"] += 1
                            bidx = {0: 0, 1: 1, 30: 2, 31: 3}[j]
                            P.add("sp", lambda e, bslot=bslot, bidx=bidx, h=h: e.dma_start(
                                out=biasD[:, bslot, :], in_=bias_bd[bidx, h]),
                                w=[("biasD", bslot)], dma=("biasD", bslot))
                            info[n] = (biasD[:, bslot, 0:W], ("biasD", bslot))
                        else:
                            info[n] = (biasI[:, h, 0:W], "biasI")
                        for ci, cl in enumerate(chunks):
                            P.add("pe", lambda e, ci=ci, cl=cl, fc=fc, pb0=pb0, sbk=sbk, jl=jl: e.matmul(
                                ps[:, sbk * 512 + ci * 128: sbk * 512 + (ci + 1) * 128],
                                lhsT=KbT[pb0:pb0 + 64, fc, cl * 128:(cl + 1) * 128],
                                rhs=QbT[pb0:pb0 + 64, fc, jl * 128:(jl + 1) * 128], start=True, stop=True),
                                r=Kk + Qk, w=[("ps", sbk), ("ps", sbk + 1)])

                    def group_epilogue(grp, h):
                        osl = grp % 2
                        obk = 0
                        for q4 in range(4):
                            P.add("pe", lambda e, q4=q4, obk=obk, osl=osl: e.transpose(
                                ps[:, obk * 512 + q4 * 65: obk * 512 + (q4 + 1) * 65],
                                OTs[0:65, osl, q4 * 128:(q4 + 1) * 128], identf[0:65, 0:65]),
                                r=[("OTs", osl), "identf"], w=[("ps", obk)])
                        g0 = 0 if h == 3 else 4
                        src3 = ps[:, obk * 512: obk * 512 + 260].rearrange("p (h c) -> p h c", c=65)
                        P.add("dve", lambda e, src3=src3, g0=g0: e.reciprocal(
                            out=st2[:, g0:g0 + 4].unsqueeze(2), in_=src3[:, :, 64:65]),
                            r=[("ps", obk)], w=[("st2", g0)])
                        P.add("dve", lambda e, src3=src3, g0=g0: e.tensor_tensor(
                            out=ob[:, g0 * 64:(g0 + 4) * 64].rearrange("p (h d) -> p h d", d=64),
                            in0=src3[:, :, 0:64],
                            in1=st2[:, g0:g0 + 4].unsqueeze(2).to_broadcast([128, 4, 64]), op=ALU.mult),
                            r=[("ps", obk), ("st2", g0)], w=[("ob", g0)])

                    def block_epilogue(j):
                        P.add("act", lambda e: e.activation(out=junk[:, :], in_=ob[:, :], func=AF.Square,
                                                           accum_out=st2[:, 8:9]),
                              r=[("ob", 0), ("ob", 4)], w=["junk", ("st2", 8)])
                        P.add("act", lambda e: e.activation(out=st2[:, 9:10], in_=st2[:, 8:9], func=AF.Ln,
                                                           scale=1.0 / 512, bias=epsA[:]),
                              r=[("st2", 8), "epsA"], w=[("st2", 8)])
                        P.add("act", lambda e: e.activation(out=st2[:, 10:11], in_=st2[:, 9:10], func=AF.Exp, scale=-0.5),
                              r=[("st2", 8)], w=[("st2", 8)])
                        P.add("dve", lambda e: e.scalar_tensor_tensor(
                            out=obn[:, :], in0=ob[:, :], scalar=st2[:, 10:11], in1=gob[:, :],
                            op0=ALU.mult, op1=ALU.mult),
                            r=[("ob", 0), ("ob", 4), ("st2", 8), "gob"], w=["obn"])
                        pb = bankb(1)
                        for c in range(4):
                            P.add("pe", lambda e, c=c, pb=pb: e.transpose(
                                pb[:, c * 128:(c + 1) * 128], obn[:, c * 128:(c + 1) * 128], identb[:]),
                                r=["obn", "identb"], w=[("ps", 1)])
                        P.add("dve", lambda e, j=j, pb=pb: e.tensor_copy(
                            out=mixbT[:, :, j * 128:(j + 1) * 128],
                            in_=pb[:, 0:512].rearrange("p (c t) -> p c t", t=128)),
                            r=[("ps", 1)], w=[("mixbT", j)])

                    pending = []
                    NS = len(steps)

                    def sinfo(n):
                        jl, h = steps[n]
                        j, chunks, border = blockinfo(jl)
                        return jl, h, j, chunks, len(chunks) * 128, n % 2, 2 + 2 * (n % 2)

                    def emit_add(n):
                        jl, h, j, chunks, W, sslot, sbk = sinfo(n)
                        bsrc, bkey = info[n]
                        P.add("dve", lambda e: e.scalar_tensor_tensor(
                            out=tmpS[:, sslot, 0:W], in0=ps[:, sbk * 512: sbk * 512 + W], scalar=0.125,
                            in1=bsrc, op0=ALU.mult, op1=ALU.add),
                            r=[("ps", sbk), ("ps", sbk + 1), bkey], w=[("tmpS", sslot)])

                    def emit_exp(n):
                        jl, h, j, chunks, W, sslot, sbk = sinfo(n)
                        P.add("act", lambda e: e.activation(
                            out=PT[:, sslot, 0:W], in_=tmpS[:, sslot, 0:W], func=AF.Exp),
                            r=[("tmpS", sslot)], w=[("PT", sslot)])

                    def emit_pv(n):
                        jl, h, j, chunks, W, sslot, sbk = sinfo(n)
                        nch = len(chunks)
                        grp = n // 4
                        otb = 6 + (grp % 2)
                        hh = h % 4
                        for ci, cl in enumerate(chunks):
                            P.add("pe", lambda e, ci=ci, cl=cl: e.matmul(
                                ps[0:65, otb * 512 + hh * 128: otb * 512 + (hh + 1) * 128],
                                lhsT=Vb[:, cl, h, :], rhs=PT[:, sslot, ci * 128:(ci + 1) * 128],
                                start=(ci == 0), stop=(ci == nch - 1)),
                                r=["Vb", ("PT", sslot)], w=[("ps", otb)])
                        if hh == 3:
                            osl = grp % 2
                            P.add("act", lambda e: e.activation(
                                out=OTs[:, osl, :], in_=ps[0:65, otb * 512:(otb + 1) * 512], func=AF.Copy),
                                r=[("ps", otb)], w=[("OTs", osl)])
                            pending.append((n + 2, lambda: group_epilogue(grp, h)))
                            if h == 7:
                                pending.append((n + 3, lambda: block_epilogue(j)))
                        while pending and pending[0][0] <= n:
                            pending.pop(0)[1]()

                    for tau in range(NS + 3):
                        if 0 <= tau - 3 < NS:
                            emit_pv(tau - 3)
                        if 0 <= tau - 2 < NS:
                            emit_exp(tau - 2)
                        if 0 <= tau - 1 < NS:
                            emit_add(tau - 1)
                        if tau < NS:
                            emit_st(tau)
                    for _, fn in pending:
                        fn()

                NB = 12
                emit_Ra(0)
                emit_Rb(0)
                emit_Ra(1)
                emit_Rb(1)
                for gb in range(NB):
                    if gb + 2 < NB:
                        emit_Ra(gb + 2)
                    emit_M(gb)
                    if gb + 2 < NB:
                        emit_Rb(gb + 2)
                    if gb % 3 == 2:
                        emit_attention(gb // 3)
            P.barrier()
            if stage == "B":
                for c in range(4):
                    P.add("sp", lambda e, c=c: e.dma_start(out=dbg[:, c, :], in_=mixbT[:, c, :]), dma=("dbg", c))
                P.barrier()
            if stage == "A":
                for c in range(4):
                    P.add("sp", lambda e, c=c: e.dma_start(out=dbg[:, 4 + c, :], in_=mixbT[:, c, :]), dma=("dbg", 4 + c))
                P.barrier()
            P.flush()

        if stage == "B":
            return nc

        with ExitStack() as sQ:
            QT = sb("QT", [128, 4, 4096], BF16, sQ)
            with ExitStack() as sA:
                WA = sb("WA", [128, 8, 768], BF16, sA)
                wstg = sb("wstgA", [128, 2, 768], F32, sA)
                xt = sb("xtA", [128, 2, 1024], F32, sA)
                xs = sb("xsA", [128, 2, 1024], BF16, sA)
                ss = sb("ssA", [128, 2, 4], F32, sA)
                hT = sb("hTA", [128, 2, 8, 128], BF16, sA)
                KT = sb("KT", [128, 2, 8192], BF16, sA)
                Va = sb("Va", [128, 64, 2, 65], BF16, sA)
                rp = sb("rp", [128, 5, 128], F32, sA)
                gqk = sb("gqk", [128, 640], F32, sA)
                goa = sb("goa", [128, 512], F32, sA)
                sq = sb("sq", [128, 640], F32, sA)
                yv = sb("yv", [128, 2, 640], F32, sA)
                t1 = sb("t1", [128, 640], F32, sA)
                t2 = sb("t2", [128, 640], F32, sA)
                zf = sb("zf", [128, 2, 640], BF16, sA)
                st = sb("stA", [128, 2, 32], F32, sA)
                PTa = sb("PTa", [128, 4, 1024], BF16, sA)
                OTa = sb("OTa", [65, 2, 512], F32, sA)
                oa = sb("oa", [128, 4, 512], F32, sA)
                oan = sb("oan", [128, 2, 512], BF16, sA)
                junk = sb("junkA", [128, 512], BF16, sA)
                st3 = sb("st3", [128, 16], F32, sA)

                if stage in ("full", "A"):
                    load_weight_bf16(WA, w_a, 768, wstg, "WA", None)
                    WAk = [("WA", dc) for dc in range(8)]
                    P.add("sp", lambda e: e.dma_start(out=gqk[:], in_=g_qk.partition_broadcast(128)),
                          w=["gqk"], dma="gqk")
                    P.add("sp", lambda e: e.dma_start(out=goa[:], in_=g_oa.partition_broadcast(128)),
                          w=["goa"], dma="goa")
                    P.add("dve", lambda e: e.memset(Va[:].rearrange("p a h c -> p (a h c)"), 1.0), w=["Va"])
                    P.add("pool", lambda e: e.memset(KT[:].rearrange("p g k -> p (g k)"), 0.0), w=["KTz"])
                    NRP = 5

                    def a_stage(k, ti):
                        own = ti < 32
                        slot = ti % 2
                        rslot = ti % NRP
                        rb = 2 + 2 * slot
                        hk = [("hTA", slot)]
                        c0 = 0 if own else 512
                        nh = 10 if own else 2
                        Wc = nh * 64
                        reg = ps[:, rb * 512 + c0: rb * 512 + c0 + Wc]
                        pk = [("ps", rb), ("ps", rb + 1)] if own else [("ps", rb + 1)]
                        sl = slice(c0, c0 + Wc)
                        if k == 0:
                            src = x_own[ti * 128:(ti + 1) * 128, :] if own else x_oth[(ti - 32) * 128:(ti - 31) * 128, :]
                            P.add("sp", lambda e: e.dma_start(out=rp[:, rslot, :], in_=rope[ti]),
                                  w=[("rp", rslot)], dma=("rp", rslot))
                            P.add("sp", lambda e: e.dma_start(out=xt[:, slot, :], in_=src),
                                  w=[("xt", slot)], dma=("xt", slot))
                            P.add("act", lambda e: e.activation(out=xs[:, slot, :], in_=xt[:, slot, :], func=AF.Square,
                                                               accum_out=ss[:, slot, 0:1]),
                                  r=[("xt", slot)], w=[("xs", slot), ("ss", slot)])
                            P.add("act", lambda e: e.activation(out=ss[:, slot, 1:2], in_=ss[:, slot, 0:1], func=AF.Sqrt,
                                                               scale=1.0 / 1024, bias=epsA[:]),
                                  r=[("ss", slot), "epsA"], w=[("ss", slot)])
                            P.add("dve", lambda e: e.reciprocal(out=ss[:, slot, 2:3], in_=ss[:, slot, 1:2]),
                                  r=[("ss", slot)], w=[("ss", slot)])
                            P.add("dve", lambda e: e.tensor_scalar(out=xs[:, slot, :], in0=xt[:, slot, :],
                                                                  scalar1=ss[:, slot, 2:3], scalar2=None, op0=ALU.mult),
                                  r=[("xt", slot), ("ss", slot)], w=[("xs", slot)])
                        elif k == 1:
                            pb = bankb(slot)
                            for dc in range(8):
                                P.add("pe", lambda e, dc=dc: e.transpose(pb[:, dc * 128:(dc + 1) * 128],
                                                                         xs[:, slot, dc * 128:(dc + 1) * 128], identb[:]),
                                      r=[("xs", slot), "identb"], w=[("ps", slot)])
                            P.add("dve", lambda e: e.tensor_tensor(
                                out=hT[:, slot, :, :], in0=pb[:, 0:1024].rearrange("p (c t) -> p c t", t=128),
                                in1=gmix[:, :].unsqueeze(2).to_broadcast([128, 8, 128]), op=ALU.mult),
                                r=[("ps", slot), "gmix"], w=[("hTA", slot)])
                        elif k == 2:
                            if own:
                                for dc in range(8):
                                    P.add("pe", lambda e, dc=dc: e.matmul(
                                        bank(rb), lhsT=hT[:, slot, dc, :], rhs=WA[:, dc, 0:512],
                                        start=(dc == 0), stop=(dc == 7)), r=hk + WAk, w=[("ps", rb)])
                            for dc in range(8):
                                P.add("pe", lambda e, dc=dc: e.matmul(
                                    bank(rb + 1)[:, 0:256], lhsT=hT[:, slot, dc, :], rhs=WA[:, dc, 512:768],
                                    start=(dc == 0), stop=(dc == 7)), r=hk + WAk, w=[("ps", rb + 1)])
                        elif k == 3:
                            g_ap = gqk[:, c0:c0 + Wc]
                            P.add("act", lambda e: e.activation(out=sq[:, sl], in_=reg, func=AF.Square),
                                  r=pk, w=["sq"])
                            P.add("dve", lambda e: e.tensor_tensor(
                                out=yv[:, slot, sl], in0=reg, in1=g_ap, op=ALU.mult), r=pk + ["gqk"], w=[("yv", slot)])
                            P.add("dve", lambda e: e.tensor_copy(
                                out=Va[:, ti, :, 0:64],
                                in_=ps[:, (rb + 1) * 512 + 128:(rb + 1) * 512 + 256].rearrange("p (h d) -> p h d", d=64)),
                                r=[("ps", rb + 1)], w=[("Va", ti)])
                            P.add("dve", lambda e: e.tensor_reduce(
                                out=st[:, slot, 0:nh], in_=sq[:, sl].rearrange("p (h d) -> p h d", d=64),
                                axis=AX.X, op=ALU.add), r=["sq"], w=[("stA", slot)])
                            if own:
                                P.add("act", lambda e: e.activation(
                                    out=st[:, slot, 10:18], in_=st[:, slot, 0:8], func=AF.Sqrt, scale=1.0, bias=epsB[:]),
                                    r=[("stA", slot), "epsB"], w=[("stA", slot)])
                                P.add("act", lambda e: e.activation(
                                    out=st[:, slot, 18:20], in_=st[:, slot, 8:10], func=AF.Sqrt, scale=1.0 / 64,
                                    bias=epsA[:]), r=[("stA", slot), "epsA"], w=[("stA", slot)])
                                P.add("dve", lambda e: e.reciprocal(out=st[:, slot, 20:30], in_=st[:, slot, 10:20]),
                                      r=[("stA", slot)], w=[("stA", slot)])
                            else:
                                P.add("act", lambda e: e.activation(
                                    out=st[:, slot, 10:12], in_=st[:, slot, 0:2], func=AF.Sqrt, scale=1.0 / 64,
                                    bias=epsA[:]), r=[("stA", slot), "epsA"], w=[("stA", slot)])
                                P.add("dve", lambda e: e.reciprocal(out=st[:, slot, 20:22], in_=st[:, slot, 10:12]),
                                      r=[("stA", slot)], w=[("stA", slot)])
                        elif k == 4:
                            y5 = yv[:, slot, sl].rearrange("p (h a t s) -> p h a t s", a=2, t=2, s=16)
                            t25 = t2[:, sl].rearrange("p (h a t s) -> p h a t s", a=2, t=2, s=16)
                            C3 = rp[:, rslot, 0:64].unsqueeze(1).to_broadcast([128, nh, 64])
                            S4 = rp[:, rslot, 64:128].rearrange("p (a t s) -> p a t s", a=2, t=2)
                            P.add("dve", lambda e: e.tensor_tensor(
                                out=t1[:, sl].rearrange("p (h d) -> p h d", d=64),
                                in0=yv[:, slot, sl].rearrange("p (h d) -> p h d", d=64), in1=C3, op=ALU.mult),
                                r=[("yv", slot), ("rp", rslot)], w=["t1"])
                            for tt in range(2):
                                P.add("pool", lambda e, tt=tt: e.tensor_tensor(
                                    out=t25[:, :, :, tt, :], in0=y5[:, :, :, 1 - tt, :],
                                    in1=S4[:, :, tt, :].unsqueeze(1).to_broadcast([128, nh, 2, 16]), op=ALU.mult),
                                    r=[("yv", slot), ("rp", rslot)], w=[("t2", tt)])
                            P.add("dve", lambda e: e.tensor_tensor(out=t1[:, sl], in0=t1[:, sl], in1=t2[:, sl],
                                                                  op=ALU.add),
                                  r=["t1", ("t2", 0), ("t2", 1)], w=["t1"])
                            P.add("dve", lambda e: e.tensor_tensor(
                                out=zf[:, slot, sl].rearrange("p (h d) -> p h d", d=64),
                                in0=t1[:, sl].rearrange("p (h d) -> p h d", d=64),
                                in1=st[:, slot, 20:20 + nh].unsqueeze(2).to_broadcast([128, nh, 64]), op=ALU.mult),
                                r=["t1", ("stA", slot)], w=[("zf", slot)])
                        elif k == 5:
                            tb = 6 + slot
                            pb = bankb(tb)
                            cs = range(0, 5) if own else range(4, 5)
                            for c in cs:
                                P.add("pe", lambda e, c=c: e.transpose(
                                    pb[:, c * 128:(c + 1) * 128], zf[:, slot, c * 128:(c + 1) * 128], identb[:]),
                                    r=[("zf", slot), "identb"], w=[("ps", tb)])
                            if own:
                                P.add("act", lambda e: e.activation(
                                    out=QT[:, :, ti * 128:(ti + 1) * 128],
                                    in_=pb[:, 0:512].rearrange("p (c t) -> p c t", t=128), func=AF.Copy),
                                    r=[("ps", tb)], w=[("QT", ti // 4)])
                            for g in range(2):
                                P.add("act", lambda e, g=g: e.activation(
                                    out=KT[g * 64:(g + 1) * 64, g, ti * 128:(ti + 1) * 128],
                                    in_=pb[g * 64:(g + 1) * 64, 512:640], func=AF.Copy),
                                    r=[("ps", tb), "KTz"], w=[("KT", ti, g)])

                    NST = 6
                    for tau in range(64 + NST - 1):
                        for k in reversed(range(NST)):
                            ti = tau - k
                            if 0 <= ti < 64:
                                a_stage(k, ti)

                    KTk = [("KT", ti, g) for ti in range(64) for g in range(2)]
                    Vak = [("Va", ti) for ti in range(64)] + ["Va"]
                    steps = [(qb, i, g, kc2) for qb in range(8) for i in range(4) for g in range(2)
                             for kc2 in range(32)]
                    NS = len(steps)

                    def emit_st(n):
                        qb, i, g, kc2 = steps[n]
                        p0 = g * 64
                        sb2 = 2 * (n % 3)
                        for u in range(2):
                            kc = 2 * kc2 + u
                            P.add("pe", lambda e, u=u, kc=kc, sb2=sb2, g=g, i=i, qb=qb: e.matmul(
                                bank(sb2 + u), lhsT=KT[:, g, kc * 128:(kc + 1) * 128],
                                rhs=QT[:, i, qb * 512:(qb + 1) * 512], start=True, stop=True),
                                r=KTk + [("QT", qb)], w=[("ps", sb2 + u)])

                    def head_epilogue(hidx, qb, i, g):
                        osl = hidx % 2
                        fo = i * 128 + g * 64
                        for hp in range(2):
                            for q2 in range(2):
                                q4 = 2 * hp + q2
                                P.add("pe", lambda e, q4=q4, q2=q2, osl=osl: e.transpose(
                                    ps[:, 7 * 512 + q2 * 65: 7 * 512 + (q2 + 1) * 65],
                                    OTa[0:65, osl, q4 * 128:(q4 + 1) * 128], identf[0:65, 0:65]),
                                    r=[("OTa", osl), "identf"], w=[("ps", 7)])
                            src3 = ps[:, 7 * 512: 7 * 512 + 130].rearrange("p (t c) -> p t c", c=65)
                            P.add("dve", lambda e, src3=src3, hp=hp: e.reciprocal(
                                out=st3[:, 2 * hp:2 * hp + 2].unsqueeze(2), in_=src3[:, :, 64:65]),
                                r=[("ps", 7)], w=[("st3", hp)])
                            P.add("dve", lambda e, src3=src3, fo=fo, hp=hp: e.tensor_tensor(
                                out=oa[:, 2 * hp:2 * hp + 2, fo:fo + 64], in0=src3[:, :, 0:64],
                                in1=st3[:, 2 * hp:2 * hp + 2].unsqueeze(2).to_broadcast([128, 2, 64]), op=ALU.mult),
                                r=[("ps", 7), ("st3", hp)], w=["oa"])

                    def qb_epilogue(qb):
                        for t in range(4):
                            P.add("act", lambda e, t=t: e.activation(out=junk[:, :], in_=oa[:, t, :], func=AF.Square,
                                                                    accum_out=st3[:, 8 + t:9 + t]),
                                  r=["oa"], w=["junkA", ("st3b", t)])
                        for t in range(4):
                            P.add("act", lambda e, t=t: e.activation(out=st3[:, 8 + t:9 + t], in_=st3[:, 8 + t:9 + t],
                                                                    func=AF.Ln, scale=1.0 / 512, bias=epsA[:]),
                                  r=[("st3b", t), "epsA"], w=[("st3b", t)])
                        for t in range(4):
                            P.add("act", lambda e, t=t: e.activation(out=st3[:, 12 + t:13 + t], in_=st3[:, 8 + t:9 + t],
                                                                    func=AF.Exp, scale=-0.5),
                                  r=[("st3b", t)], w=[("st3c", t)])
                            P.add("dve", lambda e, t=t: e.scalar_tensor_tensor(
                                out=oan[:, t % 2, :], in0=oa[:, t, :], scalar=st3[:, 12 + t:13 + t], in1=goa[:, :],
                                op0=ALU.mult, op1=ALU.mult), r=["oa", ("st3c", t), "goa"], w=[("oan", t % 2)])
                            pb = bankb(7)
                            for c in range(4):
                                P.add("pe", lambda e, c=c, pb=pb, t=t: e.transpose(
                                    pb[:, c * 128:(c + 1) * 128], oan[:, t % 2, c * 128:(c + 1) * 128], identb[:]),
                                    r=[("oan", t % 2), "identb"], w=[("ps", 7)])
                            tok0 = qb * 512 + t * 128
                            P.add("dve", lambda e, pb=pb, tok0=tok0: e.tensor_copy(
                                out=QT[:, :, tok0:tok0 + 128],
                                in_=pb[:, 0:512].rearrange("p (c t) -> p c t", t=128)),
                                r=[("ps", 7)], w=[("QT", qb)])

                    pending = []
                    emit_st(0)
                    emit_st(1)
                    for n in range(NS):
                        qb, i, g, kc2 = steps[n]
                        hidx = n // 32
                        ob_ = 6
                        osl = hidx % 2
                        pslot = n % 4
                        sb2 = 2 * (n % 3)
                        if n + 2 < NS:
                            emit_st(n + 2)
                        P.add("act", lambda e, sb2=sb2, pslot=pslot: e.activation(
                            out=PTa[:, pslot, :], in_=bank(sb2, 2), func=AF.Exp),
                            r=[("ps", sb2), ("ps", sb2 + 1)], w=[("PTa", pslot)])
                        for u in range(2):
                            kc = 2 * kc2 + u
                            P.add("pe", lambda e, u=u, kc=kc, pslot=pslot, g=g, ob_=ob_: e.matmul(
                                ps[0:65, ob_ * 512:(ob_ + 1) * 512], lhsT=Va[:, kc, g, :],
                                rhs=PTa[:, pslot, u * 512:(u + 1) * 512],
                                start=(kc == 0), stop=(kc == 63)),
                                r=Vak + [("PTa", pslot)], w=[("ps", ob_)])
                        if kc2 == 31:
                            P.add("dve", lambda e, ob_=ob_, osl=osl: e.tensor_copy(
                                out=OTa[:, osl, :], in_=ps[0:65, ob_ * 512:(ob_ + 1) * 512]),
                                r=[("ps", ob_)], w=[("OTa", osl)])
                            pending.append((n + 3, lambda hidx=hidx, qb=qb, i=i, g=g: head_epilogue(hidx, qb, i, g)))
                            if i == 3 and g == 1:
                                pending.append((n + 6, lambda qb=qb: qb_epilogue(qb)))
                        while pending and pending[0][0] <= n:
                            pending.pop(0)[1]()
                    for _, fn in pending:
                        fn()
                P.barrier()
                if stage == "A":
                    for c in range(4):
                        P.add("sp", lambda e, c=c: e.dma_start(out=dbg[:, c, :], in_=QT[:, c, :]), dma=("dbg", c))
                    P.barrier()
                P.flush()

            if stage == "A":
                return nc

            with ExitStack() as sF:
                Wo = sb("Wo", [128, 8, 1024], BF16, sF)
                wstg = sb("wstgF", [128, 2, 1024], F32, sF)
                x1 = sb("x1", [128, 2, 4, 1024], F32, sF)
                xsF = sb("xsF", [128, 2, 1024], BF16, sF)
                ssF = sb("ssF", [128, 2, 4, 8], F32, sF)
                h2T = sb("h2T", [128, 2, 8, 512], BF16, sF)
                actT = sb("actT", [128, 2, 4, 512], BF16, sF)
                wgu = sb("wgu", [128, 6, 2, 8, 128], BF16, sF)
                wdr = sb("wdr", [128, 6, 1024], BF16, sF)
                sg = sb("sg", [128, 2, 512], F32, sF)
                gfin = sb("gfin", [128, 1024], F32, sF)
                ost = sb("ost", [128, 2, 1024], F32, sF)
                junkF = sb("junkF", [128, 1024], BF16, sF)

                load_weight_bf16(Wo, w_out, 1024, wstg, "Wo", None)
                Wok = [("Wo", dc) for dc in range(8)]
                P.add("sp", lambda e: e.dma_start(out=gfin[:], in_=g_fin.partition_broadcast(128)),
                      w=["gfin"], dma="gfin")
                cnt = dict(w=0, x=0, o=0, g=0)
                groups = [list(range(s0, min(s0 + 4, NFC))) for s0 in range(0, NFC, 4)]

                def f_wout(tb, t):
                    xb = tb % 2
                    yb = 2 * (t % 2)
                    tok0 = tb * 512 + t * 128
                    P.add("sp", lambda e: e.dma_start(out=x1[:, xb, t, :], in_=x_own[tok0:tok0 + 128, :]),
                          w=[("x1", xb, t)], dma=("x1", xb, t))
                    for hf in range(2):
                        for c in range(8):
                            src = QT[:, c, tok0:tok0 + 128] if c < 4 else mixbT[:, c - 4, tok0:tok0 + 128]
                            P.add("pe", lambda e, hf=hf, c=c, src=src: e.matmul(
                                bank(yb + hf), lhsT=src, rhs=Wo[:, c, hf * 512:(hf + 1) * 512],
                                start=(c == 0), stop=(c == 7)), r=Wok, w=[("ps", yb + hf)])
                    P.add("dve", lambda e: e.tensor_tensor(out=x1[:, xb, t, :], in0=bank(yb, 2),
                                                          in1=x1[:, xb, t, :], op=ALU.add),
                          r=[("ps", yb), ("ps", yb + 1), ("x1", xb, t)], w=[("x1", xb, t)])
                    P.add("act", lambda e: e.activation(out=junkF[:, :], in_=x1[:, xb, t, :], func=AF.Square,
                                                       accum_out=ssF[:, xb, t, 0:1]),
                          r=[("x1", xb, t)], w=["junkF", ("ssF", xb, t)])
                    P.add("act", lambda e: e.activation(out=ssF[:, xb, t, 1:2], in_=ssF[:, xb, t, 0:1],
                                                       func=AF.Sqrt, scale=1.0 / 1024, bias=epsA[:]),
                          r=[("ssF", xb, t), "epsA"], w=[("ssF", xb, t)])
                    P.add("dve", lambda e: e.reciprocal(out=ssF[:, xb, t, 2:3], in_=ssF[:, xb, t, 1:2]),
                          r=[("ssF", xb, t)], w=[("ssF", xb, t)])
                    xsl = t % 2
                    P.add("dve", lambda e: e.tensor_scalar(
                        out=xsF[:, xsl, :], in0=x1[:, xb, t, :], scalar1=ssF[:, xb, t, 2:3], scalar2=None,
                        op0=ALU.mult),
                        r=[("x1", xb, t), ("ssF", xb, t)], w=[("xsF", xsl)])

                def f_tr(tb, t):
                    xb = tb % 2
                    xsl = t % 2
                    pbk = 2 * (t % 2) + 1
                    pb = bankb(pbk)
                    for dc in range(8):
                        P.add("pe", lambda e, dc=dc: e.transpose(
                            pb[:, dc * 128:(dc + 1) * 128], xsF[:, xsl, dc * 128:(dc + 1) * 128], identb[:]),
                            r=[("xsF", xsl), "identb"], w=[("ps", pbk)])
                    P.add("dve", lambda e: e.tensor_tensor(
                        out=h2T[:, xb, :, t * 128:(t + 1) * 128],
                        in0=pb[:, 0:1024].rearrange("p (c t) -> p c t", t=128),
                        in1=gffn[:, :].unsqueeze(2).to_broadcast([128, 8, 128]), op=ALU.mult),
                        r=[("ps", pbk), "gffn"], w=[("h2T", xb, t)])

                def f_prologue(tb):
                    for t in range(4):
                        f_wout(tb, t)
                        if t >= 1:
                            f_tr(tb, t - 1)
                    f_tr(tb, 3)

                def f_group(tb, grp):
                    xb = tb % 2
                    h2k = [("h2T", xb, t) for t in range(4)]
                    asl = cnt["g"] % 2
                    cnt["g"] += 1
                    wsl = {}
                    for fc in grp:
                        s_ = cnt["w"] % 6
                        cnt["w"] += 1
                        wsl[fc] = s_
                        P.add("sp", lambda e, s_=s_, fc=fc: e.dma_start(
                            out=wgu[:, s_].rearrange("p a c f -> p (a c f)"),
                            in_=wgu_s[fc].rearrange("p a c f -> p (a c f)")),
                            w=[("wgu", s_)], dma=("wgu", s_))
                        P.add("sp", lambda e, s_=s_, fc=fc: e.dma_start(out=wdr[:, s_, :], in_=wd_s[fc]),
                              w=[("wdr", s_)], dma=("wdr", s_))
                    for k, fc in enumerate(grp):
                        s_ = wsl[fc]
                        gb = 4 + 2 * (k % 2)
                        for a_ in range(2):
                            for dc in range(8):
                                P.add("pe", lambda e, a_=a_, dc=dc, s_=s_, gb=gb: e.matmul(
                                    bank(gb + a_), lhsT=wgu[:, s_, a_, dc, :], rhs=h2T[:, xb, dc, :],
                                    start=(dc == 0), stop=(dc == 7)),
                                    r=h2k + [("wgu", s_)], w=[("ps", gb + a_)])
                        P.add("act", lambda e, gb=gb, k=k: e.activation(out=sg[:, k % 2, :], in_=bank(gb), func=AF.Silu),
                              r=[("ps", gb)], w=[("sg", k % 2)])
                        P.add("dve", lambda e, gb=gb, k=k, asl=asl: e.tensor_tensor(
                            out=actT[:, asl, k, :], in0=bank(gb + 1), in1=sg[:, k % 2, :], op=ALU.mult),
                            r=[("ps", gb + 1), ("sg", k % 2)], w=[("actT", asl, k)])
                    ak = [("actT", asl, k) for k in range(len(grp))]
                    for t in range(4):
                        for hf in range(2):
                            db = (2 * t + hf) % 4
                            for k, fc in enumerate(grp):
                                s_ = wsl[fc]
                                P.add("pe", lambda e, t=t, hf=hf, k=k, s_=s_, db=db, asl=asl, n=len(grp): e.matmul(
                                    bank(db), lhsT=actT[:, asl, k, t * 128:(t + 1) * 128],
                                    rhs=wdr[:, s_, hf * 512:(hf + 1) * 512], start=(k == 0), stop=(k == n - 1)),
                                    r=ak + [("wdr", s_)], w=[("ps", db)])
                            P.add("dve", lambda e, t=t, hf=hf, db=db: e.tensor_tensor(
                                out=x1[:, xb, t, hf * 512:(hf + 1) * 512], in0=bank(db),
                                in1=x1[:, xb, t, hf * 512:(hf + 1) * 512], op=ALU.add),
                                r=[("ps", db), ("x1", xb, t)], w=[("x1", xb, t)])

                def f_epilogue(tb):
                    xb = tb % 2
                    for t in range(4):
                        P.add("act", lambda e, t=t: e.activation(out=junkF[:, :], in_=x1[:, xb, t, :], func=AF.Square,
                                                                accum_out=ssF[:, xb, t, 4:5]),
                              r=[("x1", xb, t)], w=["junkF", ("ssF", xb, t)])
                    for t in range(4):
                        P.add("act", lambda e, t=t: e.activation(out=ssF[:, xb, t, 5:6], in_=ssF[:, xb, t, 4:5],
                                                                func=AF.Sqrt, scale=1.0 / 1024, bias=epsA[:]),
                              r=[("ssF", xb, t), "epsA"], w=[("ssF", xb, t)])
                    for t in range(4):
                        osl = cnt["o"] % 2
                        cnt["o"] += 1
                        tok0 = tb * 512 + t * 128
                        P.add("dve", lambda e, t=t: e.reciprocal(out=ssF[:, xb, t, 6:7], in_=ssF[:, xb, t, 5:6]),
                              r=[("ssF", xb, t)], w=[("ssF", xb, t)])
                        P.add("dve", lambda e, t=t, osl=osl: e.scalar_tensor_tensor(
                            out=ost[:, osl, :], in0=x1[:, xb, t, :], scalar=ssF[:, xb, t, 6:7], in1=gfin[:, :],
                            op0=ALU.mult, op1=ALU.mult),
                            r=[("x1", xb, t), ("ssF", xb, t), "gfin"], w=[("ost", osl)])
                        P.add("pool", lambda e, osl=osl, tok0=tok0: e.dma_start(out=out[tok0:tok0 + 128, :],
                                                                             in_=ost[:, osl, :]),
                              r=[("ost", osl)], dma=("ost", osl))

                f_prologue(0)
                for tb in range(8):
                    for gi, grp in enumerate(groups):
                        f_group(tb, grp)
                        if tb + 1 < 8:
                            if gi < 4:
                                f_wout(tb + 1, gi)
                            if 1 <= gi <= 4:
                                f_tr(tb + 1, gi - 1)
                    f_epilogue(tb)
                P.barrier()
                P.flush()
    return nc


def _rope_tables(tok):
    row = (tok // 64).astype(np.float32)
    col = (tok % 64).astype(np.float32)
    inv = (np.float32(10000.0) ** (-np.arange(0, 32, 2, dtype=np.float32) / np.float32(32))).astype(np.float32)
    ar = (row[:, None] * inv[None, :]).astype(np.float32)
    ac = (col[:, None] * inv[None, :]).astype(np.float32)
    n = tok.shape[0]
    C = np.empty((n, 2, 2, 16), np.float32)
    S = np.empty((n, 2, 2, 16), np.float32)
    for a, ang in enumerate((ar, ac)):
        c = np.cos(ang).astype(np.float32)
        s = np.sin(ang).astype(np.float32)
        C[:, a, 0] = c
        C[:, a, 1] = c
        S[:, a, 0] = -s
        S[:, a, 1] = s
    return np.concatenate([C.reshape(n, 64), S.reshape(n, 64)], axis=1)


def _bias_table(rpb, own_row0, j, w0, nch):
    p = np.arange(128)
    ci = np.arange(nch)
    e = w0 + 2 * ci[:, None] + (p[None, :] // 64)
    kr = own_row0 - 4 + e
    kc = np.broadcast_to(p[None, :] % 64, kr.shape)
    q = np.arange(128)
    r = own_row0 + 2 * j + q // 64
    c = q % 64
    rs = np.clip(r - 4, 0, 120)
    cs = np.clip(c - 8, 0, 48)
    KR = kr[:, :, None]
    KC = kc[:, :, None]
    valid = (KR >= 0) & (KR <= 127) & (KR >= rs[None, None, :]) & (KR < rs[None, None, :] + 8) \
        & (KC >= cs[None, None, :]) & (KC < cs[None, None, :] + 16)
    dr = np.clip(KR - r[None, None, :] + 7, 0, 14)
    dc = np.clip(KC - c[None, None, :] + 15, 0, 30)
    vals = rpb[:, dr, dc]
    tab = np.where(valid[None], vals, np.float32(NEG)).astype(np.float32)
    return np.ascontiguousarray(tab.transpose(0, 2, 1, 3)).reshape(8, 128, nch * 128)


def prep_inputs(inputs):
    f = lambda a: np.ascontiguousarray(np.asarray(a, dtype=np.float32))
    x = f(inputs["x"])
    w_in = f(inputs["w_in"])[0]
    qperm = np.array([(4 * g + i) * 64 + d for i in range(4) for g in range(2) for d in range(64)])
    w_qa = w_in[:, 0:512][:, qperm]
    w_ka = w_in[:, 512:640]
    w_va = w_in[:, 640:768]
    w_qb = w_in[:, 768:1280]
    w_kb = w_in[:, 1280:1792]
    w_vb = w_in[:, 1792:2304]
    w_a = f(np.concatenate([w_qa, w_ka, w_va], axis=1))
    w_b = f(np.concatenate([w_qb, w_kb, w_vb], axis=1))
    w_out = f(inputs["w_out"])[0]
    w_out_p = f(np.concatenate([w_out[0:512][qperm], w_out[512:1024]], axis=0))
    g_oa = f(inputs["out_norm_a"])[0][qperm].reshape(1, 512)
    g_ob = f(inputs["out_norm_b"])[0].reshape(1, 512)
    g_mix = f(f(inputs["norm_mix"])[0].reshape(8, 128).T)
    g_ffn = f(f(inputs["norm_ffn"])[0].reshape(8, 128).T)
    g_fin = f(inputs["norm_final"]).reshape(1, 1024)
    g_qk = f(np.concatenate([np.tile(f(inputs["q_norm_a"])[0], 8), np.tile(f(inputs["k_norm_a"])[0], 2)])).reshape(1, 640)
    rpb = f(inputs["rpb_b"])[0]
    w_g = f(inputs["w_gate"])[0]
    w_u = f(inputs["w_up"])[0]
    w_d = f(inputs["w_down"])[0]
    ident = np.eye(128, dtype=np.float32)
    bias_i = _bias_table(rpb, 0, 5, 10, 5)
    bias_i = f(bias_i.transpose(1, 0, 2).reshape(128, 8 * 640))
    in_maps = []
    for c in range(8):
        b, hf = c // 2, c % 2
        t0 = 4096 * hf
        o0 = 4096 * (1 - hf)
        row0 = 64 * hf
        x_own = x[b, t0:t0 + 4096]
        x_oth = x[b, o0:o0 + 4096]
        x_ext = np.zeros((72, 64, 1024), np.float32)
        for e in range(72):
            gr = row0 - 4 + e
            if 0 <= gr < 128:
                x_ext[e] = x[b, gr * 64:(gr + 1) * 64]
        tok = np.concatenate([np.arange(t0, t0 + 4096), np.arange(o0, o0 + 4096)])
        rope = _rope_tables(tok).reshape(64, 128, 128)
        bd = np.stack([_bias_table(rpb, row0, 0, 0, 6), _bias_table(rpb, row0, 1, 0, 6),
                       _bias_table(rpb, row0, 30, 60, 6), _bias_table(rpb, row0, 31, 60, 6)])
        in_maps.append(dict(
            x_own=f(x_own), x_oth=f(x_oth), x_ext=f(x_ext.reshape(4608, 1024)),
            w_a=w_a, w_b=w_b, w_out=w_out_p, w_g=w_g, w_u=w_u, w_d=w_d,
            g_mix=g_mix, g_ffn=g_ffn, g_qk=g_qk, g_oa=g_oa, g_ob=g_ob, g_fin=g_fin,
            rope=f(rope), bias_i=bias_i, bias_bd=f(bd), ident=ident))
    return in_maps


def kernel(**inputs):
    in_maps = prep_inputs(inputs)
    nc = build_program("full")
    res = run_bass_kernel_spmd(nc, in_maps, core_ids=list(range(8)))
    out = np.empty((4, 8192, 1024), np.float32)
    for c in range(8):
        b, hf = c // 2, c % 2
        out[b, 4096 * hf:4096 * (hf + 1)] = np.asarray(res.results[c]["out"], dtype=np.float32)
    return out
```
